# Optimizing a Trainium2 kernel written in Bass

```python
import jax, jax.numpy as jnp
from jax import lax
import numpy as np

D_MODEL = 1024
BATCH = 4
SEQ = 4096
DEPTH = 2
DEC_BATCH = 128
DEC_SEQ = 8
PAST_LEN = 8192
PAGE_SIZE = 128

A_WIDTH = D_MODEL // 2
A_GROUPS = 4
A_GROUP_DIM = A_WIDTH // A_GROUPS
A_CHUNK = 128
B_HEADS = 8
B_KV_HEADS = 2
B_HEAD_DIM = 64
B_GROUP = B_HEADS // B_KV_HEADS
B_WIDTH = B_HEADS * B_HEAD_DIM
B_KV_WIDTH = B_KV_HEADS * B_HEAD_DIM
WINDOW = 128
C_HEADS = 4
C_HEAD_DIM = 128
C_WIDTH = C_HEADS * C_HEAD_DIM
C_CONV = 4
C_CHUNK = 64
N_BRANCH = 3
EPS = 1e-6

IN_SPLITS = (A_WIDTH, A_WIDTH, A_WIDTH,
             B_WIDTH, B_KV_WIDTH, B_KV_WIDTH, B_WIDTH,
             2 * C_WIDTH, C_WIDTH, C_HEADS, C_HEADS, C_WIDTH, C_WIDTH,
             N_BRANCH * D_MODEL)
IN_WIDTH = sum(IN_SPLITS)
IN_OFFSETS = tuple(int(s) for s in np.cumsum(IN_SPLITS)[:-1])

kernel_name = 'gated_parallel_gmlp_swa_mlstm_decoder_step'


def rmsnorm(x, g):
    xf = x.astype(jnp.float32)
    y = xf * lax.rsqrt(jnp.mean(xf * xf, axis=-1, keepdims=True) + EPS)
    return (y * g.astype(jnp.float32)).astype(x.dtype)


def causal_conv(x, buf, w, b):
    t = x.shape[1]
    xp = jnp.concatenate([buf.astype(x.dtype), x], axis=1)
    y = b
    for j in range(C_CONV):
        y = y + w[j] * xp[:, j:j + t]
    return y, xp[:, -(C_CONV - 1):]


def chunk_gmlp(u, v, vnorm_g, ws, bs):
    bsz, t = u.shape[0], u.shape[1]
    L = min(A_CHUNK, t)
    nc = t // L
    vn = rmsnorm(v, vnorm_g)
    vb = vn.reshape(bsz, nc, L, A_GROUPS, A_GROUP_DIM)
    w = ws[:, :L, :L] * jnp.tril(jnp.ones((L, L), ws.dtype))
    s = jnp.einsum('gts,bnsgc->bntgc', w, vb) + bs[:, :L].T[None, None, :, :, None]
    return u * s.reshape(bsz, t, A_WIDTH), vn


def sink_attention(qb, kb, vb, mask, sinks):
    logits = jnp.einsum('bnqkgd,bnskd->bnkgqs', qb, kb).astype(jnp.float32) * (B_HEAD_DIM ** -0.5)
    logits = jnp.where(mask[None, :, None, None], logits, -jnp.inf)
    snk = sinks.astype(jnp.float32).reshape(B_KV_HEADS, B_GROUP)[None, None, :, :, None]
    mx = jnp.maximum(logits.max(axis=-1), snk)
    p = jnp.exp(logits - mx[..., None])
    den = p.sum(axis=-1) + jnp.exp(snk - mx)
    out = jnp.einsum('bnkgqs,bnskd->bnkgqd', p, vb.astype(jnp.float32)) / den[..., None]
    return out.transpose(0, 1, 4, 2, 3, 5).astype(qb.dtype)


def swa_branch(q, k, v, qn_g, kn_g, sinks, buf_k, buf_v):
    bsz, t = q.shape[0], q.shape[1]
    q = rmsnorm(q.reshape(bsz, t, B_HEADS, B_HEAD_DIM), qn_g).reshape(bsz, t, B_KV_HEADS, B_GROUP, B_HEAD_DIM)
    k = rmsnorm(k.reshape(bsz, t, B_KV_HEADS, B_HEAD_DIM), kn_g)
    v = v.reshape(bsz, t, B_KV_HEADS, B_HEAD_DIM)
    if buf_k is None:
        nb = t // WINDOW
        kp = jnp.concatenate([jnp.zeros_like(k[:, :WINDOW]), k], axis=1)
        vp = jnp.concatenate([jnp.zeros_like(v[:, :WINDOW]), v], axis=1)
        kb = jnp.concatenate([kp[:, :t].reshape(bsz, nb, WINDOW, B_KV_HEADS, B_HEAD_DIM),
                              k.reshape(bsz, nb, WINDOW, B_KV_HEADS, B_HEAD_DIM)], axis=2)
        vb = jnp.concatenate([vp[:, :t].reshape(bsz, nb, WINDOW, B_KV_HEADS, B_HEAD_DIM),
                              v.reshape(bsz, nb, WINDOW, B_KV_HEADS, B_HEAD_DIM)], axis=2)
        qb = q.reshape(bsz, nb, WINDOW, B_KV_HEADS, B_GROUP, B_HEAD_DIM)
        qpos = jnp.arange(t).reshape(nb, WINDOW)
        kpos = (jnp.arange(nb) * WINDOW - WINDOW)[:, None] + jnp.arange(2 * WINDOW)[None]
        valid = (kpos >= 0)[:, None, :]
        new_k, new_v = k[:, t - WINDOW:], v[:, t - WINDOW:]
    else:
        wb = buf_k.shape[1]
        kall = jnp.concatenate([buf_k.astype(k.dtype), k], axis=1)
        vall = jnp.concatenate([buf_v.astype(v.dtype), v], axis=1)
        kb, vb, qb = kall[:, None], vall[:, None], q[:, None]
        qpos = jnp.arange(t)[None]
        kpos = (jnp.arange(wb + t) - wb)[None]
        valid = True
        new_k, new_v = kall[:, -wb:], vall[:, -wb:]
    diff = qpos[:, :, None] - kpos[:, None, :]
    mask = (diff >= 0) & (diff < WINDOW) & valid
    out = sink_attention(qb, kb, vb, mask, sinks).reshape(bsz, t, B_WIDTH)
    return out, new_k, new_v


def mlstm_scan(q, k, v, i_pre, logf, C0, n0, m0, chunk):
    bsz, t, nh, d = q.shape
    nc = t // chunk
    f32 = jnp.float32

    def to_chunks(a):
        a = a.astype(f32).reshape((bsz, nc, chunk) + a.shape[2:])
        return jnp.moveaxis(a, 1, 0)

    causal = jnp.tril(jnp.ones((chunk, chunk), bool))

    def step(carry, xs):
        C, n, m = carry
        qc, kc, vc, ic, fc = xs
        cum = jnp.cumsum(fc, axis=1)
        dmat = cum[:, :, None, :] - cum[:, None, :, :] + ic[:, None, :, :]
        dmat = jnp.where(causal[None, :, :, None], dmat, -jnp.inf)
        m_inter = cum + m[:, None, :]
        m_t = jnp.maximum(m_inter, dmat.max(axis=2))
        a = jnp.exp(dmat - m_t[:, :, None, :]) * jnp.einsum('bthd,bshd->btsh', qc, kc)
        w_inter = jnp.exp(m_inter - m_t)
        num = jnp.einsum('btsh,bshd->bthd', a, vc) + w_inter[..., None] * jnp.einsum('bthd,bhde->bthe', qc, C)
        den = a.sum(axis=2) + w_inter * jnp.einsum('bthd,bhd->bth', qc, n)
        h = num / jnp.maximum(jnp.abs(den), jnp.exp(-m_t))[..., None]
        total = cum[:, -1]
        g = total[:, None] - cum + ic
        m_new = jnp.maximum(total + m, g.max(axis=1))
        wsel = jnp.exp(g - m_new[:, None])
        decay = jnp.exp(total + m - m_new)
        C_new = decay[..., None, None] * C + jnp.einsum('bsh,bshd,bshe->bhde', wsel, kc, vc)
        n_new = decay[..., None] * n + jnp.einsum('bsh,bshd->bhd', wsel, kc)
        return (C_new, n_new, m_new), h

    xs = (to_chunks(q), to_chunks(k), to_chunks(v), to_chunks(i_pre), to_chunks(logf))
    (C1, n1, m1), h = lax.scan(step, (C0.astype(f32), n0.astype(f32), m0.astype(f32)), xs)
    h = jnp.moveaxis(h, 0, 1).reshape(bsz, t, nh, d)
    return h, C1, n1, m1


def mlstm_branch(qk, v, i_pre, f_pre, o_pre, conv_w, conv_b, f_bias, hnorm_g, conv_buf, C0, n0, m0):
    bsz, t = qk.shape[0], qk.shape[1]
    qk, new_buf = causal_conv(qk, conv_buf, conv_w, conv_b)
    qk = jax.nn.silu(qk)
    q, k = jnp.split(qk, 2, axis=-1)
    q = q.reshape(bsz, t, C_HEADS, C_HEAD_DIM)
    k = k.reshape(bsz, t, C_HEADS, C_HEAD_DIM) * (C_HEAD_DIM ** -0.5)
    v = v.reshape(bsz, t, C_HEADS, C_HEAD_DIM)
    logf = jax.nn.log_sigmoid((f_pre + f_bias).astype(jnp.float32))
    h, C1, n1, m1 = mlstm_scan(q, k, v, i_pre, logf, C0, n0, m0, min(C_CHUNK, t))
    h = rmsnorm(h.astype(qk.dtype), hnorm_g.reshape(C_HEADS, C_HEAD_DIM)).reshape(bsz, t, C_WIDTH)
    return h * jax.nn.sigmoid(o_pre), new_buf, C1, n1, m1


def trunk_layer(x, c, lp, past):
    bsz, t = x.shape[0], x.shape[1]
    mod = jax.nn.silu(c) @ lp['ada_w'] + lp['ada_b']
    shift, scale, gate = jnp.split(mod, 3, axis=-1)
    h = rmsnorm(x, lp['norm_g']) * (1.0 + scale[:, None]) + shift[:, None]
    z = h @ lp['w_in'] + lp['b_in']
    a_u, a_v, a_g, b_q, b_k, b_v, b_g, c_qk, c_v, c_i, c_f, c_o, c_g, m_g = jnp.split(z, IN_OFFSETS, axis=-1)
    ya, v_rows = chunk_gmlp(a_u, a_v, lp['gmlp_vnorm_g'], lp['gmlp_ws'], lp['gmlp_bs'])
    ya = ya * jax.nn.silu(a_g)
    if past is None:
        buf_k = None
        buf_v = None
        conv_buf = jnp.zeros((bsz, C_CONV - 1, 2 * C_WIDTH), x.dtype)
        C0 = jnp.zeros((bsz, C_HEADS, C_HEAD_DIM, C_HEAD_DIM), jnp.float32)
        n0 = jnp.zeros((bsz, C_HEADS, C_HEAD_DIM), jnp.float32)
        m0 = jnp.zeros((bsz, C_HEADS), jnp.float32)
    else:
        buf_k, buf_v, conv_buf, C0, n0, m0 = past
    yb, new_k, new_v = swa_branch(b_q, b_k, b_v, lp['swa_qnorm_g'], lp['swa_knorm_g'], lp['swa_sinks'], buf_k, buf_v)
    yb = yb * jax.nn.silu(b_g)
    yc, new_conv, C1, n1, m1 = mlstm_branch(c_qk, c_v, c_i, c_f, c_o, lp['mlstm_conv_w'], lp['mlstm_conv_b'],
                                            lp['mlstm_f_bias'], lp['mlstm_hnorm_g'], conv_buf, C0, n0, m0)
    yc = yc * jax.nn.silu(c_g)
    gates = jax.nn.sigmoid(m_g).reshape(bsz, t, N_BRANCH, D_MODEL)
    merged = (gates[:, :, 0] * (ya @ lp['w_branch_a'])
              + gates[:, :, 1] * (yb @ lp['w_branch_b'])
              + gates[:, :, 2] * (yc @ lp['w_branch_c']))
    x_out = x + gate[:, None] * (merged @ lp['w_out'])
    new_state = (new_k, new_v, new_conv, C1.astype(x.dtype), n1.astype(x.dtype), m1.astype(x.dtype))
    return x_out, new_state, v_rows


def setup_inputs(seed: int = 0) -> dict:
    key = jax.random.key(seed)
    ks = iter(jax.random.split(key, 40))

    def nrm(shape, s):
        return jax.random.normal(next(ks), shape, jnp.float32) * s

    wb = min(WINDOW, PAST_LEN)
    return {
        'x_prompt': nrm((BATCH, SEQ, D_MODEL), 1.0),
        'x_sample': nrm((DEC_BATCH, DEC_SEQ, D_MODEL), 1.0),
        'cache_swa_k': nrm((DEPTH, DEC_BATCH, wb, B_KV_HEADS, B_HEAD_DIM), 1.0),
        'cache_swa_v': nrm((DEPTH, DEC_BATCH, wb, B_KV_HEADS, B_HEAD_DIM), 1.0),
        'state_mlstm_conv': nrm((DEPTH, DEC_BATCH, C_CONV - 1, 2 * C_WIDTH), 1.0),
        'state_mlstm_C': nrm((DEPTH, DEC_BATCH, C_HEADS, C_HEAD_DIM, C_HEAD_DIM), 0.1),
        'state_mlstm_n': nrm((DEPTH, DEC_BATCH, C_HEADS, C_HEAD_DIM), 0.1),
        'state_mlstm_m': nrm((DEPTH, DEC_BATCH, C_HEADS), 1.0),
        'c_prompt': nrm((BATCH, D_MODEL), 1.0),
        'c_sample': nrm((DEC_BATCH, D_MODEL), 1.0),
        'ada_w': nrm((DEPTH, D_MODEL, 3 * D_MODEL), 0.3 * D_MODEL ** -0.5),
        'ada_b': nrm((DEPTH, 3 * D_MODEL), 0.02),
        'norm_g': 1.0 + nrm((DEPTH, D_MODEL), 0.05),
        'w_in': nrm((DEPTH, D_MODEL, IN_WIDTH), D_MODEL ** -0.5),
        'b_in': nrm((DEPTH, IN_WIDTH), 0.02),
        'gmlp_vnorm_g': 1.0 + nrm((DEPTH, A_WIDTH), 0.05),
        'gmlp_ws': nrm((DEPTH, A_GROUPS, A_CHUNK, A_CHUNK), A_CHUNK ** -0.5),
        'gmlp_bs': 1.0 + nrm((DEPTH, A_GROUPS, A_CHUNK), 0.1),
        'swa_qnorm_g': 1.0 + nrm((DEPTH, B_HEAD_DIM), 0.05),
        'swa_knorm_g': 1.0 + nrm((DEPTH, B_HEAD_DIM), 0.05),
        'swa_sinks': nrm((DEPTH, B_HEADS), 0.5),
        'mlstm_conv_w': nrm((DEPTH, C_CONV, 2 * C_WIDTH), C_CONV ** -0.5),
        'mlstm_conv_b': nrm((DEPTH, 2 * C_WIDTH), 0.02),
        'mlstm_f_bias': jnp.linspace(3.0, 6.0, C_HEADS, dtype=jnp.float32)[None] + nrm((DEPTH, C_HEADS), 0.1),
        'mlstm_hnorm_g': 1.0 + nrm((DEPTH, C_WIDTH), 0.05),
        'w_branch_a': nrm((DEPTH, A_WIDTH, D_MODEL), A_WIDTH ** -0.5),
        'w_branch_b': nrm((DEPTH, B_WIDTH, D_MODEL), B_WIDTH ** -0.5),
        'w_branch_c': nrm((DEPTH, C_WIDTH, D_MODEL), C_WIDTH ** -0.5),
        'w_out': nrm((DEPTH, D_MODEL, D_MODEL), D_MODEL ** -0.5),
    }


def reference(x_prompt, x_sample, cache_swa_k, cache_swa_v, state_mlstm_conv, state_mlstm_C, state_mlstm_n,
              state_mlstm_m, c_prompt, c_sample, ada_w, ada_b, norm_g, w_in, b_in, gmlp_vnorm_g, gmlp_ws, gmlp_bs,
              swa_qnorm_g, swa_knorm_g, swa_sinks, mlstm_conv_w, mlstm_conv_b, mlstm_f_bias, mlstm_hnorm_g,
              w_branch_a, w_branch_b, w_branch_c, w_out):
    x_p, x_s = x_prompt, x_sample
    states_p, states_s, vrows_s = [], [], []
    for l in range(DEPTH):
        lp = {
            'ada_w': ada_w[l], 'ada_b': ada_b[l], 'norm_g': norm_g[l], 'w_in': w_in[l], 'b_in': b_in[l],
            'gmlp_vnorm_g': gmlp_vnorm_g[l], 'gmlp_ws': gmlp_ws[l], 'gmlp_bs': gmlp_bs[l],
            'swa_qnorm_g': swa_qnorm_g[l], 'swa_knorm_g': swa_knorm_g[l], 'swa_sinks': swa_sinks[l],
            'mlstm_conv_w': mlstm_conv_w[l], 'mlstm_conv_b': mlstm_conv_b[l], 'mlstm_f_bias': mlstm_f_bias[l],
            'mlstm_hnorm_g': mlstm_hnorm_g[l], 'w_branch_a': w_branch_a[l], 'w_branch_b': w_branch_b[l],
            'w_branch_c': w_branch_c[l], 'w_out': w_out[l],
        }
        x_p, st_p, _ = trunk_layer(x_p, c_prompt, lp, None)
        past = (cache_swa_k[l], cache_swa_v[l], state_mlstm_conv[l], state_mlstm_C[l], state_mlstm_n[l],
                state_mlstm_m[l])
        x_s, st_s, vr = trunk_layer(x_s, c_sample, lp, past)
        states_p.append(st_p)
        states_s.append(st_s)
        vrows_s.append(vr)
    sp = [jnp.stack([st[j] for st in states_p]) for j in range(6)]
    ss = [jnp.stack([st[j] for st in states_s]) for j in range(6)]
    gmlp_v_sample = jnp.stack(vrows_s)
    return (x_p, x_s, sp[0], sp[1], sp[2], sp[3], sp[4], sp[5],
            ss[0], ss[1], ss[2], ss[3], ss[4], ss[5], gmlp_v_sample)
```

```python
import math
from contextlib import ExitStack

import numpy as np
import concourse.bass as bass
import concourse.mybir as mybir
from concourse.bass_utils import run_bass_kernel_spmd

F32 = mybir.dt.float32
BF16 = mybir.dt.bfloat16
AF = mybir.ActivationFunctionType
ALU = mybir.AluOpType
AX = mybir.AxisListType

D = 1024
KC = 8
NL = 2
EPS = 1e-6
NSB = 16
IN_W = 8456
O_AU, O_AV, O_AG, O_BQ, O_BK, O_BV, O_BG = 0, 512, 1024, 1536, 2048, 2176, 2304
O_CQK, O_CV, O_CI, O_CF, O_CO, O_CG, O_MG = 2816, 3840, 4352, 4356, 4360, 4872, 5384
GSZ = 4608

C_ID, C_CUR, C_PREV, C_BD, C_ONE, C_BLK, C_CACHE, C_TI, C_BM, C_E4 = 0, 128, 256, 384, 512, 640, 768, 776, 904, 920
NCONST = 924


def make_consts():
    c = np.zeros((128, NCONST), np.float32)
    i = np.arange(128)
    c[:, C_ID:C_ID + 128] = np.eye(128)
    c[:, C_CUR:C_CUR + 128] = (i[:, None] <= i[None, :])
    c[:, C_PREV:C_PREV + 128] = (i[:, None] > i[None, :])
    c[:, C_BD:C_BD + 128] = ((i[:, None] // 8) == (i[None, :] // 8)) & ((i[:, None] % 8) <= (i[None, :] % 8))
    c[:, C_ONE:C_ONE + 128] = 1.0
    c[:, C_BLK:C_BLK + 128] = ((i[:, None] // 64) == (i[None, :] // 64))
    c[:, C_CACHE:C_CACHE + 8] = (i[:, None] > np.arange(8)[None, :])
    c[:8, C_TI:C_TI + 128] = (np.arange(8)[:, None] == (i[None, :] % 8))
    c[:, C_BM:C_BM + 16] = ((i[:, None] // 8) == np.arange(16)[None, :])
    c[:4, C_E4:C_E4 + 4] = np.eye(4)
    return c


class Dep:
    __slots__ = ("w", "r", "excl")

    def __init__(self):
        self.w = None
        self.r = []
        self.excl = False


class Tile:
    def __init__(self, t, nslots=1):
        self.t = t
        self.d = [Dep() for _ in range(nslots)]

    def __getitem__(self, k):
        return self.t[k]


CAST_INFLIGHT = 9


class Sched:
    NDMA = 56

    def __init__(self, nc, es):
        self.nc = nc
        self.eng = {"pe": nc.tensor, "act": nc.scalar, "dve": nc.vector, "pool": nc.gpsimd, "sp": nc.sync}
        self.sem = {}
        self.cnt = {}
        for k in self.eng:
            self.sem[k] = es.enter_context(nc.semaphore("s_" + k))
            self.cnt[k] = 0
        self.dval = [0] * self.NDMA
        self.dnext = 0
        for i in range(self.NDMA):
            self.sem["d%d" % i] = es.enter_context(nc.semaphore("d%d" % i))
        self.waited = {k: {} for k in self.eng}

    def _wait(self, e, deps):
        need = {}
        for d in deps:
            if d is None:
                continue
            k, v = d
            if k == "pe" and e == "pe":
                continue
            if self.waited[e].get(k, 0) >= v:
                continue
            if need.get(k, 0) < v:
                need[k] = v
        for k, v in need.items():
            self.eng[e].wait_ge(self.sem[k], v)
            self.waited[e][k] = v

    @staticmethod
    def _collect(reads, writes, e=None):
        deps = []
        for d in reads:
            deps.append(d.w)
            if d.excl:
                deps.extend(r for r in d.r if r[0] != e)
        for d in writes:
            deps.append(d.w)
            deps.extend(d.r)
        return deps

    @staticmethod
    def _flat(lst):
        out = []
        for x in lst:
            if isinstance(x, Tile):
                out.extend(x.d)
            elif isinstance(x, (list, tuple)):
                out.extend(Sched._flat(x))
            elif x is not None:
                out.append(x)
        return out

    def op(self, e, fn, reads=(), writes=()):
        reads = self._flat(reads)
        writes = self._flat(writes)
        self._wait(e, self._collect(reads, writes, e))
        ins = fn(self.eng[e])
        self.cnt[e] += 1
        ins.then_inc(self.sem[e], 1)
        me = (e, self.cnt[e])
        for d in reads:
            d.r.append(me)
        for d in writes:
            d.w = me
            d.r = []
        return me

    def dma(self, out, in_, reads=(), writes=(), q="sp", **kw):
        reads = self._flat(reads)
        writes = self._flat(writes)
        i = self.dnext
        self.dnext = (self.dnext + 1) % self.NDMA
        key = "d%d" % i
        deps = self._collect(reads, writes)
        if self.dval[i] > 0:
            deps.append((key, self.dval[i]))
        if q == "pool":
            hist = self.__dict__.setdefault("pool_hist", [])
            if len(hist) >= CAST_INFLIGHT:
                deps.append(hist[-CAST_INFLIGHT])
        self._wait(q, deps)
        self.dval[i] += 16
        self.ndma = getattr(self, "ndma", {})
        self.ndma[q] = self.ndma.get(q, 0) + 1
        self.eng[q].dma_start(out=out, in_=in_, **kw).then_inc(self.sem[key], 16)
        me = (key, self.dval[i])
        if q == "pool":
            self.pool_hist.append(me)
        for d in reads:
            d.r.append(me)
        for d in writes:
            d.w = me
            d.r = []
        return me

    def barrier(self):
        targets = [(k, self.cnt[k]) for k in self.eng if self.cnt[k] > 0]
        targets += [("d%d" % i, self.dval[i]) for i in range(self.NDMA) if self.dval[i] > 0]
        for e in self.eng:
            self._wait(e, [t for t in targets if t[0] != e])


class _Stop(Exception):
    pass


def build_program(NST, debug=False, stage=None):
    ck_state = {"n": 0}
    import os as _os
    DBGON = bool(_os.environ.get("KDBG"))
    dbg_names = []

    def dbg(name, ap, deps):
        if not DBGON:
            return
        t_ = nc.dram_tensor("dbg_" + name, list(ap.shape), ap.dtype, kind="ExternalOutput").ap()
        S.dma(t_, ap, reads=deps)
        dbg_names.append("dbg_" + name)

    def ckpt(tag=""):
        ck_state["n"] += 1
        if DBGON and "S" in ck_state:
            print("CKPT", ck_state["n"], tag, dict(ck_state["S"].cnt), max(ck_state["S"].dval), getattr(ck_state["S"], "ndma", None))
        if stage is not None and ck_state["n"] >= stage:
            print("STOP at checkpoint", ck_state["n"], tag)
            raise _Stop()

    NP = NST * 512
    nc = bass.Bass("TRN2", target_bir_lowering=False)
    es = ExitStack()
    S = Sched(nc, es)
    ck_state["S"] = S

    def ap_like(ap, dims):
        return bass.AP(ap.tensor, ap.offset, [list(ap.ap[0])] + [list(d) for d in dims])

    def pstride(ap_row, step, count, free):
        ps_ = ap_row.ap[0][0]
        return bass.AP(ap_row.tensor, ap_row.offset, [[step * ps_, count]] + [list(d) for d in free])

    def split_part(ap, outer, inner, free):
        ps_ = ap.ap[0][0]
        return bass.AP(ap.tensor, ap.offset, [[inner * ps_, outer], [ps_, inner]] + [list(d) for d in free])

    def bcast_row(ap_row, nparts, n):
        return bass.AP(ap_row.tensor, ap_row.offset, [[0, nparts], [1, n]])

    def dram(name, shape, dt=F32, kind="ExternalInput"):
        return nc.dram_tensor(name, list(shape), dt, kind=kind).ap()

    xp = dram("xp", [NP, D]); xs = dram("xs", [128, D])
    cpr = dram("cpr", [1, D]); csm = dram("csm", [NSB, D])
    ck_in = dram("ck", [NL, NSB, 128, 128]); cv_in = dram("cv", [NL, NSB, 128, 128])
    sconv = dram("sconv", [NL, NSB, 3, D]); sC = dram("sC", [NL, NSB, 4, 128, 128])
    sn = dram("sn", [NL, NSB, 4, 128]); sm = dram("sm", [NL, NSB, 4])
    ada_w = dram("ada_w", [NL, D, 3 * D]); ada_b = dram("ada_b", [NL, 3 * D]); norm_g = dram("norm_g", [NL, D])
    w_in = dram("w_in", [NL, D, IN_W]); b_in = dram("b_in", [NL, IN_W])
    vnorm_g = dram("gmlp_vnorm_g", [NL, 512]); gws = dram("gmlp_ws", [NL, 4, 128, 128]); gbs = dram("gmlp_bs", [NL, 4, 128])
    qn_g = dram("swa_qnorm_g", [NL, 64]); kn_g = dram("swa_knorm_g", [NL, 64]); sinks = dram("swa_sinks", [NL, 8])
    conv_w = dram("mlstm_conv_w", [NL, 4, D]); conv_b = dram("mlstm_conv_b", [NL, D])
    f_bias = dram("mlstm_f_bias", [NL, 4]); hn_g = dram("mlstm_hnorm_g", [NL, 512])
    w_ba = dram("w_branch_a", [NL, 512, D]); w_bb = dram("w_branch_b", [NL, 512, D]); w_bc = dram("w_branch_c", [NL, 512, D])
    w_out = dram("w_out", [NL, D, D])
    consts = dram("consts", [128, NCONST])

    EO = "ExternalOutput"
    yp = dram("yp", [NP, D], kind=EO); ys = dram("ys", [128, D], kind=EO)
    kp_o = dram("kp", [NL, 128, 128], kind=EO); vp_o = dram("vp", [NL, 128, 128], kind=EO)
    convp_o = dram("convp", [NL, 3, D], kind=EO); Cp_o = dram("Cp", [NL, 4, 128, 128], kind=EO)
    np_o = dram("np_", [NL, 4, 128], kind=EO); mp_o = dram("mp", [NL, 4], kind=EO)
    ks_o = dram("ks", [NL, NSB, 128, 128], kind=EO); vs_o = dram("vs", [NL, NSB, 128, 128], kind=EO)
    convs_o = dram("convs", [NL, NSB, 3, D], kind=EO); Cs_o = dram("Cs", [NL, NSB, 4, 128, 128], kind=EO)
    ns_o = dram("ns", [NL, NSB, 4, 128], kind=EO); ms_o = dram("ms", [NL, NSB, 4], kind=EO)
    gv_o = dram("gv", [NL, 128, 512], kind=EO)

    groups = {}
    gorder = []
    wsc_sz = [0]

    def add_group(name, size):
        groups[name] = (wsc_sz[0], size)
        wsc_sz[0] += size
        gorder.append(name)

    for nm in ["ada0", "ada1", "ada2", "ada3", "adag0", "adag1"]:
        add_group(nm, 4096)
    LGROUPS = ["bq", "bkv", "bg", "cq", "ck", "cv", "cif", "co", "cg", "av", "ag", "au"] + \
              ["mg%d" % j for j in range(8)] + ["wo0", "wo1"]
    for nm in LGROUPS:
        add_group(nm, GSZ if nm.startswith("mg") else (64 if nm == "cif" else (2048 if nm == "bkv" else 4096)))
    WTOT = wsc_sz[0]
    wsc = [dram("wsc%d" % l, [128, WTOT], BF16, kind="Internal") for l in range(NL)]
    wsc_dep = [{g: Dep() for g in gorder} for _ in range(NL)]
    scr1 = dram("scr1", [NL, 128, 8], kind="Internal")
    scr2 = dram("scr2", [NL, 3, 16, 4], kind="Internal")
    scr3 = dram("scr3", [NL, 128, 8], kind="Internal")
    scr4 = dram("scr4", [NL, 8], kind="Internal")
    scr_dep = {k: Dep() for k in ["scr1", "scr2", "scr3", "scr4"]}

    cast_only = [None]

    def cast_piece(l, gname, dst_off, kc, n, src):
        if cast_only[0] is not None and gname != cast_only[0]:
            return
        off = groups[gname][0] + dst_off
        dst = wsc[l][:, off:off + kc * n].rearrange("p (k n) -> p k n", k=kc)
        S.dma(dst, src.rearrange("(k p) n -> p k n", p=128), writes=[wsc_dep[l][gname]], q="pool")

    def perm_pieces(l, gname, base, src2d):
        if cast_only[0] is not None and gname != cast_only[0]:
            return
        for c in range(4):
            for half in range(2):
                h = c + 4 * half
                off = groups[gname][0] + c * 128 + half * 64
                dst = bass.AP(wsc[l].tensor, off, [[WTOT, 128], [512, 8], [1, 64]])
                S.dma(dst, src2d[:, base + h * 64: base + (h + 1) * 64].rearrange("(k p) n -> p k n", p=128),
                      writes=[wsc_dep[l][gname]], q="pool")

    def issue_casts(l):
        wi = w_in[l]
        for j in range(4):
            cast_piece(l, "ada%d" % j, 0, 8, 512, ada_w[l][:, j * 512:(j + 1) * 512])
        for j in range(2):
            cast_piece(l, "adag%d" % j, 0, 8, 512, ada_w[l][:, 2048 + j * 512: 2048 + (j + 1) * 512])
        cast_piece(l, "av", 0, 8, 512, wi[:, O_AV:O_AV + 512])
        cast_piece(l, "ag", 0, 8, 512, wi[:, O_AG:O_AG + 512])
        cast_piece(l, "au", 0, 8, 512, wi[:, O_AU:O_AU + 512])
        perm_pieces(l, "bq", O_BQ, wi)
        cast_piece(l, "bkv", 0, 8, 256, wi[:, O_BK:O_BK + 256])
        perm_pieces(l, "bg", O_BG, wi)
        cast_piece(l, "cq", 0, 8, 512, wi[:, O_CQK:O_CQK + 512])
        cast_piece(l, "ck", 0, 8, 512, wi[:, O_CQK + 512:O_CQK + 1024])
        cast_piece(l, "cv", 0, 8, 512, wi[:, O_CV:O_CV + 512])
        cast_piece(l, "cif", 0, 8, 8, wi[:, O_CI:O_CI + 8])
        cast_piece(l, "co", 0, 8, 512, wi[:, O_CO:O_CO + 512])
        cast_piece(l, "cg", 0, 8, 512, wi[:, O_CG:O_CG + 512])
        for j in range(8):
            g = "mg%d" % j
            if cast_only[0] is not None and g != cast_only[0]:
                continue
            for br in range(3):
                off = groups[g][0] + br * 128
                dst = bass.AP(wsc[l].tensor, off, [[WTOT, 128], [384, 8], [1, 128]])
                c0 = O_MG + br * 1024 + j * 128
                S.dma(dst, wi[:, c0:c0 + 128].rearrange("(k p) n -> p k n", p=128), writes=[wsc_dep[l][g]], q="pool")
            cast_piece(l, g, 3072, 4, 128, w_ba[l][:, j * 128:(j + 1) * 128])
            for c in range(4):
                for half in range(2):
                    h = c + 4 * half
                    off = groups[g][0] + 3584 + c * 128
                    dst = bass.AP(wsc[l].tensor, off + half * 64 * WTOT, [[WTOT, 64], [1, 128]])
                    S.dma(dst, w_bb[l][h * 64:(h + 1) * 64, j * 128:(j + 1) * 128], writes=[wsc_dep[l][g]], q="pool")
            cast_piece(l, g, 4096, 4, 128, w_bc[l][:, j * 128:(j + 1) * 128])
        for j in range(2):
            cast_piece(l, "wo%d" % j, 0, 8, 512, w_out[l][:, j * 512:(j + 1) * 512])

    def sb(name, shape, dt=F32, nslots=1):
        return Tile(es.enter_context(nc.sbuf_tensor(name, list(shape), dt)), nslots)

    NPS = 8
    psum = [Tile(es.enter_context(nc.psum_tensor("ps%d" % i, [128, 512], F32))) for i in range(NPS)]
    for p_ in psum:
        p_.d[0].excl = True
    ps_i = [0]

    ps_resv = set()

    def PS():
        while (ps_i[0] % NPS) in ps_resv:
            ps_i[0] += 1
        p = psum[ps_i[0] % NPS]
        p.idx = ps_i[0] % NPS
        ps_i[0] += 1
        return p

    NTMP = 6
    tmps = [sb("tmp%d" % i, [128, 516], F32) for i in range(NTMP)]
    tmp_i = [0]

    def TMP():
        t = tmps[tmp_i[0] % NTMP]
        tmp_i[0] += 1
        return t

    rr = {"ev": 0}

    cst = sb("cst", [128, NCONST], F32)
    cbf = sb("cbf", [128, 776], BF16)
    S.dma(cst[:, :], consts[:, :], writes=[cst])
    S.op("dve", lambda e: e.tensor_copy(out=cbf[:, :], in_=cst[:, 0:776]), reads=[cst], writes=[cbf])
    ident32 = cst[:, C_ID:C_ID + 128]
    identb = cbf[:, C_ID:C_ID + 128]
    ones32 = cst[:, C_ONE:C_ONE + 128]
    onesb = cbf[:, C_ONE:C_ONE + 128]
    blk32 = cst[:, C_BLK:C_BLK + 128]

    NRING = 3
    ring = [sb("wr%d" % i, [128, GSZ], BF16) for i in range(NRING)]
    wseq = []
    for l in range(NL):
        for nm in ["ada0", "ada1", "ada2", "ada3", "adag0", "adag1"]:
            wseq.append((l, nm))
        for nm in LGROUPS:
            wseq.append((l, nm))
    for _ in range(NST):
        for l in range(NL):
            for nm in LGROUPS:
                wseq.append((l, nm))
    wstate = {"issued": 0, "next": 0}
    TMB = {"av": (0, 512), "bkv": (512, 128), "cv": (640, 512), "cif": (1152, 8), "co": (1160, 512), "cg": (1672, 512)}

    cast_done = set()
    CAST_AHEAD = 8

    def ensure_cast(upto):
        for j in range(min(upto, len(wseq))):
            key_ = wseq[j]
            if key_ in cast_done:
                continue
            if not cast_done:
                S._wait("pool", [("d%d" % i_, S.dval[i_]) for i_ in range(S.NDMA) if S.dval[i_] > 0])
            cast_done.add(key_)
            cast_only[0] = key_[1]
            issue_casts(key_[0])
            cast_only[0] = None

    def w_issue_upto(n):
        while wstate["issued"] < min(n, len(wseq)):
            i = wstate["issued"]
            ensure_cast(i + 1 + CAST_AHEAD)
            l, nm = wseq[i]
            off, sz = groups[nm]
            buf = ring[i % NRING]
            if DBGON:
                print("ISSUE load", i, nm, "at ckpt", ck_state["n"], "next", wstate["next"])
            S.dma(buf[:, 0:sz], wsc[l][:, off:off + sz], reads=[wsc_dep[l][nm]], writes=[buf])
            if nm in TMB:
                bo, nb_ = TMB[nm]
                S.dma(buf[0:2, 4096:4096 + nb_], brs[l][:, bo:bo + nb_], reads=[brs_dep[l]], writes=[buf])
            wstate["issued"] += 1

    def WNEXT(expect, hold=0):
        i = wstate["next"]
        assert wseq[i][1] == expect, (wseq[i], expect)
        w_issue_upto(i + NRING - hold)
        wstate["next"] += 1
        return ring[i % NRING]

    prm = []
    for l in range(NL):
        p = {}
        p["bfm"] = sb("bfm%d" % l, [128, 49], F32)
        p["vgB"] = sb("vgB%d" % l, [128, 512], F32)
        p["hgB"] = sb("hgB%d" % l, [128, 512], F32)
        p["fbB"] = sb("fbB%d" % l, [128, 4], F32)
        p["cw"] = sb("cw%d" % l, [128, 8, 4], F32)
        p["cb"] = sb("cb%d" % l, [128, 8], F32)
        p["gq"] = sb("gq%d" % l, [128, 1], F32)
        p["gk"] = sb("gk%d" % l, [128, 1], F32)
        p["esink"] = sb("esink%d" % l, [128, 4], F32)
        p["WT"] = sb("WT%d" % l, [128, 2, 4, 128], BF16)
        p["ng"] = sb("ng%d" % l, [128, 8], F32)
        p["adab"] = sb("adab%d" % l, [128, 16], F32)
        p["Gp"] = sb("Gp%d" % l, [128, 8], F32)
        p["Sp"] = sb("Sp%d" % l, [128, 8], F32)
        p["gate"] = sb("gate%d" % l, [128, 1024], F32)
        prm.append(p)
    GsT = sb("GsT", [128, 8, 128], F32)
    SsT = sb("SsT", [128, 8, 128], F32)
    for p in prm:
        p["Gs"] = GsT
        p["Ss"] = SsT
    gb = sb("gb", [128, 1024], BF16)
    GB = {(0, 0): (0, 0), (0, 1): (32, 0), (1, 0): (64, 0), (1, 1): (0, 512)}
    brs = [dram("brs%d" % l, [2, 2184], BF16, kind="Internal") for l in range(NL)]
    brs_dep = [Dep() for _ in range(NL)]
    hb = sb("hb", [2, 512], BF16)

    BF_AU, BF_AG, BF_BQ, BF_BK, BF_BG, BF_CQ, BF_CK, BF_MG = 0, 4, 8, 12, 13, 17, 21, 25
    BT_AV, BT_BV, BT_CV, BT_CIF, BT_CO, BT_CG = 0, 512, 640, 1152, 1160, 1672

    def load_params(l):
        p = prm[l]
        bl = b_in[l]

        def fm_bias(ci, c0, n):
            S.dma(p["bfm"][:, ci:ci + n], bl[c0:c0 + n * 128].rearrange("(c p) -> p c", p=128), writes=[p["bfm"]],
                  allow_slow_non_contiguous=True)

        def fm_bias_perm(ci, base):
            for half in range(2):
                S.dma(p["bfm"][half * 64:(half + 1) * 64, ci:ci + 4],
                      bl[base + half * 256: base + (half + 1) * 256].rearrange("(c p) -> p c", p=64),
                      writes=[p["bfm"]], allow_slow_non_contiguous=True)

        fm_bias(BF_AU, O_AU, 4); fm_bias(BF_AG, O_AG, 4); fm_bias_perm(BF_BQ, O_BQ); fm_bias(BF_BK, O_BK, 1)
        fm_bias_perm(BF_BG, O_BG); fm_bias(BF_CQ, O_CQK, 4); fm_bias(BF_CK, O_CQK + 512, 4); fm_bias(BF_MG, O_MG, 24)
        for (bo, c0, n) in [(BT_AV, O_AV, 512), (BT_BV, O_BV, 128), (BT_CV, O_CV, 512), (BT_CIF, O_CI, 8),
                            (BT_CO, O_CO, 512), (BT_CG, O_CG, 512)]:
            tb = TMP(); tb2 = TMP()
            S.dma(tb[0:1, 0:n], bl[c0:c0 + n].rearrange("(o n) -> o n", o=1), writes=[tb])
            S.op("dve", lambda e: e.tensor_copy(out=hb[0:1, 0:n], in_=tb[0:1, 0:n]), reads=[tb], writes=[hb])
            S.dma(brs[l][0:1, bo:bo + n], hb[0:1, 0:n], reads=[hb], writes=[brs_dep[l]])
            S.op("dve", lambda e: e.tensor_tensor(out=tb2[0:1, 0:n], in0=tb[0:1, 0:n], in1=hb[0:1, 0:n],
                                                  op=ALU.subtract), reads=[tb, hb], writes=[tb2])
            S.op("dve", lambda e: e.tensor_copy(out=hb[0:1, 0:n], in_=tb2[0:1, 0:n]), reads=[tb2], writes=[hb])
            S.dma(brs[l][1:2, bo:bo + n], hb[0:1, 0:n], reads=[hb], writes=[brs_dep[l]])
        S.dma(p["vgB"][:, :], bcast_row(vnorm_g[l:l + 1, :], 128, 512), writes=[p["vgB"]])
        S.dma(p["hgB"][:, :], bcast_row(hn_g[l:l + 1, :], 128, 512), writes=[p["hgB"]])
        S.dma(p["fbB"][:, :], bcast_row(f_bias[l:l + 1, :], 128, 4), writes=[p["fbB"]])
        for j in range(4):
            S.dma(p["cw"][:, :, j], conv_w[l, j].rearrange("(c p) -> p c", p=128), writes=[p["cw"]], allow_slow_non_contiguous=True)
        S.dma(p["cb"][:, :], conv_b[l].rearrange("(c p) -> p c", p=128), writes=[p["cb"]], allow_slow_non_contiguous=True)
        for half in range(2):
            S.dma(p["gq"][half * 64:(half + 1) * 64, :], qn_g[l].rearrange("(p o) -> p o", o=1), writes=[p["gq"]],
                  allow_slow_non_contiguous=True)
            S.dma(p["gk"][half * 64:(half + 1) * 64, :], kn_g[l].rearrange("(p o) -> p o", o=1), writes=[p["gk"]],
                  allow_slow_non_contiguous=True)
            S.dma(p["esink"][half * 64:(half + 1) * 64, :], bcast_row(sinks[l:l + 1, half * 4:(half + 1) * 4], 64, 4),
                  writes=[p["esink"]])
        S.op("act", lambda e: e.activation(out=p["esink"][:, :], in_=p["esink"][:, :], func=AF.Exp),
             reads=[p["esink"]], writes=[p["esink"]])
        S.dma(p["ng"][:, :], norm_g[l].rearrange("(c p) -> p c", p=128), writes=[p["ng"]], allow_slow_non_contiguous=True)
        S.dma(p["adab"][:, :], ada_b[l, 0:2048].rearrange("(c p) -> p c", p=128), writes=[p["adab"]], allow_slow_non_contiguous=True)
        for kind_ in range(2):
            base, c0 = GB[(l, kind_)]
            tb = TMP(); tb2 = TMP()
            rows = slice(base, base + 2)
            for r in range(2):
                if kind_ == 0:
                    S.dma(tb[base + r:base + r + 1, 0:512], gbs[l:l + 1, :, :].rearrange("o g t -> o (g t)"), writes=[tb])
                else:
                    src = bass.AP(gbs.tensor, l * 512, [[0, 1], [128, 4], [0, 16], [1, 8]])
                    S.dma(tb[base + r:base + r + 1, 0:512].rearrange("o (g b t) -> o g b t", g=4, t=8), src, writes=[tb],
                          allow_slow_non_contiguous=True)
            S.op("dve", lambda e: e.tensor_copy(out=gb[rows, c0:c0 + 512], in_=tb[rows, 0:512]), reads=[tb], writes=[gb])
            S.op("dve", lambda e: e.tensor_tensor(out=tb2[rows, 0:512], in0=tb[rows, 0:512], in1=gb[rows, c0:c0 + 512], op=ALU.subtract),
                 reads=[tb, gb], writes=[tb2])
            S.op("dve", lambda e: e.tensor_copy(out=junk[rows, 0:512], in_=tb2[rows, 0:512]), reads=[tb2], writes=[junk])
            S.dma(gb[base + 1:base + 2, c0:c0 + 512], junk[base + 1:base + 2, 0:512], reads=[junk], writes=[gb])
        for g in range(4):
            wt = TMP()
            S.dma(wt[:, 0:128], gws[l, g, :, :], writes=[wt])
            ps = PS()
            S.op("pe", lambda e: e.matmul(ps[:, 0:128], lhsT=wt[:, 0:128], rhs=ident32, start=True, stop=True),
                 reads=[wt, cst], writes=[ps])
            S.op("dve", lambda e: e.tensor_tensor(out=p["WT"][:, 0, g, :], in0=ps[:, 0:128], in1=cst[:, C_CUR:C_CUR + 128],
                                                  op=ALU.mult), reads=[ps, cst], writes=[p["WT"]])
            lhs = ap_like(wt[0:8, 0:8], [[0, 16], [1, 8]])
            rep8 = TMP()
            S.op("dve", lambda e: e.tensor_copy(out=rep8[0:8, 0:128].rearrange("p (b t) -> p b t", t=8), in_=lhs), reads=[wt], writes=[rep8])
            ps2 = PS()
            S.op("pe", lambda e: e.matmul(ps2[:, 0:128], lhsT=rep8[0:8, 0:128], rhs=cst[0:8, C_TI:C_TI + 128], start=True, stop=True),
                 reads=[rep8, cst], writes=[ps2])
            S.op("dve", lambda e: e.tensor_tensor(out=p["WT"][:, 1, g, :], in0=ps2[:, 0:128], in1=cst[:, C_BD:C_BD + 128],
                                                  op=ALU.mult), reads=[ps2, cst], writes=[p["WT"]])


    X = sb("X", [128, 4, D], F32, nslots=4)
    hT = sb("hT", [128, 8, 512], BF16, nslots=8)
    xn32 = sb("xn32", [128, D], F32)
    xn32b = sb("xn32b", [128, D], F32)
    convTM = xn32
    junk = sb("junk", [128, D], BF16)
    st4 = sb("st4", [128, 16], F32, nslots=16)
    vn = sb("vn", [128, 4, 512], BF16)
    yaT = sb("yaT", [128, 4, 512], BF16)
    qT = sb("qT", [128, 4, 512], BF16)
    kTall = [sb("kTall%d" % l, [128, 128 + 512], BF16) for l in range(NL)]
    vall = [sb("vall%d" % l, [128, 5, 128], BF16) for l in range(NL)]
    k32 = sb("k32", [128, 128], F32)
    v32 = sb("v32", [128, 128], F32)
    sbg = sb("sbg", [128, 4, 512], BF16)
    ybT = sb("ybT", [128, 4, 512], BF16)
    PT = [sb("PT%d" % i, [128, 512], BF16) for i in range(4)]
    qTm = sb("qTm", [128, 4, 512], BF16)
    kTm = sb("kTm", [128, 4, 512], BF16)
    kTM = sb("kTM", [128, 4, 4, 128], BF16)
    vaug = sb("vaug", [128, 4, 4, 129], BF16)
    vtil = sb("vtil", [128, 4, 4, 129], BF16)
    ogg = sb("ogg", [128, 4, 512], BF16)
    ycT = sb("ycT", [128, 4, 512], BF16)
    ycTM = sb("ycTM", [128, 4, 128], BF16)
    AT = sb("AT", [128, 4, 128], BF16)
    stg = [sb("stg%d" % i, [128, 3 + 512], F32) for i in range(2)]
    bS = sb("bS", [128, 4, 4], F32)
    ecnS = sb("ecnS", [128, 4, 4], F32)
    etotS = sb("etotS", [128, 4, 4], F32)
    mT = sb("mT", [128, 8, 512], BF16)
    scT = sb("scT", [128, 8, 256], BF16)
    epsb = sb("epsb", [128, 4], F32)
    ccarry = [sb("ccarry%d" % l, [128, 8, 3], F32) for l in range(NL)]
    C32 = [sb("C32_%d" % l, [128, 4, 129], F32) for l in range(NL)]
    Cbf = [sb("Cbf%d" % l, [128, 4, 129], BF16) for l in range(NL)]
    PL = [sb("PL%d" % l, [128, 4], F32) for l in range(NL)]
    Rm = [sb("Rm%d" % l, [128, 4], F32) for l in range(NL)]
    sm4 = sb("sm4", [4, 256], F32)
    qmask = sb("qmask", [128, 4, 2, 128], BF16)
    vmask = sb("vmask", [128, 2, 129], BF16)
    C0bf = sb("C0bf", [128, 2, 4, 129], BF16)
    AB = sb("AB", [128, 3, 16, 4], F32)
    kcT = sb("kcT", [128, 16, 128], BF16)
    vcb = sb("vcb", [128, 16, 128], BF16)
    ATb = kcT
    ycTMb = vcb
    PTc = [PT[2], PT[3]]

    S.op("pool", lambda e: e.memset(vaug[:], 1.0), writes=[vaug])
    for l in range(NL):
        S.op("pool", lambda e: e.memset(C32[l][:], 0.0), writes=[C32[l]])
        S.op("pool", lambda e: e.memset(Cbf[l][:], 0.0), writes=[Cbf[l]])
        S.op("pool", lambda e: e.memset(ccarry[l][:], 0.0), writes=[ccarry[l]])
        S.op("pool", lambda e: e.memset(PL[l][:], 0.0), writes=[PL[l]])
        S.op("pool", lambda e: e.memset(Rm[l][:], -1e30), writes=[Rm[l]])

    def prologue_casts():
        S._wait("pool", [("d%d" % i, S.dval[i]) for i in range(S.NDMA) if S.dval[i] > 0])
        for l in range(NL):
            issue_casts(l)
        ckpt("casts")

    def prologue():
        for l in range(NL):
            load_params(l)
        if DBGON:
            o_, z_ = groups["av"]
            dbg("wsc_av", wsc[0][:, o_:o_ + z_], [wsc_dep[0]["av"]])
            dbg("brs0", brs[0][:, :], [brs_dep[0]])
        ckpt("params")

    def prologue2():
      if True:
        S.dma(X[:, 0, :], bcast_row(cpr[0:1, :], 128, D), writes=[X.d[0]])
        for t_ in range(8):
            S.dma(pstride(X[t_:t_ + 1, 1, :], 8, 16, [[1, D]]), csm[:, :], writes=[X.d[1]])
        scb = hT[:, 0:4, :].rearrange("p a b -> p (a b)")
        for k in range(2):
            S.op("act", lambda e: e.activation(out=scb[:, k * D:(k + 1) * D], in_=X[:, k, :], func=AF.Silu), reads=[X.d[k]], writes=[hT])
        for k in range(2):
            ps = PS()
            pv = ps[:].bitcast(BF16)
            for kc in range(8):
                S.op("pe", lambda e: e.transpose(out=pv[:, kc * 128:(kc + 1) * 128], in_=scb[:, k * D + kc * 128: k * D + (kc + 1) * 128],
                                                 identity=identb), reads=[hT, cbf], writes=[ps])
            S.op("dve", lambda e: e.tensor_copy(out=scT[:, :, k * 128:(k + 1) * 128],
                                                in_=pv.rearrange("p (k t) -> p k t", k=8)), reads=[ps], writes=[scT])
        dbg("scT", scT[:, :, :], [scT])
        dbg("Xc", X[:, 0:2, :], [X])
    bct = xn32

    def compute_mod(l):
        p = prm[l]
        modp = st4
        for j4 in range(4):
            W = WNEXT("ada%d" % j4)
            Wv = W[:, 0:4096].rearrange("p (k n) -> p k n", k=8)
            for jj in range(4):
                j = j4 * 4 + jj
                ps = PS()
                for kc in range(8):
                    S.op("pe", lambda e: e.matmul(ps[:, 0:256], lhsT=Wv[:, kc, jj * 128:(jj + 1) * 128], rhs=scT[:, kc, :],
                                                  start=(kc == 0), stop=(kc == 7)), reads=[W, scT], writes=[ps])
                S.op("act", lambda e: e.activation(out=modp[:, j:j + 1], in_=ps[:, 0:1], func=AF.Identity,
                                                   bias=p["adab"][:, j:j + 1], scale=1.0), reads=[ps, p["adab"]], writes=[modp])
                dst = (p["Ss"][:, j, :] if j < 8 else p["Gs"][:, j - 8, :])
                S.op("act", lambda e: e.activation(out=dst, in_=ps[:, 128:256], func=AF.Identity,
                                                   bias=p["adab"][:, j:j + 1], scale=1.0),
                     reads=[ps, p["adab"]], writes=[p["Ss"] if j < 8 else p["Gs"]])
        S.op("dve", lambda e: e.tensor_copy(out=p["Sp"][:, :], in_=modp[:, 0:8]), reads=[modp], writes=[p["Sp"]])
        S.op("dve", lambda e: e.scalar_tensor_tensor(out=p["Gp"][:, :], in0=modp[:, 8:16], scalar=1.0, in1=p["ng"][:, :],
                                                     op0=ALU.add, op1=ALU.mult), reads=[modp, p["ng"]], writes=[p["Gp"]])
        for kc in range(8):
            S.op("dve", lambda e: e.tensor_scalar(out=p["Gs"][:, kc, :], in0=p["Gs"][:, kc, :], scalar1=1.0,
                                                  scalar2=p["ng"][:, kc:kc + 1], op0=ALU.add, op1=ALU.mult),
                 reads=[p["Gs"], p["ng"]], writes=[p["Gs"]])
        S.dma(bct[:, :], bcast_row(ada_b[l:l + 1, 2048:3072], 128, 1024), writes=[bct])
        for j in range(2):
            W = WNEXT("adag%d" % j)
            Wv = W[:, 0:4096].rearrange("p (k n) -> p k n", k=8)
            for k in range(2):
                ps = PS()
                for kc in range(8):
                    S.op("pe", lambda e: e.matmul(ps[:, :], lhsT=scT[:, kc, k * 128:(k + 1) * 128], rhs=Wv[:, kc, :],
                                                  start=(kc == 0), stop=(kc == 7)), reads=[W, scT], writes=[ps])
                if k == 0:
                    S.op("dve", lambda e: e.tensor_tensor(out=X[:, 2 + l, j * 512:(j + 1) * 512], in0=ps[:, :],
                                                          in1=bct[:, j * 512:(j + 1) * 512], op=ALU.add),
                         reads=[ps, bct], writes=[X.d[2 + l]])
                else:
                    S.op("dve", lambda e: e.tensor_tensor(out=p["gate"][:, j * 512:(j + 1) * 512], in0=ps[:, :],
                                                          in1=bct[:, j * 512:(j + 1) * 512], op=ALU.add),
                         reads=[ps, bct], writes=[p["gate"]])

    LN_KS = -0.5 * math.log(128.0)

    def evac_engine():
        rr["ev"] += 1
        return "dve" if rr["ev"] % 2 else "act"

    def bias_mm(ps_ap, ps_t, W, n):
        S.op("pe", lambda e: e.matmul(ps_ap, lhsT=onesb[0:2, 0:128], rhs=W[0:2, 4096:4096 + n], start=False, stop=True),
             reads=[cbf, W], writes=[ps_t])

    def rstd_from_ssq(src_ap, src_t, dst_ap, dst_t, n, inv_n):
        S.op("act", lambda e: e.activation(out=dst_ap, in_=src_ap, func=AF.Ln, bias=epsb[:, 0:1], scale=inv_n),
             reads=[src_t, epsb], writes=[dst_t])
        S.op("act", lambda e: e.activation(out=dst_ap, in_=dst_ap, func=AF.Exp, scale=-0.5), reads=[dst_t], writes=[dst_t])

    S.op("pool", lambda e: e.memset(epsb[:, 0:1], EPS), writes=[epsb])
    S.op("pool", lambda e: e.memset(epsb[:, 1:2], 1.0), writes=[epsb])
    S.op("pool", lambda e: e.memset(epsb[:, 2:3], LN_KS), writes=[epsb])
    S.op("pool", lambda e: e.memset(epsb[:, 3:4], 0.0), writes=[epsb])

    def process_tile(kind, sti, l, last_layer):
        p = prm[l]
        PL_ = "pool" if kind == 0 else "dve"
        nsub = 4 if kind == 0 else 1
        TT = nsub * 128
        first = (kind == 0 and sti == 0)
        last = (kind == 1) or (sti == NST - 1)
        cur_mask_b = cbf[:, C_CUR:C_CUR + 128] if kind == 0 else cbf[:, C_BD:C_BD + 128]
        U32 = cst[:, C_CUR:C_CUR + 128] if kind == 0 else cst[:, C_BD:C_BD + 128]

        for s in range(nsub):
            xb = xn32 if s % 2 == 0 else xn32b
            c0_ = 2 * s
            S.op("act", lambda e: e.activation(out=junk[:, :], in_=X[:, s, :], func=AF.Square, accum_out=st4[:, c0_:c0_ + 1]),
                 reads=[X.d[s]], writes=[junk, st4.d[c0_]])
            rstd_from_ssq(st4[:, c0_:c0_ + 1], st4.d[c0_], st4[:, c0_ + 1:c0_ + 2], st4.d[c0_ + 1], 1, 1.0 / D)
            S.op("dve", lambda e: e.tensor_scalar(out=xb[:, :], in0=X[:, s, :], scalar1=st4[:, c0_ + 1:c0_ + 2], scalar2=None,
                                                  op0=ALU.mult), reads=[X.d[s], st4.d[c0_ + 1]], writes=[xb])
            pa, pb = PS(), PS()
            for kc in range(8):
                pp = pa if kc < 4 else pb
                S.op("pe", lambda e: e.transpose(out=pp[:, (kc % 4) * 128:(kc % 4 + 1) * 128], in_=xb[:, kc * 128:(kc + 1) * 128],
                                                 identity=ident32), reads=[xb, cst], writes=[pp])
            for kc in range(8):
                pp = pa if kc < 4 else pb
                src = pp[:, (kc % 4) * 128:(kc % 4 + 1) * 128]
                dst = hT[:, kc, s * 128:(s + 1) * 128]
                if kind == 0:
                    if kc < 4:
                        S.op("dve", lambda e: e.tensor_scalar(out=dst, in0=src, scalar1=p["Gp"][:, kc:kc + 1],
                                                              scalar2=p["Sp"][:, kc:kc + 1], op0=ALU.mult, op1=ALU.add),
                             reads=[pp, p["Gp"], p["Sp"]], writes=[hT.d[kc]])
                    else:
                        S.op("act", lambda e: e.activation(out=dst, in_=src, func=AF.Identity, bias=p["Sp"][:, kc:kc + 1],
                                                           scale=p["Gp"][:, kc:kc + 1]),
                             reads=[pp, p["Gp"], p["Sp"]], writes=[hT.d[kc]])
                else:
                    t = TMP()
                    S.op("dve", lambda e: e.tensor_tensor(out=t[:, 0:128], in0=src, in1=p["Gs"][:, kc, :], op=ALU.mult),
                         reads=[pp, p["Gs"]], writes=[t])
                    S.op(PL_, lambda e: e.tensor_tensor(out=dst, in0=t[:, 0:128], in1=p["Ss"][:, kc, :], op=ALU.add),
                         reads=[t, p["Ss"]], writes=[hT.d[kc]])

        def fm_chunk(W, Wv, cols, kcn=8, rhsT=None):
            ps = PS()
            for kc in range(kcn):
                S.op("pe", lambda e: e.matmul(ps[:, 0:TT], lhsT=Wv[:, kc, cols], rhs=hT[:, kc, 0:TT],
                                              start=(kc == 0), stop=(kc == kcn - 1)), reads=[W, hT], writes=[ps])
            return ps

        def tm_tile(W, Wv, s, cols, n, bo):
            ps = PS()
            for kc in range(8):
                S.op("pe", lambda e: e.matmul(ps[:, 0:n], lhsT=hT[:, kc, s * 128:(s + 1) * 128], rhs=Wv[:, kc, cols],
                                              start=(kc == 0), stop=False), reads=[W, hT], writes=[ps])
            bias_mm(ps[:, 0:n], ps, W, n)
            return ps

        if kind == 1 and l == 0:
            dbg("hT_s0", hT[:, :, 0:128], [hT])
            dbg("Gs0", p["Gs"][:, :, :], [p["Gs"]])
            dbg("Ss0", p["Ss"][:, :, :], [p["Ss"]])
            dbg("xn32", xn32[:, :], [xn32])
        ckpt("P0 %d %d %d" % (kind, sti, l))
        kT = kTall[l]
        def qk_norm(ps, bias_ap, g_ap, g_t, out_bf, out_t, out32=None):
            q32 = TMP(); sq = TMP()
            S.op("act", lambda e: e.activation(out=q32[:, 0:TT], in_=ps[:, 0:TT], func=AF.Identity, bias=bias_ap, scale=1.0),
                 reads=[ps, p["bfm"]], writes=[q32])
            S.op(PL_, lambda e: e.tensor_tensor(out=sq[:, 0:TT], in0=q32[:, 0:TT], in1=q32[:, 0:TT], op=ALU.mult),
                 reads=[q32], writes=[sq])
            pq = PS()
            S.op("pe", lambda e: e.matmul(pq[:, 0:TT], lhsT=blk32, rhs=sq[:, 0:TT], start=True, stop=True),
                 reads=[cst, sq], writes=[pq])
            r = TMP()
            S.op("act", lambda e: e.activation(out=r[:, 0:TT], in_=pq[:, 0:TT], func=AF.Ln, bias=epsb[:, 0:1], scale=1.0 / 64),
                 reads=[pq, epsb], writes=[r])
            S.op("act", lambda e: e.activation(out=r[:, 0:TT], in_=r[:, 0:TT], func=AF.Exp, scale=-0.5), reads=[r], writes=[r])
            S.op("dve", lambda e: e.scalar_tensor_tensor(out=out_bf, in0=q32[:, 0:TT], scalar=g_ap, in1=r[:, 0:TT],
                                                         op0=ALU.mult, op1=ALU.mult), reads=[q32, g_t, r], writes=[out_t])
            if out32 is not None:
                S.op("dve", lambda e: e.scalar_tensor_tensor(out=out32[0], in0=q32[:, TT - 128:TT], scalar=g_ap, in1=r[:, TT - 128:TT],
                                                             op0=ALU.mult, op1=ALU.mult), reads=[q32, g_t, r], writes=[out32[1]])

        def gen_A():
            W = WNEXT("av"); Wv = W[:, 0:4096].rearrange("p (k n) -> p k n", k=8)
            if kind == 1 and l == 0:
                dbg("wav", W[:, :], [W])
            for s in range(nsub):
                ps = tm_tile(W, Wv, s, slice(0, 512), 512, BT_AV)
                c1_ = 8 + 2 * s
                S.op("act", lambda e: e.activation(out=junk[:, 0:512], in_=ps[:, :], func=AF.Square, accum_out=st4[:, c1_:c1_ + 1]),
                     reads=[ps], writes=[junk, st4.d[c1_]])
                rstd_from_ssq(st4[:, c1_:c1_ + 1], st4.d[c1_], st4[:, c1_ + 1:c1_ + 2], st4.d[c1_ + 1], 1, 1.0 / 512)
                if kind == 1:
                    t = TMP()
                    S.op("dve", lambda e: e.scalar_tensor_tensor(out=t[:, 0:512], in0=ps[:, :], scalar=st4[:, c1_ + 1:c1_ + 2], in1=p["vgB"][:, :],
                                                                 op0=ALU.mult, op1=ALU.mult), reads=[ps, st4.d[c1_ + 1], p["vgB"]], writes=[t])
                    S.dma(gv_o[l, :, :], t[:, 0:512], reads=[t])
                    S.op(PL_, lambda e: e.tensor_copy(out=vn[:, s, :], in_=t[:, 0:512]), reads=[t], writes=[vn])
                else:
                    S.op("dve", lambda e: e.scalar_tensor_tensor(out=vn[:, s, :], in0=ps[:, :], scalar=st4[:, c1_ + 1:c1_ + 2], in1=p["vgB"][:, :],
                                                                 op0=ALU.mult, op1=ALU.mult), reads=[ps, st4.d[c1_ + 1], p["vgB"]], writes=[vn])
            Wg_ = WNEXT("ag"); Wgv = Wg_[:, 0:4096].rearrange("p (k n) -> p k n", k=8)
            Wu_ = WNEXT("au", hold=1); Wuv = Wu_[:, 0:4096].rearrange("p (k n) -> p k n", k=8)
            gbase, gcol = GB[(l, kind)]
            for c in range(4):
                tA = TMP()
                ps = fm_chunk(Wg_, Wgv, slice(c * 128, (c + 1) * 128))
                S.op("act", lambda e: e.activation(out=tA[:, 0:TT], in_=ps[:, 0:TT], func=AF.Silu,
                                                   bias=p["bfm"][:, BF_AG + c:BF_AG + c + 1], scale=1.0),
                     reads=[ps, p["bfm"]], writes=[tA])
                ps = fm_chunk(Wu_, Wuv, slice(c * 128, (c + 1) * 128))
                S.op("dve", lambda e: e.scalar_tensor_tensor(out=tA[:, 0:TT], in0=ps[:, 0:TT], scalar=p["bfm"][:, BF_AU + c:BF_AU + c + 1],
                                                             in1=tA[:, 0:TT], op0=ALU.add, op1=ALU.mult),
                     reads=[ps, p["bfm"], tA], writes=[tA])
                if kind == 1 and l == 0 and c == 0:
                    dbg("tA0", tA[:, 0:128], [tA]); dbg("WT0", p["WT"][:, :, :, :], [p["WT"]]); dbg("gb", gb[:, :], [gb])
                ps = PS()
                for s in range(nsub):
                    o = ps[:, s * 128:(s + 1) * 128]
                    S.op("pe", lambda e: e.matmul(o, lhsT=vn[:, s, c * 128:(c + 1) * 128], rhs=p["WT"][:, kind, c, :],
                                                  start=True, stop=False), reads=[vn, p["WT"]], writes=[ps])
                    S.op("pe", lambda e: e.matmul(o, lhsT=onesb[gbase:gbase + 2, :], rhs=gb[gbase:gbase + 2, gcol + c * 128: gcol + (c + 1) * 128],
                                                  start=False, stop=True), reads=[cbf, gb], writes=[ps])
                S.op("dve", lambda e: e.tensor_tensor(out=yaT[:, c, 0:TT], in0=ps[:, 0:TT], in1=tA[:, 0:TT], op=ALU.mult),
                     reads=[ps, tA], writes=[yaT])
                yield

            yield
        def gen_Bproj():
            W = WNEXT("bq"); Wv = W[:, 0:4096].rearrange("p (k n) -> p k n", k=8)
            for c in range(4):
                ps = fm_chunk(W, Wv, slice(c * 128, (c + 1) * 128))
                qk_norm(ps, p["bfm"][:, BF_BQ + c:BF_BQ + c + 1], p["gq"][:, 0:1], p["gq"], qT[:, c, 0:TT], qT)
            W = WNEXT("bkv"); Wv = W[:, 0:2048].rearrange("p (k n) -> p k n", k=8)
            ps = fm_chunk(W, Wv, slice(0, 128))
            qk_norm(ps, p["bfm"][:, BF_BK:BF_BK + 1], p["gk"][:, 0:1], p["gk"], kT[:, 128:128 + TT], kT,
                    out32=((k32[:, :], k32) if last else None))
            for s in range(nsub):
                ps = tm_tile(W, Wv, s, slice(128, 256), 128, BT_BV)
                S.op("act", lambda e: e.activation(out=vall[l][:, 1 + s, :], in_=ps[:, 0:128], func=AF.Copy), reads=[ps], writes=[vall[l]])
                if last and s == nsub - 1:
                    S.op("dve", lambda e: e.tensor_copy(out=v32[:, :], in_=ps[:, 0:128]), reads=[ps], writes=[v32])
            if last:
                pk = PS()
                S.op("pe", lambda e: e.transpose(out=pk[:, 0:128], in_=k32[:, :], identity=ident32), reads=[k32, cst], writes=[pk])
                t = TMP()
                S.op("dve", lambda e: e.tensor_copy(out=t[:, 0:128], in_=pk[:, 0:128]), reads=[pk], writes=[t])
                if kind == 0:
                    if not _os.environ.get("KSKIP1"):
                        S.dma(kp_o[l, :, :], t[:, 0:128], reads=[t])
                    if not _os.environ.get("KSKIP2"):
                        tv = TMP()
                        S.op("dve", lambda e: e.tensor_copy(out=tv[:, 0:128], in_=v32[:, :]), reads=[v32], writes=[tv])
                        S.dma(vp_o[l, :, :], tv[:, 0:128], reads=[tv])
                else:
                    for (src_t, dst_o, cin) in [(t, ks_o, ck_in), (v32, vs_o, cv_in)]:
                        S.dma(dst_o[l, :, 0:120, :], cin[l, :, 8:128, :])
                        for t_ in range(8):
                            S.dma(dst_o[l, :, 120 + t_, :], pstride(src_t[t_:t_ + 1, 0:128], 8, 16, [[1, 128]]), reads=[src_t])
            W = WNEXT("bg"); Wv = W[:, 0:4096].rearrange("p (k n) -> p k n", k=8)
            for c in range(4):
                ps = fm_chunk(W, Wv, slice(c * 128, (c + 1) * 128))
                S.op("act", lambda e: e.activation(out=sbg[:, c, 0:TT], in_=ps[:, 0:TT], func=AF.Silu,
                                                   bias=p["bfm"][:, BF_BG + c:BF_BG + c + 1], scale=1.0), reads=[ps, p["bfm"]], writes=[sbg])
            if kind == 1:
                for bg in range(4):
                    kc32 = TMP()
                    kc32v = kc32[:, 0:512].rearrange("p (b f) -> p b f", b=4)
                    kcbv = junk[:, 0:512].rearrange("p (b f) -> p b f", b=4)
                    S.dma(kc32v, ck_in[l, bg * 4:(bg + 1) * 4, :, :].rearrange("b j f -> j b f"), writes=[kc32])
                    S.op("dve", lambda e: e.tensor_copy(out=kcbv, in_=kc32v), reads=[kc32], writes=[junk])
                    pk = PS(); pv = pk[:].bitcast(BF16)
                    for bb in range(4):
                        S.op("pe", lambda e: e.transpose(out=pv[:, bb * 128:(bb + 1) * 128], in_=kcbv[:, bb, :], identity=identb),
                             reads=[junk, cbf], writes=[pk])
                    S.op("act", lambda e: e.activation(out=kcT[:, bg * 4:(bg + 1) * 4, :], in_=pv[:, 0:512].rearrange("p (b j) -> p b j", b=4),
                                                       func=AF.Copy), reads=[pk], writes=[kcT])
                    vc32 = TMP()
                    vc32v = vc32[:, 0:512].rearrange("p (b f) -> p b f", b=4)
                    S.dma(vc32v, cv_in[l, bg * 4:(bg + 1) * 4, :, :].rearrange("b j f -> j b f"), writes=[vc32])
                    S.op("dve", lambda e: e.tensor_copy(out=vcb[:, bg * 4:(bg + 1) * 4, :], in_=vc32v), reads=[vc32], writes=[vcb])
            yield
        def gen_att():
            for s in range(nsub):
                O = psum[(2 * s) % 4]; Dn = psum[(2 * s + 1) % 4]
                O.idx = (2 * s) % 4; Dn.idx = (2 * s + 1) % 4
                ps_resv.add(O.idx); ps_resv.add(Dn.idx)
                lt_i = [0]

                def LTB():
                    b_ = psum[4 + lt_i[0] % 4]
                    lt_i[0] += 1
                    return b_
                blks = []
                if kind == 0 and not (first and s == 0):
                    blks.append(("prev", s))
                blks.append(("cur", s + 1))
                nb = len(blks)
                pts = {}
                for kv in range(2):
                    rows = slice(kv * 64, (kv + 1) * 64)
                    for bi, (bn, blk) in enumerate(blks):
                        LT = LTB()
                        S.op("pe", lambda e: e.matmul(LT[:, :], lhsT=kT[rows, blk * 128:(blk + 1) * 128],
                                                      rhs=qT[rows, :, s * 128:(s + 1) * 128], start=True, stop=True),
                             reads=[kT, qT], writes=[LT])
                        pt = PT[kv * 2 + bi] if kind == 0 else PT[kv]
                        S.op("act", lambda e: e.activation(out=pt[:, :], in_=LT[:, :], func=AF.Exp, scale=0.125), reads=[LT], writes=[pt])
                        mk = cur_mask_b if bn == "cur" else cbf[:, C_PREV:C_PREV + 128]
                        mk3 = ap_like(mk, [[0, 4], [1, 128]])
                        S.op(PL_, lambda e: e.tensor_tensor(out=pt[:, :].rearrange("p (c t) -> p c t", c=4),
                                                               in0=pt[:, :].rearrange("p (c t) -> p c t", c=4), in1=mk3, op=ALU.mult),
                             reads=[pt, cbf], writes=[pt])
                        pts[(kv, bi)] = pt
                        yield
                for kv in range(2):
                    rows = slice(kv * 64, (kv + 1) * 64)
                    ncache = NSB if kind == 1 else 0
                    for bi, (bn, blk) in enumerate(blks):
                        pt = pts[(kv, bi)]
                        S.op("pe", lambda e: e.matmul(O[rows, :], lhsT=vall[l][:, blk, kv * 64:(kv + 1) * 64], rhs=pt[:, :],
                                                      start=(bi == 0), stop=(bi == nb - 1 and ncache == 0)), reads=[vall[l], pt], writes=[O])
                    for bi, (bn, blk) in enumerate(blks):
                        pt = pts[(kv, bi)]
                        S.op("pe", lambda e: e.matmul(Dn[rows, :], lhsT=onesb[:, 0:64], rhs=pt[:, :],
                                                      start=(bi == 0), stop=(bi == nb - 1 and ncache == 0)), reads=[cbf, pt], writes=[Dn])
                    if kind == 1:
                        LC = LTB()
                        for b in range(NSB):
                            S.op("pe", lambda e: e.matmul(LC[:, b * 32:(b + 1) * 32], lhsT=kcT[rows, b, :],
                                                          rhs=qT[rows, :, b * 8:(b + 1) * 8], start=True, stop=True),
                                 reads=[kcT, qT], writes=[LC])
                        ptc = PTc[kv]
                        S.op("act", lambda e: e.activation(out=ptc[:, :], in_=LC[:, :], func=AF.Exp, scale=0.125), reads=[LC], writes=[ptc])
                        mk = cbf[:, C_CACHE:C_CACHE + 8]
                        mk3 = ap_like(mk, [[0, 64], [1, 8]])
                        S.op(PL_, lambda e: e.tensor_tensor(out=ptc[:, :].rearrange("p (x t) -> p x t", t=8),
                                                               in0=ptc[:, :].rearrange("p (x t) -> p x t", t=8), in1=mk3, op=ALU.mult),
                             reads=[ptc, cbf], writes=[ptc])
                        for b in range(NSB):
                            oo = O[rows, :].rearrange("p (c t) -> p c t", c=4)[:, :, b * 8:(b + 1) * 8]
                            dd = Dn[rows, :].rearrange("p (c t) -> p c t", c=4)[:, :, b * 8:(b + 1) * 8]
                            rh = ptc[:, b * 32:(b + 1) * 32].rearrange("p (c t) -> p c t", c=4)
                            S.op("pe", lambda e: e.matmul(oo, lhsT=vcb[:, b, kv * 64:(kv + 1) * 64], rhs=rh, start=False,
                                                          stop=(b == NSB - 1)), reads=[vcb, ptc], writes=[O])
                            S.op("pe", lambda e: e.matmul(dd, lhsT=onesb[:, 0:64], rhs=rh, start=False, stop=(b == NSB - 1)),
                                 reads=[cbf, ptc], writes=[Dn])
                yield
                dsb = TMP()
                es3 = ap_like(p["esink"][:, 0:4], [[1, 4], [0, 128]])
                S.op("dve", lambda e: e.tensor_tensor(out=dsb[:, 0:512].rearrange("p (c t) -> p c t", c=4),
                                                      in0=Dn[:, :].rearrange("p (c t) -> p c t", c=4), in1=es3, op=ALU.add),
                     reads=[Dn, p["esink"]], writes=[dsb])
                S.op("act", lambda e: e.activation(out=dsb[:, 0:512], in_=dsb[:, 0:512], func=AF.Ln), reads=[dsb], writes=[dsb])
                S.op("act", lambda e: e.activation(out=dsb[:, 0:512], in_=dsb[:, 0:512], func=AF.Exp, scale=-1.0), reads=[dsb], writes=[dsb])
                S.op(PL_, lambda e: e.tensor_tensor(out=dsb[:, 0:512].rearrange("p (c t) -> p c t", c=4),
                                                       in0=dsb[:, 0:512].rearrange("p (c t) -> p c t", c=4),
                                                       in1=sbg[:, :, s * 128:(s + 1) * 128], op=ALU.mult), reads=[dsb, sbg], writes=[dsb])
                S.op("dve", lambda e: e.tensor_tensor(out=ybT[:, :, s * 128:(s + 1) * 128], in0=O[:, :].rearrange("p (c t) -> p c t", c=4),
                                                      in1=dsb[:, 0:512].rearrange("p (c t) -> p c t", c=4), op=ALU.mult),
                     reads=[O, dsb], writes=[ybT])
                ps_resv.discard(O.idx); ps_resv.discard(Dn.idx)
                yield
            if kind == 0:
                S.op(PL_, lambda e: e.tensor_copy(out=kT[:, 0:128], in_=kT[:, 512:640]), reads=[kT], writes=[kT])
                S.op(PL_, lambda e: e.tensor_copy(out=vall[l][:, 0, :], in_=vall[l][:, 4, :]), reads=[vall[l]], writes=[vall[l]])

            yield
        def gen_Cproj():
            nb_ = 1 if kind == 0 else NSB
            Lc = TT // nb_
            for half, (gname, dstT, bf0) in enumerate([("cq", qTm, BF_CQ), ("ck", kTm, BF_CK)]):
                W = WNEXT(gname); Wv = W[:, 0:4096].rearrange("p (k n) -> p k n", k=8)
                for c in range(4):
                    ch = half * 4 + c
                    ps = fm_chunk(W, Wv, slice(c * 128, (c + 1) * 128))
                    sg = stg[ch % len(stg)]
                    sgv = sg[:, 0:nb_ * (3 + Lc)].rearrange("p (b x) -> p b x", b=nb_)
                    if kind == 0:
                        S.op(PL_, lambda e: e.tensor_copy(out=sgv[:, :, 0:3], in_=ccarry[l][:, ch:ch + 1, :]),
                             reads=[ccarry[l]], writes=[sg])
                    else:
                        t = TMP()
                        S.dma(t[0:48, 0:128], sconv[l, :, :, ch * 128:(ch + 1) * 128].rearrange("b j c -> (b j) c"), writes=[t])
                        pc = PS()
                        S.op("pe", lambda e: e.transpose(out=pc[:, 0:48], in_=t[0:48, 0:128], identity=cst[0:48, C_ID:C_ID + 48]),
                             reads=[t, cst], writes=[pc])
                        S.op("dve", lambda e: e.tensor_copy(out=sgv[:, :, 0:3], in_=pc[:, 0:48].rearrange("p (b j) -> p b j", j=3)),
                             reads=[pc], writes=[sg])
                    S.op("act", lambda e: e.activation(out=sgv[:, :, 3:3 + Lc], in_=ps[:, 0:TT].rearrange("p (b x) -> p b x", b=nb_),
                                                       func=AF.Identity, bias=p["bfm"][:, bf0 + c:bf0 + c + 1], scale=1.0),
                         reads=[ps, p["bfm"]], writes=[sg])
                    if kind == 0:
                        S.op(PL_, lambda e: e.tensor_copy(out=ccarry[l][:, ch:ch + 1, :], in_=sgv[:, :, Lc:Lc + 3]),
                             reads=[sg], writes=[ccarry[l]])
                    if last:
                        if kind == 0:
                            src = sg[:, 3 + TT - 128:3 + TT]
                        else:
                            src = None
                        pc = PS()
                        if kind == 0:
                            S.op("pe", lambda e: e.transpose(out=pc[:, 0:128], in_=src, identity=ident32), reads=[sg, cst], writes=[pc])
                        else:
                            t2 = TMP()
                            S.op(PL_, lambda e: e.tensor_copy(out=t2[:, 0:128].rearrange("p (b x) -> p b x", b=NSB), in_=sgv[:, :, 3:11]),
                                 reads=[sg], writes=[t2])
                            S.op("pe", lambda e: e.transpose(out=pc[:, 0:128], in_=t2[:, 0:128], identity=ident32), reads=[t2, cst], writes=[pc])
                        S.op("dve", lambda e: e.tensor_copy(out=convTM[:, ch * 128:(ch + 1) * 128], in_=pc[:, 0:128]),
                             reads=[pc], writes=[convTM])
                    acc = TMP()
                    av = acc[:, 0:TT].rearrange("p (b x) -> p b x", b=nb_)
                    S.op("dve", lambda e: e.tensor_scalar(out=av, in0=sgv[:, :, 0:Lc], scalar1=p["cw"][:, ch, 0:1], scalar2=p["cb"][:, ch:ch + 1],
                                                          op0=ALU.mult, op1=ALU.add), reads=[sg, p["cw"], p["cb"]], writes=[acc])
                    for j in range(1, 4):
                        S.op("dve", lambda e: e.scalar_tensor_tensor(out=av, in0=sgv[:, :, j:j + Lc], scalar=p["cw"][:, ch, j:j + 1], in1=av,
                                                                     op0=ALU.mult, op1=ALU.add), reads=[sg, p["cw"], acc], writes=[acc])
                    S.op("act", lambda e: e.activation(out=dstT[:, c, 0:TT], in_=acc[:, 0:TT], func=AF.Silu), reads=[acc], writes=[dstT])
                    yield
            if last:
                if kind == 0:
                    S.dma(convp_o[l, :, :], convTM[125:128, :], reads=[convTM])
                else:
                    for j_ in range(3):
                        S.dma(convs_o[l, :, j_, :], pstride(convTM[5 + j_:6 + j_, :], 8, 16, [[1, D]]), reads=[convTM])
            W = WNEXT("cv"); Wv = W[:, 0:4096].rearrange("p (k n) -> p k n", k=8)
            for s in range(nsub):
                ps = tm_tile(W, Wv, s, slice(0, 512), 512, BT_CV)
                S.op("act", lambda e: e.activation(out=vaug[:, s, :, 0:128], in_=ps[:, :].rearrange("p (h d) -> p h d", h=4), func=AF.Copy),
                     reads=[ps], writes=[vaug])
                yield
            W = WNEXT("cif"); Wv = W[:, 0:64].rearrange("p (k n) -> p k n", k=8)
            n4 = nsub * 4
            pcif = PS()
            for s in range(nsub):
                for kc in range(8):
                    S.op("pe", lambda e: e.matmul(pcif[:, s * 8:(s + 1) * 8], lhsT=hT[:, kc, s * 128:(s + 1) * 128], rhs=Wv[:, kc, 0:8],
                                                  start=(kc == 0), stop=False), reads=[W, hT], writes=[pcif])
                bias_mm(pcif[:, s * 8:(s + 1) * 8], pcif, W, 8)
            pv3 = pcif[:, 0:nsub * 8].rearrange("p (s c) -> p s c", c=8)
            xf = TMP()
            v3 = lambda o: xf[:, o:o + n4].rearrange("p (s h) -> p s h", h=4)
            fb3 = ap_like(p["fbB"][:, 0:4], [[0, nsub], [1, 4]])
            S.op("dve", lambda e: e.tensor_tensor(out=v3(0), in0=pv3[:, :, 4:8], in1=fb3, op=ALU.add),
                 reads=[pcif, p["fbB"]], writes=[xf])
            S.op("act", lambda e: e.activation(out=xf[:, 16:16 + n4], in_=xf[:, 0:n4], func=AF.Exp, scale=-1.0), reads=[xf], writes=[xf])
            S.op("act", lambda e: e.activation(out=xf[:, 32:32 + n4], in_=xf[:, 16:16 + n4], func=AF.Ln, bias=epsb[:, 1:2], scale=1.0),
                 reads=[xf, epsb], writes=[xf])
            S.op("dve", lambda e: e.tensor_copy(out=v3(48), in_=pv3[:, :, 0:4]), reads=[pcif], writes=[xf])
            pc = PS()
            S.op("pe", lambda e: e.matmul(pc[:, 0:n4], lhsT=U32, rhs=xf[:, 32:32 + n4], start=True, stop=True), reads=[cst, xf], writes=[pc])
            S.op("pe", lambda e: e.matmul(pc[:, 16:16 + n4], lhsT=ones32, rhs=xf[:, 32:32 + n4], start=True, stop=True),
                 reads=[cst, xf], writes=[pc])
            S.op("dve", lambda e: e.tensor_tensor(out=xf[:, 64:64 + n4], in0=xf[:, 48:48 + n4], in1=pc[:, 0:n4], op=ALU.add),
                 reads=[xf, pc], writes=[xf])
            fl = lambda t_: t_[:, :, :].rearrange("p s h -> p (s h)")[:, 0:n4]
            S.op("act", lambda e: e.activation(out=fl(bS), in_=xf[:, 64:64 + n4], func=AF.Exp, bias=epsb[:, 2:3], scale=1.0),
                 reads=[xf, epsb], writes=[bS])
            S.op("act", lambda e: e.activation(out=fl(ecnS), in_=pc[:, 0:n4], func=AF.Exp), reads=[pc], writes=[ecnS])
            if kind == 0:
                S.op("act", lambda e: e.activation(out=fl(etotS), in_=pc[:, 16:16 + n4], func=AF.Exp, scale=-1.0), reads=[pc], writes=[etotS])
            else:
                S.op("dve", lambda e: e.tensor_copy(out=xf[:, 80:84], in_=xf[:, 64:68]), reads=[xf], writes=[xf])
                S.op("dve", lambda e: e.tensor_copy(out=xf[:, 84:88], in_=pc[:, 0:4]), reads=[pc], writes=[xf])
                S.dma(scr1[l, :, :], xf[:, 80:88], reads=[xf], writes=[scr_dep["scr1"]])
            for s in range(nsub):
                for h in range(4):
                    eng = "dve" if h % 2 == 0 else "pool"
                    S.op(eng, lambda e: e.tensor_scalar(out=vtil[:, s, h, :], in0=vaug[:, s, h, :], scalar1=bS[:, s, h:h + 1], scalar2=None,
                                                        op0=ALU.mult), reads=[vaug, bS], writes=[vtil])
                yield
            if kind == 0:
                for s in range(nsub):
                    S.op("dve", lambda e: e.tensor_tensor(out=xf[:, 96:100], in0=xf[:, 64 + s * 4:68 + s * 4], in1=PL[l][:, :], op=ALU.add),
                         reads=[xf, PL[l]], writes=[xf])
                    S.op("dve", lambda e: e.tensor_tensor(out=Rm[l][:, :], in0=Rm[l][:, :], in1=xf[:, 96:100], op=ALU.max),
                         reads=[xf, Rm[l]], writes=[Rm[l]])
                    S.op("dve", lambda e: e.tensor_tensor(out=PL[l][:, :], in0=PL[l][:, :], in1=pc[:, 16 + s * 4:20 + s * 4], op=ALU.add),
                         reads=[pc, PL[l]], writes=[PL[l]])
            W = WNEXT("co"); Wv = W[:, 0:4096].rearrange("p (k n) -> p k n", k=8)
            for s in range(nsub):
                ps = tm_tile(W, Wv, s, slice(0, 512), 512, BT_CO)
                t = TMP()
                S.op("act", lambda e: e.activation(out=t[:, 0:512], in_=ps[:, :], func=AF.Sigmoid), reads=[ps], writes=[t])
                S.op(PL_, lambda e: e.tensor_tensor(out=ogg[:, s, :], in0=t[:, 0:512], in1=p["hgB"][:, :], op=ALU.mult),
                     reads=[t, p["hgB"]], writes=[ogg])
                yield
            W = WNEXT("cg"); Wv = W[:, 0:4096].rearrange("p (k n) -> p k n", k=8)
            for s in range(nsub):
                ps = tm_tile(W, Wv, s, slice(0, 512), 512, BT_CG)
                t = TMP()
                S.op("act", lambda e: e.activation(out=t[:, 0:512], in_=ps[:, :], func=AF.Silu), reads=[ps], writes=[t])
                S.op(PL_, lambda e: e.tensor_tensor(out=ogg[:, s, :], in0=ogg[:, s, :], in1=t[:, 0:512], op=ALU.mult),
                     reads=[t, ogg], writes=[ogg])
                yield

            for s in range(nsub):
                pk = PS(); pv = pk[:].bitcast(BF16)
                for h in range(4):
                    S.op("pe", lambda e: e.transpose(out=pv[:, h * 128:(h + 1) * 128], in_=kTm[:, h, s * 128:(s + 1) * 128], identity=identb),
                         reads=[kTm, cbf], writes=[pk])
                S.op("act", lambda e: e.activation(out=kTM[:, s, :, :], in_=pv[:, 0:512].rearrange("p (h d) -> p h d", h=4), func=AF.Copy),
                     reads=[pk], writes=[kTM])
                yield
            if kind == 1:
                eT = sm4
                S.dma(eT[0:4, 0:128].rearrange("h (b t) -> h b t", t=8),
                      bass.AP(scr1.tensor, l * 1024, [[1, 4], [64, 16], [8, 8]]), reads=[scr_dep["scr1"]], writes=[sm4],
                      allow_slow_non_contiguous=True)
                S.dma(eT[0:4, 128:144], bass.AP(scr1.tensor, l * 1024 + 7 * 8 + 4, [[1, 4], [64, 16]]), reads=[scr_dep["scr1"]], writes=[sm4],
                      allow_slow_non_contiguous=True)
                S.dma(eT[0:4, 144:160], sm[l, :, :].rearrange("b h -> h b"), writes=[sm4], allow_slow_non_contiguous=True)
                S.op("dve", lambda e: e.tensor_reduce(out=eT[0:4, 160:176], in_=eT[0:4, 0:128].rearrange("h (b t) -> h b t", t=8),
                                                      axis=AX.X, op=ALU.max), reads=[sm4], writes=[sm4])
                S.op("dve", lambda e: e.tensor_tensor(out=eT[0:4, 176:192], in0=eT[0:4, 160:176], in1=eT[0:4, 144:160], op=ALU.max),
                     reads=[sm4], writes=[sm4])
                S.op("dve", lambda e: e.tensor_tensor(out=eT[0:4, 192:208], in0=eT[0:4, 176:192], in1=eT[0:4, 128:144], op=ALU.subtract),
                     reads=[sm4], writes=[sm4])
                S.dma(ms_o[l, :, :].rearrange("b h -> h b"), eT[0:4, 192:208], reads=[sm4], allow_slow_non_contiguous=True)
                S.op("act", lambda e: e.activation(out=eT[0:4, 208:224], in_=eT[0:4, 176:192], func=AF.Exp, scale=-1.0), reads=[sm4], writes=[sm4])
                S.op("dve", lambda e: e.tensor_tensor(out=eT[0:4, 224:240], in0=eT[0:4, 144:160], in1=eT[0:4, 176:192], op=ALU.subtract),
                     reads=[sm4], writes=[sm4])
                S.op("act", lambda e: e.activation(out=eT[0:4, 224:240], in_=eT[0:4, 224:240], func=AF.Exp), reads=[sm4], writes=[sm4])
                S.op("act", lambda e: e.activation(out=eT[0:4, 240:256], in_=eT[0:4, 144:160], func=AF.Exp), reads=[sm4], writes=[sm4])
                S.dma(scr2[l, :, :, :].rearrange("k b h -> h k b"), eT[0:4, 208:256].rearrange("h (k b) -> h k b", k=3),
                      reads=[sm4], writes=[scr_dep["scr2"]], allow_slow_non_contiguous=True)
                S.dma(AB[:, :, :, :].rearrange("p k b h -> p (k b h)"),
                      bcast_row(scr2[l:l + 1, :, :, :], 128, 192),
                      reads=[scr_dep["scr2"]], writes=[AB])

            yield
        def core_st(s):
            AT_ = AT if s % 2 == 0 else ATb
            ST = PS()
            for h in range(4):
                S.op("pe", lambda e: e.matmul(ST[:, h * 128:(h + 1) * 128], lhsT=kTm[:, h, s * 128:(s + 1) * 128],
                                              rhs=qTm[:, h, s * 128:(s + 1) * 128], start=True, stop=True), reads=[kTm, qTm], writes=[ST])
            for h in range(4):
                S.op("dve", lambda e: e.scalar_tensor_tensor(out=AT_[:, h, :], in0=ST[:, h * 128:(h + 1) * 128], scalar=bS[:, s, h:h + 1],
                                                             in1=cur_mask_b, op0=ALU.mult, op1=ALU.mult), reads=[ST, bS, cbf], writes=[AT_])

        def core_part1(s):
            AT_ = AT if s % 2 == 0 else ATb
            if kind == 0:
                NB = [PS(), PS()]
                for pb in NB:
                    ps_resv.add(pb.idx)
                nbv = lambda h: NB[h // 2][:, (h % 2) * 129:(h % 2 + 1) * 129]
                nbt = lambda h: NB[h // 2]
            else:
                NB = [PS(), PS(), PS(), PS()]
                for pb in NB:
                    ps_resv.add(pb.idx)
                nbv = lambda h: NB[h][:, 0:129]
                nbt = lambda h: NB[h]
            for h in range(4):
                S.op("pe", lambda e: e.matmul(nbv(h), lhsT=AT_[:, h, :], rhs=vaug[:, s, h, :], start=True, stop=False),
                     reads=[AT_, vaug], writes=[nbt(h)])
                if kind == 0:
                    S.op("pe", lambda e: e.matmul(nbv(h), lhsT=qTm[:, h, s * 128:(s + 1) * 128], rhs=Cbf[l][:, h, :], start=False, stop=True),
                         reads=[qTm, Cbf[l]], writes=[nbt(h)])
            if kind == 1:
                for g2 in range(8):
                    C0t = [TMP(), TMP()]
                    C0v = [t_[:, 0:516].rearrange("p (h e) -> p h e", h=4) for t_ in C0t]
                    for bb in range(2):
                        b = g2 * 2 + bb
                        S.dma(C0v[bb][:, :, 0:128], sC[l, b, :, :, :].rearrange("h d e -> d h e"), writes=[C0t[bb]])
                        S.dma(C0v[bb][:, :, 128:129], sn[l, b, :, :].rearrange("h (d o) -> d h o", o=1), writes=[C0t[bb]],
                              allow_slow_non_contiguous=True)
                        bet = ap_like(AB[:, 2, b, 0:1], [[1, 4], [0, 129]])
                        S.op("dve", lambda e: e.tensor_tensor(out=C0bf[:, bb, :, :], in0=C0v[bb], in1=bet, op=ALU.mult),
                             reads=[C0t[bb], AB], writes=[C0bf])
                    S.op(PL_, lambda e: e.memset(qmask[:], 0.0), writes=[qmask])
                    for h in range(4):
                        dstq = ap_like(qmask[:, h, 0, g2 * 16:g2 * 16 + 1], [[128 + 8, 2], [1, 8]])
                        S.op(PL_, lambda e: e.tensor_copy(out=dstq, in_=qTm[:, h, g2 * 16:(g2 + 1) * 16].rearrange("p (b t) -> p b t", t=8)),
                             reads=[qTm], writes=[qmask])
                    for bb in range(2):
                        b = g2 * 2 + bb
                        for h in range(4):
                            S.op("pe", lambda e: e.matmul(nbv(h), lhsT=qmask[:, h, bb, :], rhs=C0bf[:, bb, h, :], start=False,
                                                          stop=(b == NSB - 1)), reads=[qmask, C0bf], writes=[nbt(h)])
                    for bb in range(2):
                        b = g2 * 2 + bb
                        bmk = ap_like(cst[:, C_BM + b:C_BM + b + 1], [[0, 129]])
                        C1tt = TMP()
                        C1t = C1tt[:, 0:516].rearrange("p (h e) -> p h e", h=4)
                        for hh in range(2):
                            pd = PS()
                            for h2 in range(2):
                                h = hh * 2 + h2
                                vm = vmask[:, h2, :]
                                S.op(PL_, lambda e: e.tensor_tensor(out=vm, in0=vtil[:, 0, h, :], in1=bmk, op=ALU.mult),
                                     reads=[vtil, cst], writes=[vmask])
                                S.op("pe", lambda e: e.matmul(pd[:, h2 * 129:(h2 + 1) * 129], lhsT=kTM[:, 0, h, :], rhs=vm, start=True, stop=True),
                                     reads=[kTM, vmask], writes=[pd])
                                S.op("dve", lambda e: e.tensor_scalar(out=C1t[:, h, :], in0=pd[:, h2 * 129:(h2 + 1) * 129],
                                                                      scalar1=AB[:, 0, b, h:h + 1], scalar2=None, op0=ALU.mult),
                                     reads=[pd, AB], writes=[C1tt])
                                S.op("dve", lambda e: e.scalar_tensor_tensor(out=C1t[:, h, :], in0=C0v[bb][:, h, :], scalar=AB[:, 1, b, h:h + 1],
                                                                             in1=C1t[:, h, :], op0=ALU.mult, op1=ALU.add),
                                     reads=[C0t[bb], AB, C1tt], writes=[C1tt])
                        S.dma(Cs_o[l, b, :, :, :].rearrange("h d e -> d h e"), C1t[:, :, 0:128], reads=[C1tt])
                        S.dma(ns_o[l, b, :, :].rearrange("h (d o) -> d h o", o=1), C1t[:, :, 128:129], reads=[C1tt], allow_slow_non_contiguous=True)
                for pb in NB:
                    ps_resv.discard(pb.idx)
            if kind == 0:
                DC = [PS(), PS()]
                for h in range(4):
                    dcv = DC[h // 2][:, (h % 2) * 129:(h % 2 + 1) * 129]
                    S.op("pe", lambda e: e.matmul(dcv, lhsT=kTM[:, s, h, :], rhs=vtil[:, s, h, :], start=True, stop=True),
                         reads=[kTM, vtil], writes=[DC[h // 2]])
                    S.op("dve", lambda e: e.tensor_tensor(out=C32[l][:, h, :], in0=dcv, in1=C32[l][:, h, :], op=ALU.add),
                         reads=[DC[h // 2], C32[l]], writes=[C32[l]])
                    S.op("act", lambda e: e.activation(out=C32[l][:, h, :], in_=C32[l][:, h, :], func=AF.Copy, scale=etotS[:, s, h:h + 1]),
                         reads=[C32[l], etotS], writes=[C32[l]])
                    S.op(PL_, lambda e: e.tensor_copy(out=Cbf[l][:, h, :], in_=C32[l][:, h, :]), reads=[C32[l]], writes=[Cbf[l]])
            return (NB, nbv, nbt)

        def core_part2(s, ctx):
            NB, nbv, nbt = ctx
            ycTM_ = ycTM if s % 2 == 0 else ycTMb
            sx = TMP()
            for h in range(4):
                S.op("dve", lambda e: e.tensor_copy(out=sx[:, 24 + h:25 + h], in_=nbv(h)[:, 128:129]), reads=[nbt(h)], writes=[sx])
                S.op("dve", lambda e: e.scalar_tensor_tensor(out=sx[:, h:h + 1], in0=sx[:, 24 + h:25 + h], scalar=-1.0, in1=sx[:, 24 + h:25 + h],
                                                             op0=ALU.mult, op1=ALU.max), reads=[sx], writes=[sx])
                S.op("dve", lambda e: e.tensor_tensor(out=sx[:, h:h + 1], in0=sx[:, h:h + 1], in1=ecnS[:, s, h:h + 1], op=ALU.max),
                     reads=[sx, ecnS], writes=[sx])
                S.op("act", lambda e: e.activation(out=junk[:, 0:128], in_=nbv(h)[:, 0:128], func=AF.Square, accum_out=sx[:, 8 + h:9 + h]),
                     reads=[nbt(h)], writes=[junk, sx])
            S.op("dve", lambda e: e.reciprocal(out=sx[:, 4:8], in_=sx[:, 0:4]), reads=[sx], writes=[sx])
            S.op("dve", lambda e: e.tensor_tensor(out=sx[:, 12:16], in0=sx[:, 8:12], in1=sx[:, 4:8], op=ALU.mult), reads=[sx], writes=[sx])
            S.op("dve", lambda e: e.tensor_tensor(out=sx[:, 12:16], in0=sx[:, 12:16], in1=sx[:, 4:8], op=ALU.mult), reads=[sx], writes=[sx])
            S.op("act", lambda e: e.activation(out=sx[:, 16:20], in_=sx[:, 12:16], func=AF.Ln, bias=epsb[:, 0:1], scale=1.0 / 128),
                 reads=[sx, epsb], writes=[sx])
            S.op("act", lambda e: e.activation(out=sx[:, 16:20], in_=sx[:, 16:20], func=AF.Exp, scale=-0.5), reads=[sx], writes=[sx])
            S.op("dve", lambda e: e.tensor_tensor(out=sx[:, 20:24], in0=sx[:, 16:20], in1=sx[:, 4:8], op=ALU.mult), reads=[sx], writes=[sx])
            for h in range(4):
                S.op("dve", lambda e: e.scalar_tensor_tensor(out=ycTM_[:, h, :], in0=nbv(h)[:, 0:128], scalar=sx[:, 20 + h:21 + h],
                                                             in1=ogg[:, s, h * 128:(h + 1) * 128], op0=ALU.mult, op1=ALU.mult),
                     reads=[nbt(h), sx, ogg], writes=[ycTM_])
            if kind == 0:
                for pb in NB:
                    ps_resv.discard(pb.idx)

        def core_tr(s):
            ycTM_ = ycTM if s % 2 == 0 else ycTMb
            pk = PS(); pv = pk[:].bitcast(BF16)
            for h in range(4):
                S.op("pe", lambda e: e.transpose(out=pv[:, h * 128:(h + 1) * 128], in_=ycTM_[:, h, :], identity=identb),
                     reads=[ycTM_, cbf], writes=[pk])
            S.op("act", lambda e: e.activation(out=ycT[:, :, s * 128:(s + 1) * 128], in_=pv[:, 0:512].rearrange("p (h t) -> p h t", h=4),
                                               func=AF.Copy), reads=[pk], writes=[ycT])

        def gen_Ccore():
            ctxs = {}
            core_st(0)
            if nsub > 1:
                core_st(1)
            for s in range(nsub):
                ctxs[s] = core_part1(s)
                if s + 2 < nsub:
                    core_st(s + 2)
                if s >= 1:
                    core_tr(s - 1)
                core_part2(s, ctxs[s])
                yield
            core_tr(nsub - 1)
            yield
            if kind == 0 and last:
                S.dma(scr3[l, :, 0:4], Rm[l][:, :], reads=[Rm[l]], writes=[scr_dep["scr3"]])
                S.dma(scr3[l, :, 4:8], PL[l][:, :], reads=[PL[l]], writes=[scr_dep["scr3"]])
                S.dma(sm4[0:4, 0:128], bass.AP(scr3.tensor, l * 1024, [[1, 4], [8, 128]]), reads=[scr_dep["scr3"]], writes=[sm4],
                      allow_slow_non_contiguous=True)
                S.dma(sm4[0:4, 128:129], bass.AP(scr3.tensor, l * 1024 + 4, [[1, 4], [1, 1]]), reads=[scr_dep["scr3"]], writes=[sm4],
                      allow_slow_non_contiguous=True)
                S.op("dve", lambda e: e.tensor_reduce(out=sm4[0:4, 130:131], in_=sm4[0:4, 0:128], axis=AX.X, op=ALU.max), reads=[sm4], writes=[sm4])
                S.op("dve", lambda e: e.tensor_scalar(out=sm4[0:4, 131:132], in0=sm4[0:4, 130:131], scalar1=0.0, scalar2=sm4[0:4, 128:129],
                                                      op0=ALU.max, op1=ALU.subtract), reads=[sm4], writes=[sm4])
                S.dma(mp_o[l:l + 1, :].rearrange("o h -> h o"), sm4[0:4, 131:132], reads=[sm4], allow_slow_non_contiguous=True)
                S.dma(scr4[l:l + 1, 0:4].rearrange("o h -> h o"), sm4[0:4, 131:132], reads=[sm4], writes=[scr_dep["scr4"]],
                      allow_slow_non_contiguous=True)
                emT = TMP()
                S.dma(emT[:, 0:4], bcast_row(scr4[l:l + 1, 0:4], 128, 4),
                      reads=[scr_dep["scr4"]], writes=[emT])
                S.op("act", lambda e: e.activation(out=emT[:, 4:8], in_=emT[:, 0:4], func=AF.Exp, scale=-1.0), reads=[emT], writes=[emT])
                for h in range(4):
                    S.op("dve", lambda e: e.tensor_scalar(out=C32[l][:, h, :], in0=C32[l][:, h, :], scalar1=emT[:, 4 + h:5 + h], scalar2=None,
                                                          op0=ALU.mult), reads=[C32[l], emT], writes=[C32[l]])
                S.dma(Cp_o[l, :, :, :].rearrange("h d e -> d h e"), C32[l][:, :, 0:128], reads=[C32[l]])
                S.dma(np_o[l, :, :].rearrange("h (d o) -> d h o", o=1), C32[l][:, :, 128:129], reads=[C32[l]], allow_slow_non_contiguous=True)

            if kind == 1 and l == 0:
                dbg("yaT", yaT[:, :, 0:128], [yaT]); dbg("ybT", ybT[:, :, 0:128], [ybT]); dbg("ycT", ycT[:, :, 0:128], [ycT])
            yield
        def run_seq(*gens):
            for g_ in gens:
                for _ in g_:
                    pass

        def run_il(g1, g2):
            a_, b_ = True, True
            while a_ or b_:
                if a_:
                    try:
                        next(g1)
                    except StopIteration:
                        a_ = False
                if b_:
                    try:
                        next(g2)
                    except StopIteration:
                        b_ = False

        run_seq(gen_Bproj())
        ckpt("P1 %d %d %d" % (kind, sti, l))
        if kind == 0 and _os.environ.get("KIL"):
            run_il(gen_att(), gen_Cproj())
        else:
            run_seq(gen_att(), gen_Cproj())
        ckpt("P2 %d %d %d" % (kind, sti, l))
        ckpt("P3a %d %d %d" % (kind, sti, l))
        if kind == 0 and _os.environ.get("KIL"):
            run_il(gen_Ccore(), gen_A())
        else:
            run_seq(gen_Ccore(), gen_A())
        ckpt("P3 %d %d %d" % (kind, sti, l))
        brT = [yaT, ybT, ycT]
        for j in range(8):
            W = WNEXT("mg%d" % j)
            Wg = W[:, 0:3072].rearrange("p (k n) -> p k n", k=8)
            acc = TMP()
            for br in range(3):
                pg = fm_chunk(W, Wg, slice(br * 128, (br + 1) * 128))
                sgt = TMP()
                S.op("act", lambda e: e.activation(out=sgt[:, 0:TT], in_=pg[:, 0:TT], func=AF.Sigmoid,
                                                   bias=p["bfm"][:, BF_MG + br * 8 + j:BF_MG + br * 8 + j + 1], scale=1.0),
                     reads=[pg, p["bfm"]], writes=[sgt])
                Wb = W[:, 3072 + br * 512:3072 + (br + 1) * 512].rearrange("p (k n) -> p k n", k=4)
                pp = PS()
                for kc in range(4):
                    S.op("pe", lambda e: e.matmul(pp[:, 0:TT], lhsT=Wb[:, kc, :], rhs=brT[br][:, kc, 0:TT], start=(kc == 0), stop=(kc == 3)),
                         reads=[W, brT[br]], writes=[pp])
                if br == 0:
                    S.op("dve", lambda e: e.tensor_tensor(out=acc[:, 0:TT], in0=pp[:, 0:TT], in1=sgt[:, 0:TT], op=ALU.mult),
                         reads=[pp, sgt], writes=[acc])
                else:
                    S.op("dve", lambda e: e.tensor_tensor(out=sgt[:, 0:TT], in0=pp[:, 0:TT], in1=sgt[:, 0:TT], op=ALU.mult),
                         reads=[pp, sgt], writes=[sgt])
                    dst = acc[:, 0:TT] if br == 1 else mT[:, j, 0:TT]
                    S.op(PL_, lambda e: e.tensor_tensor(out=dst, in0=acc[:, 0:TT], in1=sgt[:, 0:TT], op=ALU.add),
                         reads=[acc, sgt], writes=[acc if br == 1 else mT])
        if kind == 1 and l == 0:
            dbg("mT", mT[:, :, 0:128], [mT])
        Wo = [WNEXT("wo0"), WNEXT("wo1", hold=1)]
        Wov = [w_[:, 0:4096].rearrange("p (k n) -> p k n", k=8) for w_ in Wo]
        for s in range(nsub):
            for nt in range(2):
                ps = PS()
                for kc in range(8):
                    S.op("pe", lambda e: e.matmul(ps[:, :], lhsT=mT[:, kc, s * 128:(s + 1) * 128], rhs=Wov[nt][:, kc, :],
                                                  start=(kc == 0), stop=(kc == 7)), reads=[Wo[nt], mT], writes=[ps])
                t = TMP()
                S.op("dve", lambda e: e.tensor_tensor(out=t[:, 0:512], in0=ps[:, :], in1=p["gate"][:, nt * 512:(nt + 1) * 512],
                                                      op=ALU.mult), reads=[ps, p["gate"]], writes=[t])
                S.op(PL_, lambda e: e.tensor_tensor(out=X[:, s, nt * 512:(nt + 1) * 512], in0=X[:, s, nt * 512:(nt + 1) * 512],
                                                       in1=t[:, 0:512], op=ALU.add), reads=[t, X.d[s]], writes=[X.d[s]])

    def main_flow():
        prologue()
        prologue2()
        ckpt("prologue2")
        S.dma(X[:, 0, :], xs[:, :], writes=[X.d[0]])
        for l in range(NL):
            compute_mod(l)
            ckpt("mod %d" % l)
            process_tile(1, 0, l, l == NL - 1)
            if l == 0:
                dbg("X1", X[:, 0, :], [X.d[0]])
            ckpt("P4 sample %d" % l)
        S.dma(ys[:, :], X[:, 0, :], reads=[X.d[0]])
        for l in range(NL):
            S.op("pool", lambda e: e.tensor_copy(out=prm[l]["gate"][:, :], in_=X[:, 2 + l, :]), reads=[X.d[2 + l]], writes=[prm[l]["gate"]])
        for sti in range(NST):
            for s in range(4):
                S.dma(X[:, s, :], xp[sti * 512 + s * 128: sti * 512 + (s + 1) * 128, :], writes=[X.d[s]])
            for l in range(NL):
                process_tile(0, sti, l, l == NL - 1)
                ckpt("P4 prompt %d %d" % (sti, l))
            for s in range(4):
                S.dma(yp[sti * 512 + s * 128: sti * 512 + (s + 1) * 128, :], X[:, s, :], reads=[X.d[s]])

    try:
        main_flow()
    except _Stop:
        pass
    S.barrier()
    es.close()
    return nc


_PROG = {}


def _get_prog(NST):
    if NST not in _PROG:
        import os
        st = os.environ.get("KSTAGE")
        _PROG[NST] = build_program(NST, stage=(int(st) if st else None))
    return _PROG[NST]


def kernel(x_prompt, x_sample, cache_swa_k, cache_swa_v, state_mlstm_conv, state_mlstm_C, state_mlstm_n,
           state_mlstm_m, c_prompt, c_sample, ada_w, ada_b, norm_g, w_in, b_in, gmlp_vnorm_g, gmlp_ws, gmlp_bs,
           swa_qnorm_g, swa_knorm_g, swa_sinks, mlstm_conv_w, mlstm_conv_b, mlstm_f_bias, mlstm_hnorm_g,
           w_branch_a, w_branch_b, w_branch_c, w_out, _ncores=8):
    f = lambda a: np.ascontiguousarray(np.asarray(a, dtype=np.float32))
    x_prompt = f(x_prompt); x_sample = f(x_sample)
    B, SEQ, _ = x_prompt.shape
    DB = x_sample.shape[0]
    NST = SEQ // 512
    ncores = _ncores
    nsb = DB // ncores
    assert nsb == NSB
    nc = _get_prog(NST)
    shared = {
        "ada_w": f(ada_w), "ada_b": f(ada_b), "norm_g": f(norm_g), "w_in": f(w_in), "b_in": f(b_in),
        "gmlp_vnorm_g": f(gmlp_vnorm_g), "gmlp_ws": f(gmlp_ws), "gmlp_bs": f(gmlp_bs),
        "swa_qnorm_g": f(swa_qnorm_g), "swa_knorm_g": f(swa_knorm_g), "swa_sinks": f(swa_sinks),
        "mlstm_conv_w": f(mlstm_conv_w), "mlstm_conv_b": f(mlstm_conv_b), "mlstm_f_bias": f(mlstm_f_bias),
        "mlstm_hnorm_g": f(mlstm_hnorm_g), "w_branch_a": f(w_branch_a), "w_branch_b": f(w_branch_b),
        "w_branch_c": f(w_branch_c), "w_out": f(w_out), "consts": make_consts(),
    }
    ck = f(cache_swa_k).reshape(NL, DB, 128, 128); cv = f(cache_swa_v).reshape(NL, DB, 128, 128)
    sconv = f(state_mlstm_conv); sC = f(state_mlstm_C); sn = f(state_mlstm_n); sm = f(state_mlstm_m)
    cp = f(c_prompt); cs = f(c_sample)
    in_maps = []
    for c in range(ncores):
        b = c % B
        sl = slice(c * NSB, (c + 1) * NSB)
        m = dict(shared)
        m.update({
            "xp": x_prompt[b], "xs": x_sample[sl].reshape(128, D), "cpr": cp[b:b + 1], "csm": cs[sl],
            "ck": np.ascontiguousarray(ck[:, sl]), "cv": np.ascontiguousarray(cv[:, sl]),
            "sconv": np.ascontiguousarray(sconv[:, sl]), "sC": np.ascontiguousarray(sC[:, sl]),
            "sn": np.ascontiguousarray(sn[:, sl]), "sm": np.ascontiguousarray(sm[:, sl]),
        })
        in_maps.append(m)
    res = run_bass_kernel_spmd(nc, in_maps, core_ids=list(range(ncores)))
    R = res.results
    global DBG_OUT
    DBG_OUT = {k: v for k, v in R[0].items() if k.startswith("dbg_")}
    nb = min(B, ncores)
    y_p = np.stack([R[b]["yp"] for b in range(nb)])
    y_s = np.concatenate([R[c]["ys"].reshape(NSB, 8, D) for c in range(ncores)], axis=0)
    kp = np.stack([R[b]["kp"].reshape(NL, 128, 2, 64) for b in range(nb)], axis=1)
    vp = np.stack([R[b]["vp"].reshape(NL, 128, 2, 64) for b in range(nb)], axis=1)
    convp = np.stack([R[b]["convp"] for b in range(nb)], axis=1)
    Cp = np.stack([R[b]["Cp"] for b in range(nb)], axis=1)
    npp = np.stack([R[b]["np_"] for b in range(nb)], axis=1)
    mp = np.stack([R[b]["mp"] for b in range(nb)], axis=1)
    cat = lambda k, shp: np.concatenate([R[c][k].reshape(shp) for c in range(ncores)], axis=1)
    ks = cat("ks", (NL, NSB, 128, 2, 64)); vs = cat("vs", (NL, NSB, 128, 2, 64))
    convs = cat("convs", (NL, NSB, 3, D)); Cs = cat("Cs", (NL, NSB, 4, 128, 128))
    ns = cat("ns", (NL, NSB, 4, 128)); ms = cat("ms", (NL, NSB, 4))
    gv = cat("gv", (NL, NSB, 8, 512))
    return (y_p, y_s, kp, vp, convp, Cp, npp, mp, ks, vs, convs, Cs, ns, ms, gv)
```

```python
import math
from contextlib import ExitStack

import numpy as np
import concourse.bass as bass
import concourse.mybir as mybir
from concourse.bass_utils import run_bass_kernel_spmd

F32 = mybir.dt.float32
BF16 = mybir.dt.bfloat16
AF = mybir.ActivationFunctionType
ALU = mybir.AluOpType
AX = mybir.AxisListType

D = 1024
KC = 8
NL = 2
EPS = 1e-6
NSB = 16
IN_W = 8456
O_AU, O_AV, O_AG, O_BQ, O_BK, O_BV, O_BG = 0, 512, 1024, 1536, 2048, 2176, 2304
O_CQK, O_CV, O_CI, O_CF, O_CO, O_CG, O_MG = 2816, 3840, 4352, 4356, 4360, 4872, 5384
GSZ = 4608

C_ID, C_CUR, C_PREV, C_BD, C_ONE, C_BLK, C_CACHE, C_TI, C_BM, C_E4 = 0, 128, 256, 384, 512, 640, 768, 776, 904, 920
NCONST = 924


def make_consts():
    c = np.zeros((128, NCONST), np.float32)
    i = np.arange(128)
    c[:, C_ID:C_ID + 128] = np.eye(128)
    c[:, C_CUR:C_CUR + 128] = (i[:, None] <= i[None, :])
    c[:, C_PREV:C_PREV + 128] = (i[:, None] > i[None, :])
    c[:, C_BD:C_BD + 128] = ((i[:, None] // 8) == (i[None, :] // 8)) & ((i[:, None] % 8) <= (i[None, :] % 8))
    c[:, C_ONE:C_ONE + 128] = 1.0
    c[:, C_BLK:C_BLK + 128] = ((i[:, None] // 64) == (i[None, :] // 64))
    c[:, C_CACHE:C_CACHE + 8] = (i[:, None] > np.arange(8)[None, :])
    c[:8, C_TI:C_TI + 128] = (np.arange(8)[:, None] == (i[None, :] % 8))
    c[:, C_BM:C_BM + 16] = ((i[:, None] // 8) == np.arange(16)[None, :])
    c[:4, C_E4:C_E4 + 4] = np.eye(4)
    return c


class Dep:
    __slots__ = ("w", "r", "excl")

    def __init__(self):
        self.w = None
        self.r = []
        self.excl = False


class Tile:
    def __init__(self, t, nslots=1):
        self.t = t
        self.d = [Dep() for _ in range(nslots)]

    def __getitem__(self, k):
        return self.t[k]


CAST_INFLIGHT = 6


class Sched:
    NDMA = 56

    def __init__(self, nc, es):
        self.nc = nc
        self.eng = {"pe": nc.tensor, "act": nc.scalar, "dve": nc.vector, "pool": nc.gpsimd, "sp": nc.sync}
        self.sem = {}
        self.cnt = {}
        for k in self.eng:
            self.sem[k] = es.enter_context(nc.semaphore("s_" + k))
            self.cnt[k] = 0
        self.dval = [0] * self.NDMA
        self.dnext = 0
        for i in range(self.NDMA):
            self.sem["d%d" % i] = es.enter_context(nc.semaphore("d%d" % i))
        self.waited = {k: {} for k in self.eng}

    def _wait(self, e, deps):
        need = {}
        for d in deps:
            if d is None:
                continue
            k, v = d
            if k == "pe" and e == "pe":
                continue
            if self.waited[e].get(k, 0) >= v:
                continue
            if need.get(k, 0) < v:
                need[k] = v
        for k, v in need.items():
            self.eng[e].wait_ge(self.sem[k], v)
            self.waited[e][k] = v

    @staticmethod
    def _collect(reads, writes, e=None):
        deps = []
        for d in reads:
            deps.append(d.w)
            if d.excl:
                deps.extend(r for r in d.r if r[0] != e)
        for d in writes:
            deps.append(d.w)
            deps.extend(d.r)
        return deps

    @staticmethod
    def _flat(lst):
        out = []
        for x in lst:
            if isinstance(x, Tile):
                out.extend(x.d)
            elif isinstance(x, (list, tuple)):
                out.extend(Sched._flat(x))
            elif x is not None:
                out.append(x)
        return out

    def op(self, e, fn, reads=(), writes=()):
        reads = self._flat(reads)
        writes = self._flat(writes)
        self._wait(e, self._collect(reads, writes, e))
        ins = fn(self.eng[e])
        self.cnt[e] += 1
        ins.then_inc(self.sem[e], 1)
        me = (e, self.cnt[e])
        for d in reads:
            d.r.append(me)
        for d in writes:
            d.w = me
            d.r = []
        return me

    def dma(self, out, in_, reads=(), writes=(), q="sp", **kw):
        reads = self._flat(reads)
        writes = self._flat(writes)
        i = self.dnext
        self.dnext = (self.dnext + 1) % self.NDMA
        key = "d%d" % i
        deps = self._collect(reads, writes)
        if self.dval[i] > 0:
            deps.append((key, self.dval[i]))
        if q == "pool":
            hist = self.__dict__.setdefault("pool_hist", [])
            if len(hist) >= CAST_INFLIGHT:
                deps.append(hist[-CAST_INFLIGHT])
        self._wait(q, deps)
        self.dval[i] += 16
        self.ndma = getattr(self, "ndma", {})
        self.ndma[q] = self.ndma.get(q, 0) + 1
        self.eng[q].dma_start(out=out, in_=in_, **kw).then_inc(self.sem[key], 16)
        me = (key, self.dval[i])
        if q == "pool":
            self.pool_hist.append(me)
        for d in reads:
            d.r.append(me)
        for d in writes:
            d.w = me
            d.r = []
        return me

    def barrier(self):
        targets = [(k, self.cnt[k]) for k in self.eng if self.cnt[k] > 0]
        targets += [("d%d" % i, self.dval[i]) for i in range(self.NDMA) if self.dval[i] > 0]
        for e in self.eng:
            self._wait(e, [t for t in targets if t[0] != e])


class _Stop(Exception):
    pass


def build_program(NST, debug=False, stage=None):
    ck_state = {"n": 0}
    import os as _os
    DBGON = bool(_os.environ.get("KDBG"))
    dbg_names = []

    def dbg(name, ap, deps):
        if not DBGON:
            return
        t_ = nc.dram_tensor("dbg_" + name, list(ap.shape), ap.dtype, kind="ExternalOutput").ap()
        S.dma(t_, ap, reads=deps)
        dbg_names.append("dbg_" + name)

    def ckpt(tag=""):
        ck_state["n"] += 1
        if DBGON and "S" in ck_state:
            print("CKPT", ck_state["n"], tag, dict(ck_state["S"].cnt), max(ck_state["S"].dval), getattr(ck_state["S"], "ndma", None))
        if stage is not None and ck_state["n"] >= stage:
            print("STOP at checkpoint", ck_state["n"], tag)
            raise _Stop()

    NP = NST * 512
    nc = bass.Bass("TRN2", target_bir_lowering=False)
    es = ExitStack()
    S = Sched(nc, es)
    ck_state["S"] = S

    def ap_like(ap, dims):
        return bass.AP(ap.tensor, ap.offset, [list(ap.ap[0])] + [list(d) for d in dims])

    def pstride(ap_row, step, count, free):
        ps_ = ap_row.ap[0][0]
        return bass.AP(ap_row.tensor, ap_row.offset, [[step * ps_, count]] + [list(d) for d in free])

    def split_part(ap, outer, inner, free):
        ps_ = ap.ap[0][0]
        return bass.AP(ap.tensor, ap.offset, [[inner * ps_, outer], [ps_, inner]] + [list(d) for d in free])

    def bcast_row(ap_row, nparts, n):
        return bass.AP(ap_row.tensor, ap_row.offset, [[0, nparts], [1, n]])

    def dram(name, shape, dt=F32, kind="ExternalInput"):
        return nc.dram_tensor(name, list(shape), dt, kind=kind).ap()

    xp = dram("xp", [NP, D]); xs = dram("xs", [128, D])
    cpr = dram("cpr", [1, D]); csm = dram("csm", [NSB, D])
    ck_in = dram("ck", [NL, NSB, 128, 128]); cv_in = dram("cv", [NL, NSB, 128, 128])
    sconv = dram("sconv", [NL, NSB, 3, D]); sC = dram("sC", [NL, NSB, 4, 128, 128])
    sn = dram("sn", [NL, NSB, 4, 128]); sm = dram("sm", [NL, NSB, 4])
    ada_w = dram("ada_w", [NL, D, 3 * D]); ada_b = dram("ada_b", [NL, 3 * D]); norm_g = dram("norm_g", [NL, D])
    w_in = dram("w_in", [NL, D, IN_W]); b_in = dram("b_in", [NL, IN_W])
    vnorm_g = dram("gmlp_vnorm_g", [NL, 512]); gws = dram("gmlp_ws", [NL, 4, 128, 128]); gbs = dram("gmlp_bs", [NL, 4, 128])
    qn_g = dram("swa_qnorm_g", [NL, 64]); kn_g = dram("swa_knorm_g", [NL, 64]); sinks = dram("swa_sinks", [NL, 8])
    conv_w = dram("mlstm_conv_w", [NL, 4, D]); conv_b = dram("mlstm_conv_b", [NL, D])
    f_bias = dram("mlstm_f_bias", [NL, 4]); hn_g = dram("mlstm_hnorm_g", [NL, 512])
    w_ba = dram("w_branch_a", [NL, 512, D]); w_bb = dram("w_branch_b", [NL, 512, D]); w_bc = dram("w_branch_c", [NL, 512, D])
    w_out = dram("w_out", [NL, D, D])
    consts = dram("consts", [128, NCONST])

    EO = "ExternalOutput"
    yp = dram("yp", [NP, D], kind=EO); ys = dram("ys", [128, D], kind=EO)
    kp_o = dram("kp", [NL, 128, 128], kind=EO); vp_o = dram("vp", [NL, 128, 128], kind=EO)
    convp_o = dram("convp", [NL, 3, D], kind=EO); Cp_o = dram("Cp", [NL, 4, 128, 128], kind=EO)
    np_o = dram("np_", [NL, 4, 128], kind=EO); mp_o = dram("mp", [NL, 4], kind=EO)
    ks_o = dram("ks", [NL, NSB, 128, 128], kind=EO); vs_o = dram("vs", [NL, NSB, 128, 128], kind=EO)
    convs_o = dram("convs", [NL, NSB, 3, D], kind=EO); Cs_o = dram("Cs", [NL, NSB, 4, 128, 128], kind=EO)
    ns_o = dram("ns", [NL, NSB, 4, 128], kind=EO); ms_o = dram("ms", [NL, NSB, 4], kind=EO)
    gv_o = dram("gv", [NL, 128, 512], kind=EO)

    groups = {}
    gorder = []
    wsc_sz = [0]

    def add_group(name, size):
        groups[name] = (wsc_sz[0], size)
        wsc_sz[0] += size
        gorder.append(name)

    for nm in ["ada0", "ada1", "ada2", "ada3", "adag0", "adag1"]:
        add_group(nm, 4096)
    LGROUPS = ["bq", "bkv", "bg", "cq", "ck", "cv", "cif", "co", "cg", "av", "ag", "au"] + \
              ["mg%d" % j for j in range(8)] + ["wo0", "wo1"]
    for nm in LGROUPS:
        add_group(nm, GSZ if nm.startswith("mg") else (64 if nm == "cif" else (2048 if nm == "bkv" else 4096)))
    WTOT = wsc_sz[0]
    wsc = [dram("wsc%d" % l, [128, WTOT], BF16, kind="Internal") for l in range(NL)]
    wsc_dep = [{g: Dep() for g in gorder} for _ in range(NL)]
    scr1 = dram("scr1", [NL, 128, 8], kind="Internal")
    scr2 = dram("scr2", [NL, 3, 16, 4], kind="Internal")
    scr3 = dram("scr3", [NL, 128, 8], kind="Internal")
    scr4 = dram("scr4", [NL, 8], kind="Internal")
    scr_dep = {k: Dep() for k in ["scr1", "scr2", "scr3", "scr4"]}

    cast_only = [None]

    def cast_piece(l, gname, dst_off, kc, n, src):
        if cast_only[0] is not None and gname != cast_only[0]:
            return
        off = groups[gname][0] + dst_off
        dst = wsc[l][:, off:off + kc * n].rearrange("p (k n) -> p k n", k=kc)
        S.dma(dst, src.rearrange("(k p) n -> p k n", p=128), writes=[wsc_dep[l][gname]], q="pool")

    def perm_pieces(l, gname, base, src2d):
        if cast_only[0] is not None and gname != cast_only[0]:
            return
        for c in range(4):
            for half in range(2):
                h = c + 4 * half
                off = groups[gname][0] + c * 128 + half * 64
                dst = bass.AP(wsc[l].tensor, off, [[WTOT, 128], [512, 8], [1, 64]])
                S.dma(dst, src2d[:, base + h * 64: base + (h + 1) * 64].rearrange("(k p) n -> p k n", p=128),
                      writes=[wsc_dep[l][gname]], q="pool")

    def issue_casts(l):
        wi = w_in[l]
        for j in range(4):
            cast_piece(l, "ada%d" % j, 0, 8, 512, ada_w[l][:, j * 512:(j + 1) * 512])
        for j in range(2):
            cast_piece(l, "adag%d" % j, 0, 8, 512, ada_w[l][:, 2048 + j * 512: 2048 + (j + 1) * 512])
        cast_piece(l, "av", 0, 8, 512, wi[:, O_AV:O_AV + 512])
        cast_piece(l, "ag", 0, 8, 512, wi[:, O_AG:O_AG + 512])
        cast_piece(l, "au", 0, 8, 512, wi[:, O_AU:O_AU + 512])
        perm_pieces(l, "bq", O_BQ, wi)
        cast_piece(l, "bkv", 0, 8, 256, wi[:, O_BK:O_BK + 256])
        perm_pieces(l, "bg", O_BG, wi)
        cast_piece(l, "cq", 0, 8, 512, wi[:, O_CQK:O_CQK + 512])
        cast_piece(l, "ck", 0, 8, 512, wi[:, O_CQK + 512:O_CQK + 1024])
        cast_piece(l, "cv", 0, 8, 512, wi[:, O_CV:O_CV + 512])
        cast_piece(l, "cif", 0, 8, 8, wi[:, O_CI:O_CI + 8])
        cast_piece(l, "co", 0, 8, 512, wi[:, O_CO:O_CO + 512])
        cast_piece(l, "cg", 0, 8, 512, wi[:, O_CG:O_CG + 512])
        for j in range(8):
            g = "mg%d" % j
            if cast_only[0] is not None and g != cast_only[0]:
                continue
            for br in range(3):
                off = groups[g][0] + br * 128
                dst = bass.AP(wsc[l].tensor, off, [[WTOT, 128], [384, 8], [1, 128]])
                c0 = O_MG + br * 1024 + j * 128
                S.dma(dst, wi[:, c0:c0 + 128].rearrange("(k p) n -> p k n", p=128), writes=[wsc_dep[l][g]], q="pool")
            cast_piece(l, g, 3072, 4, 128, w_ba[l][:, j * 128:(j + 1) * 128])
            for c in range(4):
                for half in range(2):
                    h = c + 4 * half
                    off = groups[g][0] + 3584 + c * 128
                    dst = bass.AP(wsc[l].tensor, off + half * 64 * WTOT, [[WTOT, 64], [1, 128]])
                    S.dma(dst, w_bb[l][h * 64:(h + 1) * 64, j * 128:(j + 1) * 128], writes=[wsc_dep[l][g]], q="pool")
            cast_piece(l, g, 4096, 4, 128, w_bc[l][:, j * 128:(j + 1) * 128])
        for j in range(2):
            cast_piece(l, "wo%d" % j, 0, 8, 512, w_out[l][:, j * 512:(j + 1) * 512])

    def sb(name, shape, dt=F32, nslots=1):
        return Tile(es.enter_context(nc.sbuf_tensor(name, list(shape), dt)), nslots)

    NPS = 8
    psum = [Tile(es.enter_context(nc.psum_tensor("ps%d" % i, [128, 512], F32))) for i in range(NPS)]
    for p_ in psum:
        p_.d[0].excl = True
    ps_i = [0]

    ps_resv = set()

    def PS():
        while (ps_i[0] % NPS) in ps_resv:
            ps_i[0] += 1
        p = psum[ps_i[0] % NPS]
        p.idx = ps_i[0] % NPS
        ps_i[0] += 1
        return p

    NTMP = 6
    tmps = [sb("tmp%d" % i, [128, 516], F32) for i in range(NTMP)]
    tmp_i = [0]

    def TMP():
        t = tmps[tmp_i[0] % NTMP]
        tmp_i[0] += 1
        return t

    rr = {"ev": 0}

    cst = sb("cst", [128, NCONST], F32)
    cbf = sb("cbf", [128, 776], BF16)
    S.dma(cst[:, :], consts[:, :], writes=[cst])
    S.op("dve", lambda e: e.tensor_copy(out=cbf[:, :], in_=cst[:, 0:776]), reads=[cst], writes=[cbf])
    ident32 = cst[:, C_ID:C_ID + 128]
    identb = cbf[:, C_ID:C_ID + 128]
    ones32 = cst[:, C_ONE:C_ONE + 128]
    onesb = cbf[:, C_ONE:C_ONE + 128]
    blk32 = cst[:, C_BLK:C_BLK + 128]

    NRING = 3
    ring = [sb("wr%d" % i, [128, GSZ], BF16) for i in range(NRING)]
    wseq = []
    for l in range(NL):
        for nm in ["ada0", "ada1", "ada2", "ada3", "adag0", "adag1"]:
            wseq.append((l, nm))
        for nm in LGROUPS:
            wseq.append((l, nm))
    for _ in range(NST):
        for l in range(NL):
            for nm in LGROUPS:
                wseq.append((l, nm))
    wstate = {"issued": 0, "next": 0}
    TMB = {"av": (0, 512), "bkv": (512, 128), "cv": (640, 512), "cif": (1152, 8), "co": (1160, 512), "cg": (1672, 512)}

    cast_done = set()
    CAST_AHEAD = 6

    def ensure_cast(upto):
        for j in range(min(upto, len(wseq))):
            key_ = wseq[j]
            if key_ in cast_done:
                continue
            if not cast_done:
                S._wait("pool", [("d%d" % i_, S.dval[i_]) for i_ in range(S.NDMA) if S.dval[i_] > 0])
            cast_done.add(key_)
            cast_only[0] = key_[1]
            issue_casts(key_[0])
            cast_only[0] = None

    def w_issue_upto(n):
        while wstate["issued"] < min(n, len(wseq)):
            i = wstate["issued"]
            ensure_cast(i + 1 + CAST_AHEAD)
            l, nm = wseq[i]
            off, sz = groups[nm]
            buf = ring[i % NRING]
            if DBGON:
                print("ISSUE load", i, nm, "at ckpt", ck_state["n"], "next", wstate["next"])
            S.dma(buf[:, 0:sz], wsc[l][:, off:off + sz], reads=[wsc_dep[l][nm]], writes=[buf])
            if nm in TMB:
                bo, nb_ = TMB[nm]
                S.dma(buf[0:2, 4096:4096 + nb_], brs[l][:, bo:bo + nb_], reads=[brs_dep[l]], writes=[buf])
            wstate["issued"] += 1

    def WNEXT(expect, hold=0):
        i = wstate["next"]
        assert wseq[i][1] == expect, (wseq[i], expect)
        w_issue_upto(i + NRING - hold)
        wstate["next"] += 1
        return ring[i % NRING]

    prm = []
    for l in range(NL):
        p = {}
        p["bfm"] = sb("bfm%d" % l, [128, 49], F32)
        p["vgB"] = sb("vgB%d" % l, [128, 512], F32)
        p["hgB"] = sb("hgB%d" % l, [128, 512], F32)
        p["fbB"] = sb("fbB%d" % l, [128, 4], F32)
        p["cw"] = sb("cw%d" % l, [128, 8, 4], F32)
        p["cb"] = sb("cb%d" % l, [128, 8], F32)
        p["gq"] = sb("gq%d" % l, [128, 1], F32)
        p["gk"] = sb("gk%d" % l, [128, 1], F32)
        p["esink"] = sb("esink%d" % l, [128, 4], F32)
        p["WT"] = sb("WT%d" % l, [128, 2, 4, 128], BF16)
        p["ng"] = sb("ng%d" % l, [128, 8], F32)
        p["adab"] = sb("adab%d" % l, [128, 16], F32)
        p["Gp"] = sb("Gp%d" % l, [128, 8], F32)
        p["Sp"] = sb("Sp%d" % l, [128, 8], F32)
        p["gate"] = sb("gate%d" % l, [128, 1024], F32)
        prm.append(p)
    GsT = sb("GsT", [128, 8, 128], F32)
    SsT = sb("SsT", [128, 8, 128], F32)
    for p in prm:
        p["Gs"] = GsT
        p["Ss"] = SsT
    gb = sb("gb", [128, 1024], BF16)
    GB = {(0, 0): (0, 0), (0, 1): (32, 0), (1, 0): (64, 0), (1, 1): (0, 512)}
    brs = [dram("brs%d" % l, [2, 2184], BF16, kind="Internal") for l in range(NL)]
    brs_dep = [Dep() for _ in range(NL)]
    hb = sb("hb", [2, 512], BF16)

    BF_AU, BF_AG, BF_BQ, BF_BK, BF_BG, BF_CQ, BF_CK, BF_MG = 0, 4, 8, 12, 13, 17, 21, 25
    BT_AV, BT_BV, BT_CV, BT_CIF, BT_CO, BT_CG = 0, 512, 640, 1152, 1160, 1672

    def load_params(l):
        p = prm[l]
        bl = b_in[l]

        def fm_bias(ci, c0, n):
            S.dma(p["bfm"][:, ci:ci + n], bl[c0:c0 + n * 128].rearrange("(c p) -> p c", p=128), writes=[p["bfm"]],
                  allow_slow_non_contiguous=True)

        def fm_bias_perm(ci, base):
            for half in range(2):
                S.dma(p["bfm"][half * 64:(half + 1) * 64, ci:ci + 4],
                      bl[base + half * 256: base + (half + 1) * 256].rearrange("(c p) -> p c", p=64),
                      writes=[p["bfm"]], allow_slow_non_contiguous=True)

        fm_bias(BF_AU, O_AU, 4); fm_bias(BF_AG, O_AG, 4); fm_bias_perm(BF_BQ, O_BQ); fm_bias(BF_BK, O_BK, 1)
        fm_bias_perm(BF_BG, O_BG); fm_bias(BF_CQ, O_CQK, 4); fm_bias(BF_CK, O_CQK + 512, 4); fm_bias(BF_MG, O_MG, 24)
        for (bo, c0, n) in [(BT_AV, O_AV, 512), (BT_BV, O_BV, 128), (BT_CV, O_CV, 512), (BT_CIF, O_CI, 8),
                            (BT_CO, O_CO, 512), (BT_CG, O_CG, 512)]:
            tb = TMP(); tb2 = TMP()
            S.dma(tb[0:1, 0:n], bl[c0:c0 + n].rearrange("(o n) -> o n", o=1), writes=[tb])
            S.op("dve", lambda e: e.tensor_copy(out=hb[0:1, 0:n], in_=tb[0:1, 0:n]), reads=[tb], writes=[hb])
            S.dma(brs[l][0:1, bo:bo + n], hb[0:1, 0:n], reads=[hb], writes=[brs_dep[l]])
            S.op("dve", lambda e: e.tensor_tensor(out=tb2[0:1, 0:n], in0=tb[0:1, 0:n], in1=hb[0:1, 0:n],
                                                  op=ALU.subtract), reads=[tb, hb], writes=[tb2])
            S.op("dve", lambda e: e.tensor_copy(out=hb[0:1, 0:n], in_=tb2[0:1, 0:n]), reads=[tb2], writes=[hb])
            S.dma(brs[l][1:2, bo:bo + n], hb[0:1, 0:n], reads=[hb], writes=[brs_dep[l]])
        S.dma(p["vgB"][:, :], bcast_row(vnorm_g[l:l + 1, :], 128, 512), writes=[p["vgB"]])
        S.dma(p["hgB"][:, :], bcast_row(hn_g[l:l + 1, :], 128, 512), writes=[p["hgB"]])
        S.dma(p["fbB"][:, :], bcast_row(f_bias[l:l + 1, :], 128, 4), writes=[p["fbB"]])
        for j in range(4):
            S.dma(p["cw"][:, :, j], conv_w[l, j].rearrange("(c p) -> p c", p=128), writes=[p["cw"]], allow_slow_non_contiguous=True)
        S.dma(p["cb"][:, :], conv_b[l].rearrange("(c p) -> p c", p=128), writes=[p["cb"]], allow_slow_non_contiguous=True)
        for half in range(2):
            S.dma(p["gq"][half * 64:(half + 1) * 64, :], qn_g[l].rearrange("(p o) -> p o", o=1), writes=[p["gq"]],
                  allow_slow_non_contiguous=True)
            S.dma(p["gk"][half * 64:(half + 1) * 64, :], kn_g[l].rearrange("(p o) -> p o", o=1), writes=[p["gk"]],
                  allow_slow_non_contiguous=True)
            S.dma(p["esink"][half * 64:(half + 1) * 64, :], bcast_row(sinks[l:l + 1, half * 4:(half + 1) * 4], 64, 4),
                  writes=[p["esink"]])
        S.op("act", lambda e: e.activation(out=p["esink"][:, :], in_=p["esink"][:, :], func=AF.Exp),
             reads=[p["esink"]], writes=[p["esink"]])
        S.dma(p["ng"][:, :], norm_g[l].rearrange("(c p) -> p c", p=128), writes=[p["ng"]], allow_slow_non_contiguous=True)
        S.dma(p["adab"][:, :], ada_b[l, 0:2048].rearrange("(c p) -> p c", p=128), writes=[p["adab"]], allow_slow_non_contiguous=True)
        for kind_ in range(2):
            base, c0 = GB[(l, kind_)]
            tb = TMP(); tb2 = TMP()
            rows = slice(base, base + 2)
            for r in range(2):
                if kind_ == 0:
                    S.dma(tb[base + r:base + r + 1, 0:512], gbs[l:l + 1, :, :].rearrange("o g t -> o (g t)"), writes=[tb])
                else:
                    src = bass.AP(gbs.tensor, l * 512, [[0, 1], [128, 4], [0, 16], [1, 8]])
                    S.dma(tb[base + r:base + r + 1, 0:512].rearrange("o (g b t) -> o g b t", g=4, t=8), src, writes=[tb],
                          allow_slow_non_contiguous=True)
            S.op("dve", lambda e: e.tensor_copy(out=gb[rows, c0:c0 + 512], in_=tb[rows, 0:512]), reads=[tb], writes=[gb])
            S.op("dve", lambda e: e.tensor_tensor(out=tb2[rows, 0:512], in0=tb[rows, 0:512], in1=gb[rows, c0:c0 + 512], op=ALU.subtract),
                 reads=[tb, gb], writes=[tb2])
            S.op("dve", lambda e: e.tensor_copy(out=junk[rows, 0:512], in_=tb2[rows, 0:512]), reads=[tb2], writes=[junk])
            S.dma(gb[base + 1:base + 2, c0:c0 + 512], junk[base + 1:base + 2, 0:512], reads=[junk], writes=[gb])
        for g in range(4):
            wt = TMP()
            S.dma(wt[:, 0:128], gws[l, g, :, :], writes=[wt])
            ps = PS()
            S.op("pe", lambda e: e.matmul(ps[:, 0:128], lhsT=wt[:, 0:128], rhs=ident32, start=True, stop=True),
                 reads=[wt, cst], writes=[ps])
            S.op("dve", lambda e: e.tensor_tensor(out=p["WT"][:, 0, g, :], in0=ps[:, 0:128], in1=cst[:, C_CUR:C_CUR + 128],
                                                  op=ALU.mult), reads=[ps, cst], writes=[p["WT"]])
            lhs = ap_like(wt[0:8, 0:8], [[0, 16], [1, 8]])
            rep8 = TMP()
            S.op("dve", lambda e: e.tensor_copy(out=rep8[0:8, 0:128].rearrange("p (b t) -> p b t", t=8), in_=lhs), reads=[wt], writes=[rep8])
            ps2 = PS()
            S.op("pe", lambda e: e.matmul(ps2[:, 0:128], lhsT=rep8[0:8, 0:128], rhs=cst[0:8, C_TI:C_TI + 128], start=True, stop=True),
                 reads=[rep8, cst], writes=[ps2])
            S.op("dve", lambda e: e.tensor_tensor(out=p["WT"][:, 1, g, :], in0=ps2[:, 0:128], in1=cst[:, C_BD:C_BD + 128],
                                                  op=ALU.mult), reads=[ps2, cst], writes=[p["WT"]])


    X = sb("X", [128, 4, D], F32, nslots=4)
    hT = sb("hT", [128, 8, 512], BF16, nslots=8)
    xn32 = sb("xn32", [128, D], F32)
    xn32b = sb("xn32b", [128, D], F32)
    convTM = xn32
    junk = sb("junk", [128, D], BF16)
    st4 = sb("st4", [128, 16], F32, nslots=16)
    vn = sb("vn", [128, 4, 512], BF16)
    yaT = sb("yaT", [128, 4, 512], BF16)
    qT = sb("qT", [128, 4, 512], BF16)
    kTall = [sb("kTall%d" % l, [128, 128 + 512], BF16) for l in range(NL)]
    vall = [sb("vall%d" % l, [128, 5, 128], BF16) for l in range(NL)]
    k32 = sb("k32", [128, 128], F32)
    v32 = sb("v32", [128, 128], F32)
    sbg = sb("sbg", [128, 4, 512], BF16)
    ybT = sb("ybT", [128, 4, 512], BF16)
    PT = [sb("PT%d" % i, [128, 512], BF16) for i in range(4)]
    qTm = sb("qTm", [128, 4, 512], BF16)
    kTm = sb("kTm", [128, 4, 512], BF16)
    kTM = sb("kTM", [128, 4, 4, 128], BF16)
    vaug = sb("vaug", [128, 4, 4, 129], BF16)
    vtil = sb("vtil", [128, 4, 4, 129], BF16)
    ogg = sb("ogg", [128, 4, 512], BF16)
    ycT = sb("ycT", [128, 4, 512], BF16)
    ycTM = sb("ycTM", [128, 4, 128], BF16)
    AT = sb("AT", [128, 4, 128], BF16)
    stg = [sb("stg%d" % i, [128, 3 + 512], F32) for i in range(2)]
    bS = sb("bS", [128, 4, 4], F32)
    ecnS = sb("ecnS", [128, 4, 4], F32)
    etotS = sb("etotS", [128, 4, 4], F32)
    mT = sb("mT", [128, 8, 512], BF16)
    scT = sb("scT", [128, 8, 256], BF16)
    epsb = sb("epsb", [128, 4], F32)
    ccarry = [sb("ccarry%d" % l, [128, 8, 3], F32) for l in range(NL)]
    C32 = [sb("C32_%d" % l, [128, 4, 129], F32) for l in range(NL)]
    Cbf = [sb("Cbf%d" % l, [128, 4, 129], BF16) for l in range(NL)]
    PL = [sb("PL%d" % l, [128, 4], F32) for l in range(NL)]
    Rm = [sb("Rm%d" % l, [128, 4], F32) for l in range(NL)]
    sm4 = sb("sm4", [4, 256], F32)
    qmask = sb("qmask", [128, 4, 2, 128], BF16)
    vmask = sb("vmask", [128, 2, 129], BF16)
    C0bf = sb("C0bf", [128, 2, 4, 129], BF16)
    AB = sb("AB", [128, 3, 16, 4], F32)
    kcT = sb("kcT", [128, 16, 128], BF16)
    vcb = sb("vcb", [128, 16, 128], BF16)
    ATb = kcT
    ycTMb = vcb
    PTc = [PT[2], PT[3]]

    S.op("pool", lambda e: e.memset(vaug[:], 1.0), writes=[vaug])
    for l in range(NL):
        S.op("pool", lambda e: e.memset(C32[l][:], 0.0), writes=[C32[l]])
        S.op("pool", lambda e: e.memset(Cbf[l][:], 0.0), writes=[Cbf[l]])
        S.op("pool", lambda e: e.memset(ccarry[l][:], 0.0), writes=[ccarry[l]])
        S.op("pool", lambda e: e.memset(PL[l][:], 0.0), writes=[PL[l]])
        S.op("pool", lambda e: e.memset(Rm[l][:], -1e30), writes=[Rm[l]])

    def prologue_casts():
        S._wait("pool", [("d%d" % i, S.dval[i]) for i in range(S.NDMA) if S.dval[i] > 0])
        for l in range(NL):
            issue_casts(l)
        ckpt("casts")

    def prologue():
        for l in range(NL):
            load_params(l)
        if DBGON:
            o_, z_ = groups["av"]
            dbg("wsc_av", wsc[0][:, o_:o_ + z_], [wsc_dep[0]["av"]])
            dbg("brs0", brs[0][:, :], [brs_dep[0]])
        ckpt("params")

    def prologue2():
      if True:
        S.dma(X[:, 0, :], bcast_row(cpr[0:1, :], 128, D), writes=[X.d[0]])
        for t_ in range(8):
            S.dma(pstride(X[t_:t_ + 1, 1, :], 8, 16, [[1, D]]), csm[:, :], writes=[X.d[1]])
        scb = hT[:, 0:4, :].rearrange("p a b -> p (a b)")
        for k in range(2):
            S.op("act", lambda e: e.activation(out=scb[:, k * D:(k + 1) * D], in_=X[:, k, :], func=AF.Silu), reads=[X.d[k]], writes=[hT])
        for k in range(2):
            ps = PS()
            pv = ps[:].bitcast(BF16)
            for kc in range(8):
                S.op("pe", lambda e: e.transpose(out=pv[:, kc * 128:(kc + 1) * 128], in_=scb[:, k * D + kc * 128: k * D + (kc + 1) * 128],
                                                 identity=identb), reads=[hT, cbf], writes=[ps])
            S.op("dve", lambda e: e.tensor_copy(out=scT[:, :, k * 128:(k + 1) * 128],
                                                in_=pv.rearrange("p (k t) -> p k t", k=8)), reads=[ps], writes=[scT])
        dbg("scT", scT[:, :, :], [scT])
        dbg("Xc", X[:, 0:2, :], [X])
    bct = xn32

    def compute_mod(l):
        p = prm[l]
        modp = st4
        for j4 in range(4):
            W = WNEXT("ada%d" % j4)
            Wv = W[:, 0:4096].rearrange("p (k n) -> p k n", k=8)
            for jj in range(4):
                j = j4 * 4 + jj
                ps = PS()
                for kc in range(8):
                    S.op("pe", lambda e: e.matmul(ps[:, 0:256], lhsT=Wv[:, kc, jj * 128:(jj + 1) * 128], rhs=scT[:, kc, :],
                                                  start=(kc == 0), stop=(kc == 7)), reads=[W, scT], writes=[ps])
                S.op("act", lambda e: e.activation(out=modp[:, j:j + 1], in_=ps[:, 0:1], func=AF.Identity,
                                                   bias=p["adab"][:, j:j + 1], scale=1.0), reads=[ps, p["adab"]], writes=[modp])
                dst = (p["Ss"][:, j, :] if j < 8 else p["Gs"][:, j - 8, :])
                S.op("act", lambda e: e.activation(out=dst, in_=ps[:, 128:256], func=AF.Identity,
                                                   bias=p["adab"][:, j:j + 1], scale=1.0),
                     reads=[ps, p["adab"]], writes=[p["Ss"] if j < 8 else p["Gs"]])
        S.op("dve", lambda e: e.tensor_copy(out=p["Sp"][:, :], in_=modp[:, 0:8]), reads=[modp], writes=[p["Sp"]])
        S.op("dve", lambda e: e.scalar_tensor_tensor(out=p["Gp"][:, :], in0=modp[:, 8:16], scalar=1.0, in1=p["ng"][:, :],
                                                     op0=ALU.add, op1=ALU.mult), reads=[modp, p["ng"]], writes=[p["Gp"]])
        for kc in range(8):
            S.op("dve", lambda e: e.tensor_scalar(out=p["Gs"][:, kc, :], in0=p["Gs"][:, kc, :], scalar1=1.0,
                                                  scalar2=p["ng"][:, kc:kc + 1], op0=ALU.add, op1=ALU.mult),
                 reads=[p["Gs"], p["ng"]], writes=[p["Gs"]])
        S.dma(bct[:, :], bcast_row(ada_b[l:l + 1, 2048:3072], 128, 1024), writes=[bct])
        for j in range(2):
            W = WNEXT("adag%d" % j)
            Wv = W[:, 0:4096].rearrange("p (k n) -> p k n", k=8)
            for k in range(2):
                ps = PS()
                for kc in range(8):
                    S.op("pe", lambda e: e.matmul(ps[:, :], lhsT=scT[:, kc, k * 128:(k + 1) * 128], rhs=Wv[:, kc, :],
                                                  start=(kc == 0), stop=(kc == 7)), reads=[W, scT], writes=[ps])
                if k == 0:
                    S.op("dve", lambda e: e.tensor_tensor(out=X[:, 2 + l, j * 512:(j + 1) * 512], in0=ps[:, :],
                                                          in1=bct[:, j * 512:(j + 1) * 512], op=ALU.add),
                         reads=[ps, bct], writes=[X.d[2 + l]])
                else:
                    S.op("dve", lambda e: e.tensor_tensor(out=p["gate"][:, j * 512:(j + 1) * 512], in0=ps[:, :],
                                                          in1=bct[:, j * 512:(j + 1) * 512], op=ALU.add),
                         reads=[ps, bct], writes=[p["gate"]])

    LN_KS = -0.5 * math.log(128.0)

    def evac_engine():
        rr["ev"] += 1
        return "dve" if rr["ev"] % 2 else "act"

    def bias_mm(ps_ap, ps_t, W, n):
        S.op("pe", lambda e: e.matmul(ps_ap, lhsT=onesb[0:2, 0:128], rhs=W[0:2, 4096:4096 + n], start=False, stop=True),
             reads=[cbf, W], writes=[ps_t])

    def rstd_from_ssq(src_ap, src_t, dst_ap, dst_t, n, inv_n):
        S.op("act", lambda e: e.activation(out=dst_ap, in_=src_ap, func=AF.Ln, bias=epsb[:, 0:1], scale=inv_n),
             reads=[src_t, epsb], writes=[dst_t])
        S.op("act", lambda e: e.activation(out=dst_ap, in_=dst_ap, func=AF.Exp, scale=-0.5), reads=[dst_t], writes=[dst_t])

    S.op("pool", lambda e: e.memset(epsb[:, 0:1], EPS), writes=[epsb])
    S.op("pool", lambda e: e.memset(epsb[:, 1:2], 1.0), writes=[epsb])
    S.op("pool", lambda e: e.memset(epsb[:, 2:3], LN_KS), writes=[epsb])
    S.op("pool", lambda e: e.memset(epsb[:, 3:4], 0.0), writes=[epsb])

    def process_tile(kind, sti, l, last_layer):
        p = prm[l]
        PL_ = "pool" if kind == 0 else "dve"
        nsub = 4 if kind == 0 else 1
        TT = nsub * 128
        first = (kind == 0 and sti == 0)
        last = (kind == 1) or (sti == NST - 1)
        cur_mask_b = cbf[:, C_CUR:C_CUR + 128] if kind == 0 else cbf[:, C_BD:C_BD + 128]
        U32 = cst[:, C_CUR:C_CUR + 128] if kind == 0 else cst[:, C_BD:C_BD + 128]

        for s in range(nsub):
            xb = xn32 if s % 2 == 0 else xn32b
            c0_ = 2 * s
            S.op("act", lambda e: e.activation(out=junk[:, :], in_=X[:, s, :], func=AF.Square, accum_out=st4[:, c0_:c0_ + 1]),
                 reads=[X.d[s]], writes=[junk, st4.d[c0_]])
            rstd_from_ssq(st4[:, c0_:c0_ + 1], st4.d[c0_], st4[:, c0_ + 1:c0_ + 2], st4.d[c0_ + 1], 1, 1.0 / D)
            S.op("dve", lambda e: e.tensor_scalar(out=xb[:, :], in0=X[:, s, :], scalar1=st4[:, c0_ + 1:c0_ + 2], scalar2=None,
                                                  op0=ALU.mult), reads=[X.d[s], st4.d[c0_ + 1]], writes=[xb])
            pa, pb = PS(), PS()
            for kc in range(8):
                pp = pa if kc < 4 else pb
                S.op("pe", lambda e: e.transpose(out=pp[:, (kc % 4) * 128:(kc % 4 + 1) * 128], in_=xb[:, kc * 128:(kc + 1) * 128],
                                                 identity=ident32), reads=[xb, cst], writes=[pp])
            for kc in range(8):
                pp = pa if kc < 4 else pb
                src = pp[:, (kc % 4) * 128:(kc % 4 + 1) * 128]
                dst = hT[:, kc, s * 128:(s + 1) * 128]
                if kind == 0:
                    if kc < 4:
                        S.op("dve", lambda e: e.tensor_scalar(out=dst, in0=src, scalar1=p["Gp"][:, kc:kc + 1],
                                                              scalar2=p["Sp"][:, kc:kc + 1], op0=ALU.mult, op1=ALU.add),
                             reads=[pp, p["Gp"], p["Sp"]], writes=[hT.d[kc]])
                    else:
                        S.op("act", lambda e: e.activation(out=dst, in_=src, func=AF.Identity, bias=p["Sp"][:, kc:kc + 1],
                                                           scale=p["Gp"][:, kc:kc + 1]),
                             reads=[pp, p["Gp"], p["Sp"]], writes=[hT.d[kc]])
                else:
                    t = TMP()
                    S.op("dve", lambda e: e.tensor_tensor(out=t[:, 0:128], in0=src, in1=p["Gs"][:, kc, :], op=ALU.mult),
                         reads=[pp, p["Gs"]], writes=[t])
                    S.op(PL_, lambda e: e.tensor_tensor(out=dst, in0=t[:, 0:128], in1=p["Ss"][:, kc, :], op=ALU.add),
                         reads=[t, p["Ss"]], writes=[hT.d[kc]])

        def fm_chunk(W, Wv, cols, kcn=8, rhsT=None):
            ps = PS()
            for kc in range(kcn):
                S.op("pe", lambda e: e.matmul(ps[:, 0:TT], lhsT=Wv[:, kc, cols], rhs=hT[:, kc, 0:TT],
                                              start=(kc == 0), stop=(kc == kcn - 1)), reads=[W, hT], writes=[ps])
            return ps

        def tm_tile(W, Wv, s, cols, n, bo):
            ps = PS()
            for kc in range(8):
                S.op("pe", lambda e: e.matmul(ps[:, 0:n], lhsT=hT[:, kc, s * 128:(s + 1) * 128], rhs=Wv[:, kc, cols],
                                              start=(kc == 0), stop=False), reads=[W, hT], writes=[ps])
            bias_mm(ps[:, 0:n], ps, W, n)
            return ps

        if kind == 1 and l == 0:
            dbg("hT_s0", hT[:, :, 0:128], [hT])
            dbg("Gs0", p["Gs"][:, :, :], [p["Gs"]])
            dbg("Ss0", p["Ss"][:, :, :], [p["Ss"]])
            dbg("xn32", xn32[:, :], [xn32])
        ckpt("P0 %d %d %d" % (kind, sti, l))
        kT = kTall[l]
        def qk_norm(ps, bias_ap, g_ap, g_t, out_bf, out_t, out32=None):
            q32 = TMP(); sq = TMP()
            S.op("act", lambda e: e.activation(out=q32[:, 0:TT], in_=ps[:, 0:TT], func=AF.Identity, bias=bias_ap, scale=1.0),
                 reads=[ps, p["bfm"]], writes=[q32])
            S.op(PL_, lambda e: e.tensor_tensor(out=sq[:, 0:TT], in0=q32[:, 0:TT], in1=q32[:, 0:TT], op=ALU.mult),
                 reads=[q32], writes=[sq])
            pq = PS()
            S.op("pe", lambda e: e.matmul(pq[:, 0:TT], lhsT=blk32, rhs=sq[:, 0:TT], start=True, stop=True),
                 reads=[cst, sq], writes=[pq])
            r = TMP()
            S.op("act", lambda e: e.activation(out=r[:, 0:TT], in_=pq[:, 0:TT], func=AF.Ln, bias=epsb[:, 0:1], scale=1.0 / 64),
                 reads=[pq, epsb], writes=[r])
            S.op("act", lambda e: e.activation(out=r[:, 0:TT], in_=r[:, 0:TT], func=AF.Exp, scale=-0.5), reads=[r], writes=[r])
            S.op("dve", lambda e: e.scalar_tensor_tensor(out=out_bf, in0=q32[:, 0:TT], scalar=g_ap, in1=r[:, 0:TT],
                                                         op0=ALU.mult, op1=ALU.mult), reads=[q32, g_t, r], writes=[out_t])
            if out32 is not None:
                S.op("dve", lambda e: e.scalar_tensor_tensor(out=out32[0], in0=q32[:, TT - 128:TT], scalar=g_ap, in1=r[:, TT - 128:TT],
                                                             op0=ALU.mult, op1=ALU.mult), reads=[q32, g_t, r], writes=[out32[1]])

        def gen_A():
            W = WNEXT("av"); Wv = W[:, 0:4096].rearrange("p (k n) -> p k n", k=8)
            if kind == 1 and l == 0:
                dbg("wav", W[:, :], [W])
            for s in range(nsub):
                ps = tm_tile(W, Wv, s, slice(0, 512), 512, BT_AV)
                c1_ = 8 + 2 * s
                S.op("act", lambda e: e.activation(out=junk[:, 0:512], in_=ps[:, :], func=AF.Square, accum_out=st4[:, c1_:c1_ + 1]),
                     reads=[ps], writes=[junk, st4.d[c1_]])
                rstd_from_ssq(st4[:, c1_:c1_ + 1], st4.d[c1_], st4[:, c1_ + 1:c1_ + 2], st4.d[c1_ + 1], 1, 1.0 / 512)
                if kind == 1:
                    t = TMP()
                    S.op("dve", lambda e: e.scalar_tensor_tensor(out=t[:, 0:512], in0=ps[:, :], scalar=st4[:, c1_ + 1:c1_ + 2], in1=p["vgB"][:, :],
                                                                 op0=ALU.mult, op1=ALU.mult), reads=[ps, st4.d[c1_ + 1], p["vgB"]], writes=[t])
                    S.dma(gv_o[l, :, :], t[:, 0:512], reads=[t])
                    S.op(PL_, lambda e: e.tensor_copy(out=vn[:, s, :], in_=t[:, 0:512]), reads=[t], writes=[vn])
                else:
                    S.op("dve", lambda e: e.scalar_tensor_tensor(out=vn[:, s, :], in0=ps[:, :], scalar=st4[:, c1_ + 1:c1_ + 2], in1=p["vgB"][:, :],
                                                                 op0=ALU.mult, op1=ALU.mult), reads=[ps, st4.d[c1_ + 1], p["vgB"]], writes=[vn])
            Wg_ = WNEXT("ag"); Wgv = Wg_[:, 0:4096].rearrange("p (k n) -> p k n", k=8)
            Wu_ = WNEXT("au", hold=1); Wuv = Wu_[:, 0:4096].rearrange("p (k n) -> p k n", k=8)
            gbase, gcol = GB[(l, kind)]
            for c in range(4):
                tA = TMP()
                ps = fm_chunk(Wg_, Wgv, slice(c * 128, (c + 1) * 128))
                S.op("act", lambda e: e.activation(out=tA[:, 0:TT], in_=ps[:, 0:TT], func=AF.Silu,
                                                   bias=p["bfm"][:, BF_AG + c:BF_AG + c + 1], scale=1.0),
                     reads=[ps, p["bfm"]], writes=[tA])
                ps = fm_chunk(Wu_, Wuv, slice(c * 128, (c + 1) * 128))
                S.op("dve", lambda e: e.scalar_tensor_tensor(out=tA[:, 0:TT], in0=ps[:, 0:TT], scalar=p["bfm"][:, BF_AU + c:BF_AU + c + 1],
                                                             in1=tA[:, 0:TT], op0=ALU.add, op1=ALU.mult),
                     reads=[ps, p["bfm"], tA], writes=[tA])
                if kind == 1 and l == 0 and c == 0:
                    dbg("tA0", tA[:, 0:128], [tA]); dbg("WT0", p["WT"][:, :, :, :], [p["WT"]]); dbg("gb", gb[:, :], [gb])
                ps = PS()
                for s in range(nsub):
                    o = ps[:, s * 128:(s + 1) * 128]
                    S.op("pe", lambda e: e.matmul(o, lhsT=vn[:, s, c * 128:(c + 1) * 128], rhs=p["WT"][:, kind, c, :],
                                                  start=True, stop=False), reads=[vn, p["WT"]], writes=[ps])
                    S.op("pe", lambda e: e.matmul(o, lhsT=onesb[gbase:gbase + 2, :], rhs=gb[gbase:gbase + 2, gcol + c * 128: gcol + (c + 1) * 128],
                                                  start=False, stop=True), reads=[cbf, gb], writes=[ps])
                S.op("dve", lambda e: e.tensor_tensor(out=yaT[:, c, 0:TT], in0=ps[:, 0:TT], in1=tA[:, 0:TT], op=ALU.mult),
                     reads=[ps, tA], writes=[yaT])
                yield

            yield
        def gen_Bproj():
            W = WNEXT("bq"); Wv = W[:, 0:4096].rearrange("p (k n) -> p k n", k=8)
            for c in range(4):
                ps = fm_chunk(W, Wv, slice(c * 128, (c + 1) * 128))
                qk_norm(ps, p["bfm"][:, BF_BQ + c:BF_BQ + c + 1], p["gq"][:, 0:1], p["gq"], qT[:, c, 0:TT], qT)
            W = WNEXT("bkv"); Wv = W[:, 0:2048].rearrange("p (k n) -> p k n", k=8)
            ps = fm_chunk(W, Wv, slice(0, 128))
            qk_norm(ps, p["bfm"][:, BF_BK:BF_BK + 1], p["gk"][:, 0:1], p["gk"], kT[:, 128:128 + TT], kT,
                    out32=((k32[:, :], k32) if last else None))
            for s in range(nsub):
                ps = tm_tile(W, Wv, s, slice(128, 256), 128, BT_BV)
                S.op("act", lambda e: e.activation(out=vall[l][:, 1 + s, :], in_=ps[:, 0:128], func=AF.Copy), reads=[ps], writes=[vall[l]])
                if last and s == nsub - 1:
                    S.op("dve", lambda e: e.tensor_copy(out=v32[:, :], in_=ps[:, 0:128]), reads=[ps], writes=[v32])
            if last:
                pk = PS()
                S.op("pe", lambda e: e.transpose(out=pk[:, 0:128], in_=k32[:, :], identity=ident32), reads=[k32, cst], writes=[pk])
                t = TMP()
                S.op("dve", lambda e: e.tensor_copy(out=t[:, 0:128], in_=pk[:, 0:128]), reads=[pk], writes=[t])
                if kind == 0:
                    if not _os.environ.get("KSKIP1"):
                        S.dma(kp_o[l, :, :], t[:, 0:128], reads=[t])
                    if not _os.environ.get("KSKIP2"):
                        tv = TMP()
                        S.op("dve", lambda e: e.tensor_copy(out=tv[:, 0:128], in_=v32[:, :]), reads=[v32], writes=[tv])
                        S.dma(vp_o[l, :, :], tv[:, 0:128], reads=[tv])
                else:
                    for (src_t, dst_o, cin) in [(t, ks_o, ck_in), (v32, vs_o, cv_in)]:
                        S.dma(dst_o[l, :, 0:120, :], cin[l, :, 8:128, :])
                        for t_ in range(8):
                            S.dma(dst_o[l, :, 120 + t_, :], pstride(src_t[t_:t_ + 1, 0:128], 8, 16, [[1, 128]]), reads=[src_t])
            W = WNEXT("bg"); Wv = W[:, 0:4096].rearrange("p (k n) -> p k n", k=8)
            for c in range(4):
                ps = fm_chunk(W, Wv, slice(c * 128, (c + 1) * 128))
                S.op("act", lambda e: e.activation(out=sbg[:, c, 0:TT], in_=ps[:, 0:TT], func=AF.Silu,
                                                   bias=p["bfm"][:, BF_BG + c:BF_BG + c + 1], scale=1.0), reads=[ps, p["bfm"]], writes=[sbg])
            if kind == 1:
                for bg in range(4):
                    kc32 = TMP()
                    kc32v = kc32[:, 0:512].rearrange("p (b f) -> p b f", b=4)
                    kcbv = junk[:, 0:512].rearrange("p (b f) -> p b f", b=4)
                    S.dma(kc32v, ck_in[l, bg * 4:(bg + 1) * 4, :, :].rearrange("b j f -> j b f"), writes=[kc32])
                    S.op("dve", lambda e: e.tensor_copy(out=kcbv, in_=kc32v), reads=[kc32], writes=[junk])
                    pk = PS(); pv = pk[:].bitcast(BF16)
                    for bb in range(4):
                        S.op("pe", lambda e: e.transpose(out=pv[:, bb * 128:(bb + 1) * 128], in_=kcbv[:, bb, :], identity=identb),
                             reads=[junk, cbf], writes=[pk])
                    S.op("act", lambda e: e.activation(out=kcT[:, bg * 4:(bg + 1) * 4, :], in_=pv[:, 0:512].rearrange("p (b j) -> p b j", b=4),
                                                       func=AF.Copy), reads=[pk], writes=[kcT])
                    vc32 = TMP()
                    vc32v = vc32[:, 0:512].rearrange("p (b f) -> p b f", b=4)
                    S.dma(vc32v, cv_in[l, bg * 4:(bg + 1) * 4, :, :].rearrange("b j f -> j b f"), writes=[vc32])
                    S.op("dve", lambda e: e.tensor_copy(out=vcb[:, bg * 4:(bg + 1) * 4, :], in_=vc32v), reads=[vc32], writes=[vcb])
            yield
        def gen_att():
            for s in range(nsub):
                O = psum[(2 * s) % 4]; Dn = psum[(2 * s + 1) % 4]
                O.idx = (2 * s) % 4; Dn.idx = (2 * s + 1) % 4
                ps_resv.add(O.idx); ps_resv.add(Dn.idx)
                lt_i = [0]

                def LTB():
                    b_ = psum[4 + lt_i[0] % 4]
                    lt_i[0] += 1
                    return b_
                blks = []
                if kind == 0 and not (first and s == 0):
                    blks.append(("prev", s))
                blks.append(("cur", s + 1))
                nb = len(blks)
                pts = {}
                for kv in range(2):
                    rows = slice(kv * 64, (kv + 1) * 64)
                    for bi, (bn, blk) in enumerate(blks):
                        LT = LTB()
                        S.op("pe", lambda e: e.matmul(LT[:, :], lhsT=kT[rows, blk * 128:(blk + 1) * 128],
                                                      rhs=qT[rows, :, s * 128:(s + 1) * 128], start=True, stop=True),
                             reads=[kT, qT], writes=[LT])
                        pt = PT[kv * 2 + bi] if kind == 0 else PT[kv]
                        S.op("act", lambda e: e.activation(out=pt[:, :], in_=LT[:, :], func=AF.Exp, scale=0.125), reads=[LT], writes=[pt])
                        mk = cur_mask_b if bn == "cur" else cbf[:, C_PREV:C_PREV + 128]
                        mk3 = ap_like(mk, [[0, 4], [1, 128]])
                        S.op("dve", lambda e: e.tensor_tensor(out=pt[:, :].rearrange("p (c t) -> p c t", c=4),
                                                               in0=pt[:, :].rearrange("p (c t) -> p c t", c=4), in1=mk3, op=ALU.mult),
                             reads=[pt, cbf], writes=[pt])
                        pts[(kv, bi)] = pt
                        yield
                for kv in range(2):
                    rows = slice(kv * 64, (kv + 1) * 64)
                    ncache = NSB if kind == 1 else 0
                    for bi, (bn, blk) in enumerate(blks):
                        pt = pts[(kv, bi)]
                        S.op("pe", lambda e: e.matmul(O[rows, :], lhsT=vall[l][:, blk, kv * 64:(kv + 1) * 64], rhs=pt[:, :],
                                                      start=(bi == 0), stop=(bi == nb - 1 and ncache == 0)), reads=[vall[l], pt], writes=[O])
                    for bi, (bn, blk) in enumerate(blks):
                        pt = pts[(kv, bi)]
                        S.op("pe", lambda e: e.matmul(Dn[rows, :], lhsT=onesb[:, 0:64], rhs=pt[:, :],
                                                      start=(bi == 0), stop=(bi == nb - 1 and ncache == 0)), reads=[cbf, pt], writes=[Dn])
                    if kind == 1:
                        LC = LTB()
                        for b in range(NSB):
                            S.op("pe", lambda e: e.matmul(LC[:, b * 32:(b + 1) * 32], lhsT=kcT[rows, b, :],
                                                          rhs=qT[rows, :, b * 8:(b + 1) * 8], start=True, stop=True),
                                 reads=[kcT, qT], writes=[LC])
                        ptc = PTc[kv]
                        S.op("act", lambda e: e.activation(out=ptc[:, :], in_=LC[:, :], func=AF.Exp, scale=0.125), reads=[LC], writes=[ptc])
                        mk = cbf[:, C_CACHE:C_CACHE + 8]
                        mk3 = ap_like(mk, [[0, 64], [1, 8]])
                        S.op(PL_, lambda e: e.tensor_tensor(out=ptc[:, :].rearrange("p (x t) -> p x t", t=8),
                                                               in0=ptc[:, :].rearrange("p (x t) -> p x t", t=8), in1=mk3, op=ALU.mult),
                             reads=[ptc, cbf], writes=[ptc])
                        for b in range(NSB):
                            oo = O[rows, :].rearrange("p (c t) -> p c t", c=4)[:, :, b * 8:(b + 1) * 8]
                            dd = Dn[rows, :].rearrange("p (c t) -> p c t", c=4)[:, :, b * 8:(b + 1) * 8]
                            rh = ptc[:, b * 32:(b + 1) * 32].rearrange("p (c t) -> p c t", c=4)
                            S.op("pe", lambda e: e.matmul(oo, lhsT=vcb[:, b, kv * 64:(kv + 1) * 64], rhs=rh, start=False,
                                                          stop=(b == NSB - 1)), reads=[vcb, ptc], writes=[O])
                            S.op("pe", lambda e: e.matmul(dd, lhsT=onesb[:, 0:64], rhs=rh, start=False, stop=(b == NSB - 1)),
                                 reads=[cbf, ptc], writes=[Dn])
                yield
                dsb = TMP()
                es3 = ap_like(p["esink"][:, 0:4], [[1, 4], [0, 128]])
                S.op("dve", lambda e: e.tensor_tensor(out=dsb[:, 0:512].rearrange("p (c t) -> p c t", c=4),
                                                      in0=Dn[:, :].rearrange("p (c t) -> p c t", c=4), in1=es3, op=ALU.add),
                     reads=[Dn, p["esink"]], writes=[dsb])
                S.op("act", lambda e: e.activation(out=dsb[:, 0:512], in_=dsb[:, 0:512], func=AF.Ln), reads=[dsb], writes=[dsb])
                S.op("act", lambda e: e.activation(out=dsb[:, 0:512], in_=dsb[:, 0:512], func=AF.Exp, scale=-1.0), reads=[dsb], writes=[dsb])
                S.op(PL_, lambda e: e.tensor_tensor(out=dsb[:, 0:512].rearrange("p (c t) -> p c t", c=4),
                                                       in0=dsb[:, 0:512].rearrange("p (c t) -> p c t", c=4),
                                                       in1=sbg[:, :, s * 128:(s + 1) * 128], op=ALU.mult), reads=[dsb, sbg], writes=[dsb])
                S.op("dve", lambda e: e.tensor_tensor(out=ybT[:, :, s * 128:(s + 1) * 128], in0=O[:, :].rearrange("p (c t) -> p c t", c=4),
                                                      in1=dsb[:, 0:512].rearrange("p (c t) -> p c t", c=4), op=ALU.mult),
                     reads=[O, dsb], writes=[ybT])
                ps_resv.discard(O.idx); ps_resv.discard(Dn.idx)
                yield
            if kind == 0:
                S.op(PL_, lambda e: e.tensor_copy(out=kT[:, 0:128], in_=kT[:, 512:640]), reads=[kT], writes=[kT])
                S.op(PL_, lambda e: e.tensor_copy(out=vall[l][:, 0, :], in_=vall[l][:, 4, :]), reads=[vall[l]], writes=[vall[l]])

            yield
        def gen_Cproj():
            nb_ = 1 if kind == 0 else NSB
            Lc = TT // nb_
            for half, (gname, dstT, bf0) in enumerate([("cq", qTm, BF_CQ), ("ck", kTm, BF_CK)]):
                W = WNEXT(gname); Wv = W[:, 0:4096].rearrange("p (k n) -> p k n", k=8)
                for c in range(4):
                    ch = half * 4 + c
                    ps = fm_chunk(W, Wv, slice(c * 128, (c + 1) * 128))
                    sg = stg[ch % len(stg)]
                    sgv = sg[:, 0:nb_ * (3 + Lc)].rearrange("p (b x) -> p b x", b=nb_)
                    if kind == 0:
                        S.op(PL_, lambda e: e.tensor_copy(out=sgv[:, :, 0:3], in_=ccarry[l][:, ch:ch + 1, :]),
                             reads=[ccarry[l]], writes=[sg])
                    else:
                        t = TMP()
                        S.dma(t[0:48, 0:128], sconv[l, :, :, ch * 128:(ch + 1) * 128].rearrange("b j c -> (b j) c"), writes=[t])
                        pc = PS()
                        S.op("pe", lambda e: e.transpose(out=pc[:, 0:48], in_=t[0:48, 0:128], identity=cst[0:48, C_ID:C_ID + 48]),
                             reads=[t, cst], writes=[pc])
                        S.op("dve", lambda e: e.tensor_copy(out=sgv[:, :, 0:3], in_=pc[:, 0:48].rearrange("p (b j) -> p b j", j=3)),
                             reads=[pc], writes=[sg])
                    S.op("act", lambda e: e.activation(out=sgv[:, :, 3:3 + Lc], in_=ps[:, 0:TT].rearrange("p (b x) -> p b x", b=nb_),
                                                       func=AF.Identity, bias=p["bfm"][:, bf0 + c:bf0 + c + 1], scale=1.0),
                         reads=[ps, p["bfm"]], writes=[sg])
                    if kind == 0:
                        S.op(PL_, lambda e: e.tensor_copy(out=ccarry[l][:, ch:ch + 1, :], in_=sgv[:, :, Lc:Lc + 3]),
                             reads=[sg], writes=[ccarry[l]])
                    if last:
                        if kind == 0:
                            src = sg[:, 3 + TT - 128:3 + TT]
                        else:
                            src = None
                        pc = PS()
                        if kind == 0:
                            S.op("pe", lambda e: e.transpose(out=pc[:, 0:128], in_=src, identity=ident32), reads=[sg, cst], writes=[pc])
                        else:
                            t2 = TMP()
                            S.op(PL_, lambda e: e.tensor_copy(out=t2[:, 0:128].rearrange("p (b x) -> p b x", b=NSB), in_=sgv[:, :, 3:11]),
                                 reads=[sg], writes=[t2])
                            S.op("pe", lambda e: e.transpose(out=pc[:, 0:128], in_=t2[:, 0:128], identity=ident32), reads=[t2, cst], writes=[pc])
                        S.op("dve", lambda e: e.tensor_copy(out=convTM[:, ch * 128:(ch + 1) * 128], in_=pc[:, 0:128]),
                             reads=[pc], writes=[convTM])
                    acc = TMP()
                    av = acc[:, 0:TT].rearrange("p (b x) -> p b x", b=nb_)
                    S.op("dve", lambda e: e.tensor_scalar(out=av, in0=sgv[:, :, 0:Lc], scalar1=p["cw"][:, ch, 0:1], scalar2=p["cb"][:, ch:ch + 1],
                                                          op0=ALU.mult, op1=ALU.add), reads=[sg, p["cw"], p["cb"]], writes=[acc])
                    for j in range(1, 4):
                        S.op("dve", lambda e: e.scalar_tensor_tensor(out=av, in0=sgv[:, :, j:j + Lc], scalar=p["cw"][:, ch, j:j + 1], in1=av,
                                                                     op0=ALU.mult, op1=ALU.add), reads=[sg, p["cw"], acc], writes=[acc])
                    S.op("act", lambda e: e.activation(out=dstT[:, c, 0:TT], in_=acc[:, 0:TT], func=AF.Silu), reads=[acc], writes=[dstT])
                    yield
            if last:
                if kind == 0:
                    S.dma(convp_o[l, :, :], convTM[125:128, :], reads=[convTM])
                else:
                    for j_ in range(3):
                        S.dma(convs_o[l, :, j_, :], pstride(convTM[5 + j_:6 + j_, :], 8, 16, [[1, D]]), reads=[convTM])
            W = WNEXT("cv"); Wv = W[:, 0:4096].rearrange("p (k n) -> p k n", k=8)
            for s in range(nsub):
                ps = tm_tile(W, Wv, s, slice(0, 512), 512, BT_CV)
                S.op("act", lambda e: e.activation(out=vaug[:, s, :, 0:128], in_=ps[:, :].rearrange("p (h d) -> p h d", h=4), func=AF.Copy),
                     reads=[ps], writes=[vaug])
                yield
            W = WNEXT("cif"); Wv = W[:, 0:64].rearrange("p (k n) -> p k n", k=8)
            n4 = nsub * 4
            pcif = PS()
            for s in range(nsub):
                for kc in range(8):
                    S.op("pe", lambda e: e.matmul(pcif[:, s * 8:(s + 1) * 8], lhsT=hT[:, kc, s * 128:(s + 1) * 128], rhs=Wv[:, kc, 0:8],
                                                  start=(kc == 0), stop=False), reads=[W, hT], writes=[pcif])
                bias_mm(pcif[:, s * 8:(s + 1) * 8], pcif, W, 8)
            pv3 = pcif[:, 0:nsub * 8].rearrange("p (s c) -> p s c", c=8)
            xf = TMP()
            v3 = lambda o: xf[:, o:o + n4].rearrange("p (s h) -> p s h", h=4)
            fb3 = ap_like(p["fbB"][:, 0:4], [[0, nsub], [1, 4]])
            S.op("dve", lambda e: e.tensor_tensor(out=v3(0), in0=pv3[:, :, 4:8], in1=fb3, op=ALU.add),
                 reads=[pcif, p["fbB"]], writes=[xf])
            S.op("act", lambda e: e.activation(out=xf[:, 16:16 + n4], in_=xf[:, 0:n4], func=AF.Exp, scale=-1.0), reads=[xf], writes=[xf])
            S.op("act", lambda e: e.activation(out=xf[:, 32:32 + n4], in_=xf[:, 16:16 + n4], func=AF.Ln, bias=epsb[:, 1:2], scale=1.0),
                 reads=[xf, epsb], writes=[xf])
            S.op("dve", lambda e: e.tensor_copy(out=v3(48), in_=pv3[:, :, 0:4]), reads=[pcif], writes=[xf])
            pc = PS()
            S.op("pe", lambda e: e.matmul(pc[:, 0:n4], lhsT=U32, rhs=xf[:, 32:32 + n4], start=True, stop=True), reads=[cst, xf], writes=[pc])
            S.op("pe", lambda e: e.matmul(pc[:, 16:16 + n4], lhsT=ones32, rhs=xf[:, 32:32 + n4], start=True, stop=True),
                 reads=[cst, xf], writes=[pc])
            S.op("dve", lambda e: e.tensor_tensor(out=xf[:, 64:64 + n4], in0=xf[:, 48:48 + n4], in1=pc[:, 0:n4], op=ALU.add),
                 reads=[xf, pc], writes=[xf])
            fl = lambda t_: t_[:, :, :].rearrange("p s h -> p (s h)")[:, 0:n4]
            S.op("act", lambda e: e.activation(out=fl(bS), in_=xf[:, 64:64 + n4], func=AF.Exp, bias=epsb[:, 2:3], scale=1.0),
                 reads=[xf, epsb], writes=[bS])
            S.op("act", lambda e: e.activation(out=fl(ecnS), in_=pc[:, 0:n4], func=AF.Exp), reads=[pc], writes=[ecnS])
            if kind == 0:
                S.op("act", lambda e: e.activation(out=fl(etotS), in_=pc[:, 16:16 + n4], func=AF.Exp, scale=-1.0), reads=[pc], writes=[etotS])
            else:
                S.op("dve", lambda e: e.tensor_copy(out=xf[:, 80:84], in_=xf[:, 64:68]), reads=[xf], writes=[xf])
                S.op("dve", lambda e: e.tensor_copy(out=xf[:, 84:88], in_=pc[:, 0:4]), reads=[pc], writes=[xf])
                S.dma(scr1[l, :, :], xf[:, 80:88], reads=[xf], writes=[scr_dep["scr1"]])
            for s in range(nsub):
                for h in range(4):
                    eng = "dve" if h % 2 == 0 else "pool"
                    S.op(eng, lambda e: e.tensor_scalar(out=vtil[:, s, h, :], in0=vaug[:, s, h, :], scalar1=bS[:, s, h:h + 1], scalar2=None,
                                                        op0=ALU.mult), reads=[vaug, bS], writes=[vtil])
                yield
            if kind == 0:
                for s in range(nsub):
                    S.op("dve", lambda e: e.tensor_tensor(out=xf[:, 96:100], in0=xf[:, 64 + s * 4:68 + s * 4], in1=PL[l][:, :], op=ALU.add),
                         reads=[xf, PL[l]], writes=[xf])
                    S.op("dve", lambda e: e.tensor_tensor(out=Rm[l][:, :], in0=Rm[l][:, :], in1=xf[:, 96:100], op=ALU.max),
                         reads=[xf, Rm[l]], writes=[Rm[l]])
                    S.op("dve", lambda e: e.tensor_tensor(out=PL[l][:, :], in0=PL[l][:, :], in1=pc[:, 16 + s * 4:20 + s * 4], op=ALU.add),
                         reads=[pc, PL[l]], writes=[PL[l]])
            W = WNEXT("co"); Wv = W[:, 0:4096].rearrange("p (k n) -> p k n", k=8)
            for s in range(nsub):
                ps = tm_tile(W, Wv, s, slice(0, 512), 512, BT_CO)
                t = TMP()
                S.op("act", lambda e: e.activation(out=t[:, 0:512], in_=ps[:, :], func=AF.Sigmoid), reads=[ps], writes=[t])
                S.op(PL_, lambda e: e.tensor_tensor(out=ogg[:, s, :], in0=t[:, 0:512], in1=p["hgB"][:, :], op=ALU.mult),
                     reads=[t, p["hgB"]], writes=[ogg])
                yield
            W = WNEXT("cg"); Wv = W[:, 0:4096].rearrange("p (k n) -> p k n", k=8)
            for s in range(nsub):
                ps = tm_tile(W, Wv, s, slice(0, 512), 512, BT_CG)
                t = TMP()
                S.op("act", lambda e: e.activation(out=t[:, 0:512], in_=ps[:, :], func=AF.Silu), reads=[ps], writes=[t])
                S.op(PL_, lambda e: e.tensor_tensor(out=ogg[:, s, :], in0=ogg[:, s, :], in1=t[:, 0:512], op=ALU.mult),
                     reads=[t, ogg], writes=[ogg])
                yield

            for s in range(nsub):
                pk = PS(); pv = pk[:].bitcast(BF16)
                for h in range(4):
                    S.op("pe", lambda e: e.transpose(out=pv[:, h * 128:(h + 1) * 128], in_=kTm[:, h, s * 128:(s + 1) * 128], identity=identb),
                         reads=[kTm, cbf], writes=[pk])
                S.op("act", lambda e: e.activation(out=kTM[:, s, :, :], in_=pv[:, 0:512].rearrange("p (h d) -> p h d", h=4), func=AF.Copy),
                     reads=[pk], writes=[kTM])
                yield
            if kind == 1:
                eT = sm4
                S.dma(eT[0:4, 0:128].rearrange("h (b t) -> h b t", t=8),
                      bass.AP(scr1.tensor, l * 1024, [[1, 4], [64, 16], [8, 8]]), reads=[scr_dep["scr1"]], writes=[sm4],
                      allow_slow_non_contiguous=True)
                S.dma(eT[0:4, 128:144], bass.AP(scr1.tensor, l * 1024 + 7 * 8 + 4, [[1, 4], [64, 16]]), reads=[scr_dep["scr1"]], writes=[sm4],
                      allow_slow_non_contiguous=True)
                S.dma(eT[0:4, 144:160], sm[l, :, :].rearrange("b h -> h b"), writes=[sm4], allow_slow_non_contiguous=True)
                S.op("dve", lambda e: e.tensor_reduce(out=eT[0:4, 160:176], in_=eT[0:4, 0:128].rearrange("h (b t) -> h b t", t=8),
                                                      axis=AX.X, op=ALU.max), reads=[sm4], writes=[sm4])
                S.op("dve", lambda e: e.tensor_tensor(out=eT[0:4, 176:192], in0=eT[0:4, 160:176], in1=eT[0:4, 144:160], op=ALU.max),
                     reads=[sm4], writes=[sm4])
                S.op("dve", lambda e: e.tensor_tensor(out=eT[0:4, 192:208], in0=eT[0:4, 176:192], in1=eT[0:4, 128:144], op=ALU.subtract),
                     reads=[sm4], writes=[sm4])
                S.dma(ms_o[l, :, :].rearrange("b h -> h b"), eT[0:4, 192:208], reads=[sm4], allow_slow_non_contiguous=True)
                S.op("act", lambda e: e.activation(out=eT[0:4, 208:224], in_=eT[0:4, 176:192], func=AF.Exp, scale=-1.0), reads=[sm4], writes=[sm4])
                S.op("dve", lambda e: e.tensor_tensor(out=eT[0:4, 224:240], in0=eT[0:4, 144:160], in1=eT[0:4, 176:192], op=ALU.subtract),
                     reads=[sm4], writes=[sm4])
                S.op("act", lambda e: e.activation(out=eT[0:4, 224:240], in_=eT[0:4, 224:240], func=AF.Exp), reads=[sm4], writes=[sm4])
                S.op("act", lambda e: e.activation(out=eT[0:4, 240:256], in_=eT[0:4, 144:160], func=AF.Exp), reads=[sm4], writes=[sm4])
                S.dma(scr2[l, :, :, :].rearrange("k b h -> h k b"), eT[0:4, 208:256].rearrange("h (k b) -> h k b", k=3),
                      reads=[sm4], writes=[scr_dep["scr2"]], allow_slow_non_contiguous=True)
                S.dma(AB[:, :, :, :].rearrange("p k b h -> p (k b h)"),
                      bcast_row(scr2[l:l + 1, :, :, :], 128, 192),
                      reads=[scr_dep["scr2"]], writes=[AB])

            yield
        def core_st(s):
            AT_ = AT if s % 2 == 0 else ATb
            ST = PS()
            for h in range(4):
                S.op("pe", lambda e: e.matmul(ST[:, h * 128:(h + 1) * 128], lhsT=kTm[:, h, s * 128:(s + 1) * 128],
                                              rhs=qTm[:, h, s * 128:(s + 1) * 128], start=True, stop=True), reads=[kTm, qTm], writes=[ST])
            for h in range(4):
                S.op("dve", lambda e: e.scalar_tensor_tensor(out=AT_[:, h, :], in0=ST[:, h * 128:(h + 1) * 128], scalar=bS[:, s, h:h + 1],
                                                             in1=cur_mask_b, op0=ALU.mult, op1=ALU.mult), reads=[ST, bS, cbf], writes=[AT_])

        def core_part1(s):
            AT_ = AT if s % 2 == 0 else ATb
            if kind == 0:
                NB = [PS(), PS()]
                for pb in NB:
                    ps_resv.add(pb.idx)
                nbv = lambda h: NB[h // 2][:, (h % 2) * 129:(h % 2 + 1) * 129]
                nbt = lambda h: NB[h // 2]
            else:
                NB = [PS(), PS(), PS(), PS()]
                for pb in NB:
                    ps_resv.add(pb.idx)
                nbv = lambda h: NB[h][:, 0:129]
                nbt = lambda h: NB[h]
            for h in range(4):
                S.op("pe", lambda e: e.matmul(nbv(h), lhsT=AT_[:, h, :], rhs=vaug[:, s, h, :], start=True, stop=False),
                     reads=[AT_, vaug], writes=[nbt(h)])
                if kind == 0:
                    S.op("pe", lambda e: e.matmul(nbv(h), lhsT=qTm[:, h, s * 128:(s + 1) * 128], rhs=Cbf[l][:, h, :], start=False, stop=True),
                         reads=[qTm, Cbf[l]], writes=[nbt(h)])
            if kind == 1:
                for g2 in range(8):
                    C0t = [TMP(), TMP()]
                    C0v = [t_[:, 0:516].rearrange("p (h e) -> p h e", h=4) for t_ in C0t]
                    for bb in range(2):
                        b = g2 * 2 + bb
                        S.dma(C0v[bb][:, :, 0:128], sC[l, b, :, :, :].rearrange("h d e -> d h e"), writes=[C0t[bb]])
                        S.dma(C0v[bb][:, :, 128:129], sn[l, b, :, :].rearrange("h (d o) -> d h o", o=1), writes=[C0t[bb]],
                              allow_slow_non_contiguous=True)
                        bet = ap_like(AB[:, 2, b, 0:1], [[1, 4], [0, 129]])
                        S.op("dve", lambda e: e.tensor_tensor(out=C0bf[:, bb, :, :], in0=C0v[bb], in1=bet, op=ALU.mult),
                             reads=[C0t[bb], AB], writes=[C0bf])
                    S.op(PL_, lambda e: e.memset(qmask[:], 0.0), writes=[qmask])
                    for h in range(4):
                        dstq = ap_like(qmask[:, h, 0, g2 * 16:g2 * 16 + 1], [[128 + 8, 2], [1, 8]])
                        S.op(PL_, lambda e: e.tensor_copy(out=dstq, in_=qTm[:, h, g2 * 16:(g2 + 1) * 16].rearrange("p (b t) -> p b t", t=8)),
                             reads=[qTm], writes=[qmask])
                    for bb in range(2):
                        b = g2 * 2 + bb
                        for h in range(4):
                            S.op("pe", lambda e: e.matmul(nbv(h), lhsT=qmask[:, h, bb, :], rhs=C0bf[:, bb, h, :], start=False,
                                                          stop=(b == NSB - 1)), reads=[qmask, C0bf], writes=[nbt(h)])
                    for bb in range(2):
                        b = g2 * 2 + bb
                        bmk = ap_like(cst[:, C_BM + b:C_BM + b + 1], [[0, 129]])
                        C1tt = TMP()
                        C1t = C1tt[:, 0:516].rearrange("p (h e) -> p h e", h=4)
                        for hh in range(2):
                            pd = PS()
                            for h2 in range(2):
                                h = hh * 2 + h2
                                vm = vmask[:, h2, :]
                                S.op(PL_, lambda e: e.tensor_tensor(out=vm, in0=vtil[:, 0, h, :], in1=bmk, op=ALU.mult),
                                     reads=[vtil, cst], writes=[vmask])
                                S.op("pe", lambda e: e.matmul(pd[:, h2 * 129:(h2 + 1) * 129], lhsT=kTM[:, 0, h, :], rhs=vm, start=True, stop=True),
                                     reads=[kTM, vmask], writes=[pd])
                                S.op("dve", lambda e: e.tensor_scalar(out=C1t[:, h, :], in0=pd[:, h2 * 129:(h2 + 1) * 129],
                                                                      scalar1=AB[:, 0, b, h:h + 1], scalar2=None, op0=ALU.mult),
                                     reads=[pd, AB], writes=[C1tt])
                                S.op("dve", lambda e: e.scalar_tensor_tensor(out=C1t[:, h, :], in0=C0v[bb][:, h, :], scalar=AB[:, 1, b, h:h + 1],
                                                                             in1=C1t[:, h, :], op0=ALU.mult, op1=ALU.add),
                                     reads=[C0t[bb], AB, C1tt], writes=[C1tt])
                        S.dma(Cs_o[l, b, :, :, :].rearrange("h d e -> d h e"), C1t[:, :, 0:128], reads=[C1tt])
                        S.dma(ns_o[l, b, :, :].rearrange("h (d o) -> d h o", o=1), C1t[:, :, 128:129], reads=[C1tt], allow_slow_non_contiguous=True)
                for pb in NB:
                    ps_resv.discard(pb.idx)
            if kind == 0:
                DC = [PS(), PS()]
                for h in range(4):
                    dcv = DC[h // 2][:, (h % 2) * 129:(h % 2 + 1) * 129]
                    S.op("pe", lambda e: e.matmul(dcv, lhsT=kTM[:, s, h, :], rhs=vtil[:, s, h, :], start=True, stop=True),
                         reads=[kTM, vtil], writes=[DC[h // 2]])
                    S.op("dve", lambda e: e.tensor_tensor(out=C32[l][:, h, :], in0=dcv, in1=C32[l][:, h, :], op=ALU.add),
                         reads=[DC[h // 2], C32[l]], writes=[C32[l]])
                    S.op("act", lambda e: e.activation(out=C32[l][:, h, :], in_=C32[l][:, h, :], func=AF.Copy, scale=etotS[:, s, h:h + 1]),
                         reads=[C32[l], etotS], writes=[C32[l]])
                    S.op(PL_, lambda e: e.tensor_copy(out=Cbf[l][:, h, :], in_=C32[l][:, h, :]), reads=[C32[l]], writes=[Cbf[l]])
            return (NB, nbv, nbt)

        def core_part2(s, ctx):
            NB, nbv, nbt = ctx
            ycTM_ = ycTM if s % 2 == 0 else ycTMb
            sx = TMP()
            for h in range(4):
                S.op("dve", lambda e: e.tensor_copy(out=sx[:, 24 + h:25 + h], in_=nbv(h)[:, 128:129]), reads=[nbt(h)], writes=[sx])
                S.op("dve", lambda e: e.scalar_tensor_tensor(out=sx[:, h:h + 1], in0=sx[:, 24 + h:25 + h], scalar=-1.0, in1=sx[:, 24 + h:25 + h],
                                                             op0=ALU.mult, op1=ALU.max), reads=[sx], writes=[sx])
                S.op("dve", lambda e: e.tensor_tensor(out=sx[:, h:h + 1], in0=sx[:, h:h + 1], in1=ecnS[:, s, h:h + 1], op=ALU.max),
                     reads=[sx, ecnS], writes=[sx])
                S.op("act", lambda e: e.activation(out=junk[:, 0:128], in_=nbv(h)[:, 0:128], func=AF.Square, accum_out=sx[:, 8 + h:9 + h]),
                     reads=[nbt(h)], writes=[junk, sx])
            S.op("dve", lambda e: e.reciprocal(out=sx[:, 4:8], in_=sx[:, 0:4]), reads=[sx], writes=[sx])
            S.op("dve", lambda e: e.tensor_tensor(out=sx[:, 12:16], in0=sx[:, 8:12], in1=sx[:, 4:8], op=ALU.mult), reads=[sx], writes=[sx])
            S.op("dve", lambda e: e.tensor_tensor(out=sx[:, 12:16], in0=sx[:, 12:16], in1=sx[:, 4:8], op=ALU.mult), reads=[sx], writes=[sx])
            S.op("act", lambda e: e.activation(out=sx[:, 16:20], in_=sx[:, 12:16], func=AF.Ln, bias=epsb[:, 0:1], scale=1.0 / 128),
                 reads=[sx, epsb], writes=[sx])
            S.op("act", lambda e: e.activation(out=sx[:, 16:20], in_=sx[:, 16:20], func=AF.Exp, scale=-0.5), reads=[sx], writes=[sx])
            S.op("dve", lambda e: e.tensor_tensor(out=sx[:, 20:24], in0=sx[:, 16:20], in1=sx[:, 4:8], op=ALU.mult), reads=[sx], writes=[sx])
            for h in range(4):
                S.op("dve", lambda e: e.scalar_tensor_tensor(out=ycTM_[:, h, :], in0=nbv(h)[:, 0:128], scalar=sx[:, 20 + h:21 + h],
                                                             in1=ogg[:, s, h * 128:(h + 1) * 128], op0=ALU.mult, op1=ALU.mult),
                     reads=[nbt(h), sx, ogg], writes=[ycTM_])
            if kind == 0:
                for pb in NB:
                    ps_resv.discard(pb.idx)

        def core_tr(s):
            ycTM_ = ycTM if s % 2 == 0 else ycTMb
            pk = PS(); pv = pk[:].bitcast(BF16)
            for h in range(4):
                S.op("pe", lambda e: e.transpose(out=pv[:, h * 128:(h + 1) * 128], in_=ycTM_[:, h, :], identity=identb),
                     reads=[ycTM_, cbf], writes=[pk])
            S.op("act", lambda e: e.activation(out=ycT[:, :, s * 128:(s + 1) * 128], in_=pv[:, 0:512].rearrange("p (h t) -> p h t", h=4),
                                               func=AF.Copy), reads=[pk], writes=[ycT])

        def gen_Ccore():
            ctxs = {}
            core_st(0)
            if nsub > 1:
                core_st(1)
            for s in range(nsub):
                ctxs[s] = core_part1(s)
                if s + 2 < nsub:
                    core_st(s + 2)
                if s >= 1:
                    core_tr(s - 1)
                core_part2(s, ctxs[s])
                yield
            core_tr(nsub - 1)
            yield
            if kind == 0 and last:
                S.dma(scr3[l, :, 0:4], Rm[l][:, :], reads=[Rm[l]], writes=[scr_dep["scr3"]])
                S.dma(scr3[l, :, 4:8], PL[l][:, :], reads=[PL[l]], writes=[scr_dep["scr3"]])
                S.dma(sm4[0:4, 0:128], bass.AP(scr3.tensor, l * 1024, [[1, 4], [8, 128]]), reads=[scr_dep["scr3"]], writes=[sm4],
                      allow_slow_non_contiguous=True)
                S.dma(sm4[0:4, 128:129], bass.AP(scr3.tensor, l * 1024 + 4, [[1, 4], [1, 1]]), reads=[scr_dep["scr3"]], writes=[sm4],
                      allow_slow_non_contiguous=True)
                S.op("dve", lambda e: e.tensor_reduce(out=sm4[0:4, 130:131], in_=sm4[0:4, 0:128], axis=AX.X, op=ALU.max), reads=[sm4], writes=[sm4])
                S.op("dve", lambda e: e.tensor_scalar(out=sm4[0:4, 131:132], in0=sm4[0:4, 130:131], scalar1=0.0, scalar2=sm4[0:4, 128:129],
                                                      op0=ALU.max, op1=ALU.subtract), reads=[sm4], writes=[sm4])
                S.dma(mp_o[l:l + 1, :].rearrange("o h -> h o"), sm4[0:4, 131:132], reads=[sm4], allow_slow_non_contiguous=True)
                S.dma(scr4[l:l + 1, 0:4].rearrange("o h -> h o"), sm4[0:4, 131:132], reads=[sm4], writes=[scr_dep["scr4"]],
                      allow_slow_non_contiguous=True)
                emT = TMP()
                S.dma(emT[:, 0:4], bcast_row(scr4[l:l + 1, 0:4], 128, 4),
                      reads=[scr_dep["scr4"]], writes=[emT])
                S.op("act", lambda e: e.activation(out=emT[:, 4:8], in_=emT[:, 0:4], func=AF.Exp, scale=-1.0), reads=[emT], writes=[emT])
                for h in range(4):
                    S.op("dve", lambda e: e.tensor_scalar(out=C32[l][:, h, :], in0=C32[l][:, h, :], scalar1=emT[:, 4 + h:5 + h], scalar2=None,
                                                          op0=ALU.mult), reads=[C32[l], emT], writes=[C32[l]])
                S.dma(Cp_o[l, :, :, :].rearrange("h d e -> d h e"), C32[l][:, :, 0:128], reads=[C32[l]])
                S.dma(np_o[l, :, :].rearrange("h (d o) -> d h o", o=1), C32[l][:, :, 128:129], reads=[C32[l]], allow_slow_non_contiguous=True)

            if kind == 1 and l == 0:
                dbg("yaT", yaT[:, :, 0:128], [yaT]); dbg("ybT", ybT[:, :, 0:128], [ybT]); dbg("ycT", ycT[:, :, 0:128], [ycT])
            yield
        def run_seq(*gens):
            for g_ in gens:
                for _ in g_:
                    pass

        def run_il(g1, g2):
            a_, b_ = True, True
            while a_ or b_:
                if a_:
                    try:
                        next(g1)
                    except StopIteration:
                        a_ = False
                if b_:
                    try:
                        next(g2)
                    except StopIteration:
                        b_ = False

        run_seq(gen_Bproj())
        ckpt("P1 %d %d %d" % (kind, sti, l))
        if kind == 0 and _os.environ.get("KIL"):
            run_il(gen_att(), gen_Cproj())
        else:
            run_seq(gen_att(), gen_Cproj())
        ckpt("P2 %d %d %d" % (kind, sti, l))
        ckpt("P3a %d %d %d" % (kind, sti, l))
        if kind == 0 and _os.environ.get("KIL"):
            run_il(gen_Ccore(), gen_A())
        else:
            run_seq(gen_Ccore(), gen_A())
        ckpt("P3 %d %d %d" % (kind, sti, l))
        brT = [yaT, ybT, ycT]
        for j in range(8):
            W = WNEXT("mg%d" % j)
            Wg = W[:, 0:3072].rearrange("p (k n) -> p k n", k=8)
            acc = TMP()
            for br in range(3):
                pg = fm_chunk(W, Wg, slice(br * 128, (br + 1) * 128))
                sgt = TMP()
                S.op("act", lambda e: e.activation(out=sgt[:, 0:TT], in_=pg[:, 0:TT], func=AF.Sigmoid,
                                                   bias=p["bfm"][:, BF_MG + br * 8 + j:BF_MG + br * 8 + j + 1], scale=1.0),
                     reads=[pg, p["bfm"]], writes=[sgt])
                Wb = W[:, 3072 + br * 512:3072 + (br + 1) * 512].rearrange("p (k n) -> p k n", k=4)
                pp = PS()
                for kc in range(4):
                    S.op("pe", lambda e: e.matmul(pp[:, 0:TT], lhsT=Wb[:, kc, :], rhs=brT[br][:, kc, 0:TT], start=(kc == 0), stop=(kc == 3)),
                         reads=[W, brT[br]], writes=[pp])
                if br == 0:
                    S.op("dve", lambda e: e.tensor_tensor(out=acc[:, 0:TT], in0=pp[:, 0:TT], in1=sgt[:, 0:TT], op=ALU.mult),
                         reads=[pp, sgt], writes=[acc])
                else:
                    S.op("dve", lambda e: e.tensor_tensor(out=sgt[:, 0:TT], in0=pp[:, 0:TT], in1=sgt[:, 0:TT], op=ALU.mult),
                         reads=[pp, sgt], writes=[sgt])
                    dst = acc[:, 0:TT] if br == 1 else mT[:, j, 0:TT]
                    S.op(PL_, lambda e: e.tensor_tensor(out=dst, in0=acc[:, 0:TT], in1=sgt[:, 0:TT], op=ALU.add),
                         reads=[acc, sgt], writes=[acc if br == 1 else mT])
        if kind == 1 and l == 0:
            dbg("mT", mT[:, :, 0:128], [mT])
        Wo = [WNEXT("wo0"), WNEXT("wo1", hold=1)]
        Wov = [w_[:, 0:4096].rearrange("p (k n) -> p k n", k=8) for w_ in Wo]
        for s in range(nsub):
            for nt in range(2):
                ps = PS()
                for kc in range(8):
                    S.op("pe", lambda e: e.matmul(ps[:, :], lhsT=mT[:, kc, s * 128:(s + 1) * 128], rhs=Wov[nt][:, kc, :],
                                                  start=(kc == 0), stop=(kc == 7)), reads=[Wo[nt], mT], writes=[ps])
                t = TMP()
                S.op("dve", lambda e: e.tensor_tensor(out=t[:, 0:512], in0=ps[:, :], in1=p["gate"][:, nt * 512:(nt + 1) * 512],
                                                      op=ALU.mult), reads=[ps, p["gate"]], writes=[t])
                S.op(PL_, lambda e: e.tensor_tensor(out=X[:, s, nt * 512:(nt + 1) * 512], in0=X[:, s, nt * 512:(nt + 1) * 512],
                                                       in1=t[:, 0:512], op=ALU.add), reads=[t, X.d[s]], writes=[X.d[s]])

    def main_flow():
        prologue()
        prologue2()
        ckpt("prologue2")
        S.dma(X[:, 0, :], xs[:, :], writes=[X.d[0]])
        for l in range(NL):
            compute_mod(l)
            ckpt("mod %d" % l)
            process_tile(1, 0, l, l == NL - 1)
            if l == 0:
                dbg("X1", X[:, 0, :], [X.d[0]])
            ckpt("P4 sample %d" % l)
        S.dma(ys[:, :], X[:, 0, :], reads=[X.d[0]])
        for l in range(NL):
            S.op("pool", lambda e: e.tensor_copy(out=prm[l]["gate"][:, :], in_=X[:, 2 + l, :]), reads=[X.d[2 + l]], writes=[prm[l]["gate"]])
        for sti in range(NST):
            for s in range(4):
                S.dma(X[:, s, :], xp[sti * 512 + s * 128: sti * 512 + (s + 1) * 128, :], writes=[X.d[s]])
            for l in range(NL):
                process_tile(0, sti, l, l == NL - 1)
                ckpt("P4 prompt %d %d" % (sti, l))
            for s in range(4):
                S.dma(yp[sti * 512 + s * 128: sti * 512 + (s + 1) * 128, :], X[:, s, :], reads=[X.d[s]])

    try:
        main_flow()
    except _Stop:
        pass
    S.barrier()
    es.close()
    return nc


_PROG = {}


def _get_prog(NST):
    if NST not in _PROG:
        import os
        st = os.environ.get("KSTAGE")
        _PROG[NST] = build_program(NST, stage=(int(st) if st else None))
    return _PROG[NST]


def kernel(x_prompt, x_sample, cache_swa_k, cache_swa_v, state_mlstm_conv, state_mlstm_C, state_mlstm_n,
           state_mlstm_m, c_prompt, c_sample, ada_w, ada_b, norm_g, w_in, b_in, gmlp_vnorm_g, gmlp_ws, gmlp_bs,
           swa_qnorm_g, swa_knorm_g, swa_sinks, mlstm_conv_w, mlstm_conv_b, mlstm_f_bias, mlstm_hnorm_g,
           w_branch_a, w_branch_b, w_branch_c, w_out, _ncores=8):
    f = lambda a: np.ascontiguousarray(np.asarray(a, dtype=np.float32))
    x_prompt = f(x_prompt); x_sample = f(x_sample)
    B, SEQ, _ = x_prompt.shape
    DB = x_sample.shape[0]
    NST = SEQ // 512
    ncores = _ncores
    nsb = DB // ncores
    assert nsb == NSB
    nc = _get_prog(NST)
    shared = {
        "ada_w": f(ada_w), "ada_b": f(ada_b), "norm_g": f(norm_g), "w_in": f(w_in), "b_in": f(b_in),
        "gmlp_vnorm_g": f(gmlp_vnorm_g), "gmlp_ws": f(gmlp_ws), "gmlp_bs": f(gmlp_bs),
        "swa_qnorm_g": f(swa_qnorm_g), "swa_knorm_g": f(swa_knorm_g), "swa_sinks": f(swa_sinks),
        "mlstm_conv_w": f(mlstm_conv_w), "mlstm_conv_b": f(mlstm_conv_b), "mlstm_f_bias": f(mlstm_f_bias),
        "mlstm_hnorm_g": f(mlstm_hnorm_g), "w_branch_a": f(w_branch_a), "w_branch_b": f(w_branch_b),
        "w_branch_c": f(w_branch_c), "w_out": f(w_out), "consts": make_consts(),
    }
    ck = f(cache_swa_k).reshape(NL, DB, 128, 128); cv = f(cache_swa_v).reshape(NL, DB, 128, 128)
    sconv = f(state_mlstm_conv); sC = f(state_mlstm_C); sn = f(state_mlstm_n); sm = f(state_mlstm_m)
    cp = f(c_prompt); cs = f(c_sample)
    in_maps = []
    for c in range(ncores):
        b = c % B
        sl = slice(c * NSB, (c + 1) * NSB)
        m = dict(shared)
        m.update({
            "xp": x_prompt[b], "xs": x_sample[sl].reshape(128, D), "cpr": cp[b:b + 1], "csm": cs[sl],
            "ck": np.ascontiguousarray(ck[:, sl]), "cv": np.ascontiguousarray(cv[:, sl]),
            "sconv": np.ascontiguousarray(sconv[:, sl]), "sC": np.ascontiguousarray(sC[:, sl]),
            "sn": np.ascontiguousarray(sn[:, sl]), "sm": np.ascontiguousarray(sm[:, sl]),
        })
        in_maps.append(m)
    res = run_bass_kernel_spmd(nc, in_maps, core_ids=list(range(ncores)))
    R = res.results
    global DBG_OUT
    DBG_OUT = {k: v for k, v in R[0].items() if k.startswith("dbg_")}
    nb = min(B, ncores)
    y_p = np.stack([R[b]["yp"] for b in range(nb)])
    y_s = np.concatenate([R[c]["ys"].reshape(NSB, 8, D) for c in range(ncores)], axis=0)
    kp = np.stack([R[b]["kp"].reshape(NL, 128, 2, 64) for b in range(nb)], axis=1)
    vp = np.stack([R[b]["vp"].reshape(NL, 128, 2, 64) for b in range(nb)], axis=1)
    convp = np.stack([R[b]["convp"] for b in range(nb)], axis=1)
    Cp = np.stack([R[b]["Cp"] for b in range(nb)], axis=1)
    npp = np.stack([R[b]["np_"] for b in range(nb)], axis=1)
    mp = np.stack([R[b]["mp"] for b in range(nb)], axis=1)
    cat = lambda k, shp: np.concatenate([R[c][k].reshape(shp) for c in range(ncores)], axis=1)
    ks = cat("ks", (NL, NSB, 128, 2, 64)); vs = cat("vs", (NL, NSB, 128, 2, 64))
    convs = cat("convs", (NL, NSB, 3, D)); Cs = cat("Cs", (NL, NSB, 4, 128, 128))
    ns = cat("ns", (NL, NSB, 4, 128)); ms = cat("ms", (NL, NSB, 4))
    gv = cat("gv", (NL, NSB, 8, 512))
    return (y_p, y_s, kp, vp, convp, Cp, npp, mp, ks, vs, convs, Cs, ns, ms, gv)
```
